# Optimizing a Trainium2 kernel written in Bass

```python
import math
import jax
import jax.numpy as jnp
from jax import lax
import numpy as np

D_MODEL = 1024
BATCH = 8
SEQ = 2048
DEPTH = 4
DEC_BATCH = 32
DEC_SEQ = 4
PAST_LEN = 8192
PAGE_SIZE = 128

N_MIXERS = 3
N_A_LAYERS = len(range(0, DEPTH, N_MIXERS))
N_B_LAYERS = len(range(1, DEPTH, N_MIXERS))
N_C_LAYERS = len(range(2, DEPTH, N_MIXERS))
NORM_EPS = 1e-6
NEG_BIG = -1e30

A_HEADS = 8
A_DK = 128
A_DV = D_MODEL // A_HEADS
A_WIDTH = A_HEADS * A_DV
A_CHUNK = 32
A_IN = 2 * A_HEADS * A_DK + 2 * A_WIDTH
A_EXP_CLIP = 60.0

B_HEADS = 8
B_HEAD_DIM = D_MODEL // B_HEADS
B_WIDTH = B_HEADS * B_HEAD_DIM
B_GROUPS = ((128, 1), (512, 4), (2048, 16))
B_BLOCK = 128
B_IN = 3 * len(B_GROUPS) * B_WIDTH + B_WIDTH
ROPE_THETA = 10000.0

C_WIDTH = D_MODEL
C_GROUP_CH = 16
C_GROUPS = C_WIDTH // C_GROUP_CH
C_STATE = 64
C_IN = 2 * C_WIDTH
C_MAX_RE = -1e-4

kernel_name = 'hybrid_hgrn2_dilswa_s5_decode_step'


def rms_norm(x, w):
    xf = x.astype(jnp.float32)
    y = xf * lax.rsqrt(jnp.mean(xf * xf, axis=-1, keepdims=True) + NORM_EPS)
    return (y * w.astype(jnp.float32)).astype(x.dtype)


def apply_rope(x, pos):
    half = x.shape[-1] // 2
    inv_freq = ROPE_THETA ** (-jnp.arange(half, dtype=jnp.float32) / half)
    ang = pos[:, None] * inv_freq[None, :]
    cos = jnp.cos(ang)[None, :, None, :]
    sin = jnp.sin(ang)[None, :, None, :]
    xf = x.astype(jnp.float32)
    x1, x2 = xf[..., :half], xf[..., half:]
    return jnp.concatenate([x1 * cos - x2 * sin, x2 * cos + x1 * sin], axis=-1).astype(x.dtype)


def hgrn2_recurrence(q, k, v, log_f, state0):
    bn, t, h, dk = q.shape
    dv = v.shape[-1]
    c = A_CHUNK if t % A_CHUNK == 0 else t
    nc = t // c

    def chunks(a):
        return a.reshape(bn, nc, c, h, a.shape[-1]).transpose(1, 0, 3, 2, 4)

    causal = jnp.tril(jnp.ones((c, c), dtype=bool))

    def step(s, inp):
        qc, kc, vc, gc = inp
        b = jnp.cumsum(gc, axis=-2)
        diff = jnp.where(causal[:, :, None], b[..., :, None, :] - b[..., None, :, :], NEG_BIG)
        scores = jnp.einsum('bhtk,bhsk,bhtsk->bhts', qc, kc, jnp.exp(diff))
        o = scores @ vc + jnp.einsum('bhtk,bhkv->bhtv', qc * jnp.exp(b), s)
        b_last = b[..., -1:, :]
        s = jnp.exp(b_last[..., 0, :])[..., None] * s + jnp.einsum('bhsk,bhsv->bhkv', kc * jnp.exp(b_last - b), vc)
        return s, o

    s0 = jnp.zeros((bn, h, dk, dv), jnp.float32) if state0 is None else state0.astype(jnp.float32)
    s_last, o = lax.scan(step, s0, (chunks(q), chunks(k), chunks(v), chunks(log_f)))
    o = o.transpose(1, 0, 3, 2, 4).reshape(bn, t, h, dv)
    return o, s_last


def hgrn2_mixer(h, w_in, lower_bound, onorm_w, w_out, state0):
    bn, t, _ = h.shape
    f32 = jnp.float32
    proj = h @ w_in
    nq = A_HEADS * A_DK
    q = jax.nn.silu(proj[..., :nq].astype(f32)).reshape(bn, t, A_HEADS, A_DK)
    zf = proj[..., nq:2 * nq].astype(f32).reshape(bn, t, A_HEADS, A_DK)
    v = proj[..., 2 * nq:2 * nq + A_WIDTH].astype(f32).reshape(bn, t, A_HEADS, A_DV)
    gate = proj[..., 2 * nq + A_WIDTH:]
    lb = lower_bound.astype(f32).reshape(A_HEADS, A_DK)
    log_f = jax.nn.log_sigmoid(zf) + jnp.log1p(lb * jnp.exp(jnp.minimum(-zf, A_EXP_CLIP)))
    k = (1.0 - lb) * jax.nn.sigmoid(-zf)
    o, s_last = hgrn2_recurrence(q, k, v, log_f, state0)
    o = rms_norm(o, onorm_w.reshape(A_HEADS, A_DV))
    o = o.reshape(bn, t, A_WIDTH).astype(h.dtype) * jax.nn.silu(gate)
    return o @ w_out, s_last


def dilated_group_prompt(q, k, v, dil, n_keys):
    bn, s, h, hd = q.shape
    n = s // dil
    nb = -(-n // B_BLOCK)
    npad = nb * B_BLOCK

    def to_res(a):
        a = a.reshape(bn, n, dil, h, hd).transpose(0, 2, 1, 3, 4).reshape(bn * dil, n, h, hd)
        a = jnp.pad(a, ((0, 0), (0, npad - n), (0, 0), (0, 0)))
        return a.reshape(bn * dil, nb, B_BLOCK, h, hd)

    def with_prev(a):
        prev = jnp.pad(a[:, :-1], ((0, 0), (1, 0), (0, 0), (0, 0), (0, 0)))
        return jnp.concatenate([prev, a], axis=2)

    def from_res(a):
        rest = a.shape[3:]
        a = a.reshape((bn, dil, npad) + rest)[:, :, :n]
        return jnp.swapaxes(a, 1, 2).reshape((bn, s) + rest)

    qb = to_res(q)
    kk = with_prev(to_res(k))
    vv = with_prev(to_res(v))
    a_idx = jnp.arange(B_BLOCK)[:, None]
    c_idx = jnp.arange(2 * B_BLOCK)[None, :]
    rel = a_idx - c_idx + B_BLOCK
    blk = jnp.arange(nb)[:, None, None]
    valid = (rel >= 0) & (rel <= n_keys) & (blk * B_BLOCK + c_idx - B_BLOCK >= 0)
    sc = jnp.einsum('zjqhd,zjkhd->zjhqk', qb, kk).astype(jnp.float32) * (hd ** -0.5)
    sc = jnp.where(valid[None, :, None], sc, NEG_BIG)
    lse = jax.nn.logsumexp(sc, axis=-1)
    p = jnp.exp(sc - lse[..., None])
    o = jnp.einsum('zjhqk,zjkhd->zjqhd', p.astype(vv.dtype), vv).astype(jnp.float32)
    return from_res(o), from_res(jnp.swapaxes(lse, 2, 3))


def dilated_group_sample(q, k, v, buf, dil, n_keys, window):
    db, t, h, hd = q.shape
    lb = buf.shape[1]
    k_all = jnp.concatenate([buf[:, :, 0], k], axis=1)
    v_all = jnp.concatenate([buf[:, :, 1], v], axis=1)
    idx = lb + jnp.arange(t)[:, None] - dil * jnp.arange(n_keys + 1)[None, :]
    valid = idx >= 0
    flat = jnp.maximum(idx, 0).reshape(-1)
    kg = jnp.take(k_all, flat, axis=1).reshape(db, t, n_keys + 1, h, hd)
    vg = jnp.take(v_all, flat, axis=1).reshape(db, t, n_keys + 1, h, hd)
    sc = jnp.einsum('bthd,btkhd->bthk', q, kg).astype(jnp.float32) * (hd ** -0.5)
    sc = jnp.where(valid[None, :, None, :], sc, NEG_BIG)
    lse = jax.nn.logsumexp(sc, axis=-1)
    p = jnp.exp(sc - lse[..., None])
    o = jnp.einsum('bthk,btkhd->bthd', p.astype(vg.dtype), vg).astype(jnp.float32)
    keep = min(window, k_all.shape[1])
    new_buf = jnp.stack([k_all[:, -keep:], v_all[:, -keep:]], axis=2)
    return o, lse, new_buf


def dilated_mixer(h, w_in, w_out, pos, bufs):
    bn, t, _ = h.shape
    proj = h @ w_in
    outs, lses, new_bufs = [], [], []
    for gi, (window, dil) in enumerate(B_GROUPS):
        base = gi * 3 * B_WIDTH
        q, k, v = [proj[..., base + m * B_WIDTH: base + (m + 1) * B_WIDTH].reshape(bn, t, B_HEADS, B_HEAD_DIM)
                   for m in range(3)]
        q = apply_rope(q, pos)
        k = apply_rope(k, pos)
        n_keys = window // dil
        if bufs is None:
            o, lse = dilated_group_prompt(q, k, v, dil, n_keys)
            keep = min(window, t)
            nbuf = jnp.stack([k[:, t - keep:], v[:, t - keep:]], axis=2)
        else:
            o, lse, nbuf = dilated_group_sample(q, k, v, bufs[gi], dil, n_keys, window)
        outs.append(o)
        lses.append(lse)
        new_bufs.append(nbuf)
    wts = jax.nn.softmax(jnp.stack(lses, axis=0), axis=0)
    o = jnp.sum(wts[..., None] * jnp.stack(outs, axis=0), axis=0)
    gate = proj[..., 3 * len(B_GROUPS) * B_WIDTH:]
    o = o.reshape(bn, t, B_WIDTH).astype(h.dtype) * jax.nn.silu(gate)
    return o @ w_out, new_bufs


def s5_mixer(h, w_in, a_re, a_im, b_re, b_im, c_re, c_im, d_skip, log_dt, w_glu, b_glu, w_out, state0):
    bn, t, _ = h.shape
    f32 = jnp.float32
    proj = h @ w_in
    u = proj[..., :C_WIDTH].astype(f32)
    gate = proj[..., C_WIDTH:]
    ug = u.reshape(bn, t, C_GROUPS, C_GROUP_CH)
    lam_re = jnp.minimum(a_re.astype(f32), C_MAX_RE)
    lam_im = a_im.astype(f32)
    dt = jnp.exp(log_dt.astype(f32))[:, None]
    mag = jnp.exp(lam_re * dt)
    bar_re = mag * jnp.cos(lam_im * dt)
    bar_im = mag * jnp.sin(lam_im * dt)
    den = lam_re * lam_re + lam_im * lam_im
    xr = bar_re - 1.0
    coef_re = (xr * lam_re + bar_im * lam_im) / den
    coef_im = (bar_im * lam_re - xr * lam_im) / den
    br, bi = b_re.astype(f32), b_im.astype(f32)
    bbar_re = coef_re[..., None] * br - coef_im[..., None] * bi
    bbar_im = coef_re[..., None] * bi + coef_im[..., None] * br
    bu_re = jnp.einsum('btgc,gpc->btgp', ug, bbar_re)
    bu_im = jnp.einsum('btgc,gpc->btgp', ug, bbar_im)
    if state0 is not None:
        s_re = state0[..., 0].astype(f32)
        s_im = state0[..., 1].astype(f32)
        bu_re = bu_re.at[:, 0].add(bar_re * s_re - bar_im * s_im)
        bu_im = bu_im.at[:, 0].add(bar_re * s_im + bar_im * s_re)
    a_re_t = jnp.broadcast_to(bar_re, (1, t) + bar_re.shape)
    a_im_t = jnp.broadcast_to(bar_im, (1, t) + bar_im.shape)

    def combine(e1, e2):
        a1r, a1i, b1r, b1i = e1
        a2r, a2i, b2r, b2i = e2
        return (a2r * a1r - a2i * a1i, a2r * a1i + a2i * a1r,
                a2r * b1r - a2i * b1i + b2r, a2r * b1i + a2i * b1r + b2i)

    _, _, xs_re, xs_im = lax.associative_scan(combine, (a_re_t, a_im_t, bu_re, bu_im), axis=1)
    y = (jnp.einsum('gcp,btgp->btgc', c_re.astype(f32), xs_re)
         - jnp.einsum('gcp,btgp->btgc', c_im.astype(f32), xs_im))
    y = y.reshape(bn, t, C_WIDTH) + d_skip.astype(f32) * u
    y = jax.nn.gelu(y)
    y = y * jax.nn.sigmoid(y @ w_glu.astype(f32) + b_glu.astype(f32))
    y = y.astype(h.dtype) * jax.nn.silu(gate)
    new_state = jnp.stack([xs_re[:, -1], xs_im[:, -1]], axis=-1)
    return y @ w_out, new_state


def setup_inputs(seed: int = 0) -> dict:
    key = jax.random.key(seed)
    ks = iter(jax.random.split(key, 40))
    f32 = jnp.float32

    def nrm(shape, scale=1.0):
        return scale * jax.random.normal(next(ks), shape, f32)

    def buf_len(w):
        return min(w, PAST_LEN)

    kv_shape = lambda w: (N_B_LAYERS, DEC_BATCH, buf_len(w), 2, B_HEADS, B_HEAD_DIM)
    return {
        'x_prompt': nrm((BATCH, SEQ, D_MODEL)),
        'x_sample': nrm((DEC_BATCH, DEC_SEQ, D_MODEL)),
        'state_hgrn': nrm((N_A_LAYERS, DEC_BATCH, A_HEADS, A_DK, A_DV), 0.5),
        'cache_kv_w128': nrm(kv_shape(B_GROUPS[0][0])),
        'cache_kv_w512': nrm(kv_shape(B_GROUPS[1][0])),
        'cache_kv_w2048': nrm(kv_shape(B_GROUPS[2][0])),
        'state_s5': nrm((N_C_LAYERS, DEC_BATCH, C_GROUPS, C_STATE, 2), 0.2),
        'norm_w': 1.0 + nrm((DEPTH, D_MODEL), 0.02),
        'final_norm_w': 1.0 + nrm((D_MODEL,), 0.02),
        'a_w_in': nrm((N_A_LAYERS, D_MODEL, A_IN), D_MODEL ** -0.5),
        'a_lb_logits': nrm((N_A_LAYERS, A_HEADS * A_DK), 1.0),
        'a_onorm_w': 1.0 + nrm((N_A_LAYERS, A_WIDTH), 0.02),
        'a_w_out': nrm((N_A_LAYERS, A_WIDTH, D_MODEL), A_WIDTH ** -0.5),
        'b_w_in': nrm((N_B_LAYERS, D_MODEL, B_IN), D_MODEL ** -0.5),
        'b_w_out': nrm((N_B_LAYERS, B_WIDTH, D_MODEL), B_WIDTH ** -0.5),
        'c_w_in': nrm((N_C_LAYERS, D_MODEL, C_IN), D_MODEL ** -0.5),
        'c_a_re': -0.5 + nrm((N_C_LAYERS, C_GROUPS, C_STATE), 0.01),
        'c_a_im': jnp.pi * jnp.arange(C_STATE, dtype=f32) + nrm((N_C_LAYERS, C_GROUPS, C_STATE), 0.01),
        'c_b_re': nrm((N_C_LAYERS, C_GROUPS, C_STATE, C_GROUP_CH), (2.0 * C_GROUP_CH) ** -0.5),
        'c_b_im': nrm((N_C_LAYERS, C_GROUPS, C_STATE, C_GROUP_CH), (2.0 * C_GROUP_CH) ** -0.5),
        'c_c_re': nrm((N_C_LAYERS, C_GROUPS, C_GROUP_CH, C_STATE), (2.0 * C_STATE) ** -0.5),
        'c_c_im': nrm((N_C_LAYERS, C_GROUPS, C_GROUP_CH, C_STATE), (2.0 * C_STATE) ** -0.5),
        'c_d': nrm((N_C_LAYERS, C_WIDTH), 1.0),
        'c_log_dt': jax.random.uniform(next(ks), (N_C_LAYERS, C_GROUPS), f32, math.log(0.001), math.log(0.1)),
        'c_w_glu': nrm((N_C_LAYERS, C_WIDTH, C_WIDTH), C_WIDTH ** -0.5),
        'c_b_glu': nrm((N_C_LAYERS, C_WIDTH), 0.01),
        'c_w_out': nrm((N_C_LAYERS, C_WIDTH, D_MODEL), C_WIDTH ** -0.5),
    }


def reference(x_prompt, x_sample, state_hgrn, cache_kv_w128, cache_kv_w512, cache_kv_w2048, state_s5,
              norm_w, final_norm_w, a_w_in, a_lb_logits, a_onorm_w, a_w_out, b_w_in, b_w_out,
              c_w_in, c_a_re, c_a_im, c_b_re, c_b_im, c_c_re, c_c_im, c_d, c_log_dt, c_w_glu, c_b_glu, c_w_out):
    f32 = jnp.float32
    pos_p = jnp.arange(x_prompt.shape[1], dtype=f32)
    pos_s = PAST_LEN + jnp.arange(x_sample.shape[1], dtype=f32)
    p_lb = jax.nn.softmax(a_lb_logits.astype(f32), axis=0)
    lower_bounds = jnp.cumsum(p_lb, axis=0) - p_lb[0:1]
    caches = (cache_kv_w128, cache_kv_w512, cache_kv_w2048)
    xp, xs = x_prompt, x_sample
    hgrn_p, hgrn_s, s5_p, s5_s = [], [], [], []
    kv_p = [[] for _ in B_GROUPS]
    kv_s = [[] for _ in B_GROUPS]
    for layer in range(DEPTH):
        kind, j = layer % N_MIXERS, layer // N_MIXERS
        hp = rms_norm(xp, norm_w[layer])
        hs = rms_norm(xs, norm_w[layer])
        if kind == 0:
            wts = (a_w_in[j], lower_bounds[j], a_onorm_w[j], a_w_out[j])
            dp, st = hgrn2_mixer(hp, *wts, None)
            hgrn_p.append(st)
            ds, st = hgrn2_mixer(hs, *wts, state_hgrn[j])
            hgrn_s.append(st)
        elif kind == 1:
            dp, bufs = dilated_mixer(hp, b_w_in[j], b_w_out[j], pos_p, None)
            for g in range(len(B_GROUPS)):
                kv_p[g].append(bufs[g])
            ds, bufs = dilated_mixer(hs, b_w_in[j], b_w_out[j], pos_s, [cc[j] for cc in caches])
            for g in range(len(B_GROUPS)):
                kv_s[g].append(bufs[g])
        else:
            wts = (c_w_in[j], c_a_re[j], c_a_im[j], c_b_re[j], c_b_im[j], c_c_re[j], c_c_im[j],
                   c_d[j], c_log_dt[j], c_w_glu[j], c_b_glu[j], c_w_out[j])
            dp, st = s5_mixer(hp, *wts, None)
            s5_p.append(st)
            ds, st = s5_mixer(hs, *wts, state_s5[j])
            s5_s.append(st)
        xp = xp + dp
        xs = xs + ds
    y_prompt = rms_norm(xp, final_norm_w)
    y_sample = rms_norm(xs, final_norm_w)
    new_hgrn_prompt = jnp.stack(hgrn_p, axis=0)
    new_hgrn_sample = jnp.stack(hgrn_s, axis=0)
    new_kv128_prompt = jnp.stack(kv_p[0], axis=0)
    new_kv128_sample = jnp.stack(kv_s[0], axis=0)
    new_kv512_prompt = jnp.stack(kv_p[1], axis=0)
    new_kv512_sample = jnp.stack(kv_s[1], axis=0)
    new_kv2048_prompt = jnp.stack(kv_p[2], axis=0)
    new_kv2048_sample = jnp.stack(kv_s[2], axis=0)
    new_s5_prompt = jnp.stack(s5_p, axis=0)
    new_s5_sample = jnp.stack(s5_s, axis=0)
    return (y_prompt, y_sample, new_hgrn_prompt, new_hgrn_sample, new_kv128_prompt, new_kv128_sample,
            new_kv512_prompt, new_kv512_sample, new_kv2048_prompt, new_kv2048_sample,
            new_s5_prompt, new_s5_sample)
```

```python
import contextlib
import math
import numpy as np
import concourse.bass as bass
import concourse.mybir as mybir
from concourse.bass_utils import run_bass_kernel_spmd

F32 = mybir.dt.float32
BF16 = mybir.dt.bfloat16
AF = mybir.ActivationFunctionType
ALU = mybir.AluOpType
AX = mybir.AxisListType

NCORES = 8
D = 1024
SEQ = 2048
NT = 17
T = NT * 128
TBS = [(0, 512), (512, 512), (1024, 512), (1536, 512), (2048, 128)]
EPS = 1e-6
PAST = 8192
STOP_AFTER = None
HG_LEVEL = 99
HG_SUB = 99
HG_HEADS = 8
HG_TBS = 5


class Prog:
    ENGS = ("pe", "act", "dve", "pool", "sp")
    NDS = {"sp": 40, "pool": 16, "act": 16}

    def __init__(self, nc, st):
        self.nc = nc
        self.sems = {e: st.enter_context(nc.semaphore("s_" + e)) for e in self.ENGS}
        self.dsems = {q: [st.enter_context(nc.semaphore(f"d_{q}{i}")) for i in range(n)] for q, n in self.NDS.items()}
        self.cnt = {e: 0 for e in self.ENGS}
        self.ndma = {q: 0 for q in self.NDS}
        self.q = {e: [] for e in self.ENGS}
        self.pend = {e: ([], []) for e in self.ENGS}
        self.reg = {}
        self.waited = {e: {} for e in self.ENGS}
        self.eng = {"pe": nc.tensor, "act": nc.scalar, "dve": nc.vector, "pool": nc.gpsimd, "sp": nc.sync}

    def _deps(self, eng, reads, writes, deps):
        ds = set(d for d in deps if d is not None)
        for k in reads:
            r = self.reg.get(k)
            if r and r[0] is not None:
                ds.add(r[0])
        for k in writes:
            r = self.reg.get(k)
            if r:
                if r[0] is not None:
                    ds.add(r[0])
                ds.update(r[1])
        if eng == "pe":
            ds = set(d for d in ds if d[0] != "pe")
        return ds

    def _register(self, tok, reads, writes):
        for k in reads:
            r = self.reg.setdefault(k, [None, []])
            r[1].append(tok)
        for k in writes:
            self.reg[k] = [tok, []]

    def op(self, eng, fn, reads=(), writes=(), inc=True, deps=()):
        ds = self._deps(eng, reads, writes, deps)
        if inc:
            self.cnt[eng] += 1
            tok = (eng, self.cnt[eng])
            pr, pw = self.pend[eng]
            self._register(tok, list(reads) + pr, list(writes) + pw)
            self.pend[eng] = ([], [])
        else:
            tok = None
            self.pend[eng][0].extend(reads)
            self.pend[eng][1].extend(writes)
        self.q[eng].append(("op", fn, ds, inc))
        return tok

    def dma(self, queue, out, in_, reads=(), writes=(), deps=()):
        ds = self._deps(queue, reads, writes, deps)
        i = self.ndma[queue]
        self.ndma[queue] += 1
        tok = ("dma", queue, i)
        self._register(tok, reads, writes)
        self.q[queue].append(("dma", (out, in_, i), ds, True))
        return tok

    def barrier(self, final=False):
        toks = [(e, self.cnt[e]) for e in self.ENGS if self.cnt[e] > 0]
        for qn, n in self.ndma.items():
            if qn == "act" and not final:
                continue
            nd = self.NDS[qn]
            for i in range(max(0, n - nd), n):
                toks.append(("dma", qn, i))
        for e in self.ENGS:
            self.op(e, lambda en: en.nop(), deps=toks)
        self.reg = {}

    def emit(self):
        nc = self.nc
        with nc.Block() as block:
            def run(ename):
                def body(e):
                    waited = self.waited[ename]

                    def wait(sem, key, val):
                        if waited.get(key, 0) >= val:
                            return
                        e.wait_ge(sem, val)
                        waited[key] = val

                    for kind, payload, deps, inc in self.q[ename]:
                        for d in sorted(deps, key=str):
                            if d[0] == "dma":
                                _, qn, i = d
                                nd = self.NDS[qn]
                                wait(self.dsems[qn][i % nd], (qn, i % nd), 16 * (i // nd + 1))
                            else:
                                wait(self.sems[d[0]], d[0], d[1])
                        if kind == "op":
                            ins = payload(e)
                            if inc:
                                ins.then_inc(self.sems[ename], 1)
                        else:
                            out, in_, i = payload
                            nd = self.NDS[ename]
                            if i >= nd:
                                wait(self.dsems[ename][i % nd], (ename, i % nd), 16 * (i // nd))
                            e.dma_start(out=out, in_=in_).then_inc(self.dsems[ename][i % nd], 16)
                    self.q[ename] = []
                return body

            block.tensor(run("pe"))
            block.scalar(run("act"))
            block.vector(run("dve"))
            block.gpsimd(run("pool"))
            block.sync(run("sp"))


class K:
    pass


def _mm(P, out, lhsT, rhs, start, stop, reads, writes, inc, tp=None):
    if tp is None:
        return P.op("pe", lambda e: e.matmul(out, lhsT, rhs, start=start, stop=stop, skip_group_check=True), reads, writes, inc=inc)
    return P.op("pe", lambda e: e.matmul(out, lhsT, rhs, start=start, stop=stop, tile_position=tp, skip_group_check=True), reads, writes, inc=inc)


def _act(P, out, in_, func, reads, writes, scale=1.0, bias=0.0, accum=None):
    if accum is None:
        return P.op("act", lambda e: e.activation(out, in_, func, bias=bias, scale=scale), reads, writes)
    return P.op("act", lambda e: e.activation(out, in_, func, bias=bias, scale=scale, accum_out=accum), reads, writes)


def _tt(P, eng, out, a, b, op, reads, writes):
    return P.op(eng, lambda e: e.tensor_tensor(out, a, b, op), reads, writes)


def _ts(P, eng, out, a, s1, s2, op0, op1, reads, writes):
    if op1 is None:
        return P.op(eng, lambda e: e.tensor_scalar(out, a, s1, None, op0), reads, writes)
    return P.op(eng, lambda e: e.tensor_scalar(out, a, s1, s2, op0, op1), reads, writes)


def _stt(P, eng, out, a, sc, b, op0, op1, reads, writes):
    return P.op(eng, lambda e: e.scalar_tensor_tensor(out, a, sc, b, op0, op1), reads, writes)


def _cp(P, eng, out, in_, reads, writes):
    if eng == "act":
        return P.op("act", lambda e: e.copy(out, in_), reads, writes)
    return P.op(eng, lambda e: e.tensor_copy(out, in_), reads, writes)


def x_tile_src(k, src, n):
    if src == "in":
        if n < 16:
            return [((0, 128), k.xp[n * 128:(n + 1) * 128, :])]
        return [((32 * b, 32 * b + 4), k.xs[4 * b:4 * b + 4, :]) for b in range(4)]
    return [((0, 128), k.xres[n * 128:(n + 1) * 128, :])]


def load_x_tile(k, P, src, n, buf, key):
    toks = []
    if src == "in" and n == 16:
        P.op("pool", lambda e: e.memset(buf[:], 0.0), [], [key])
    for (p0, p1), ap in x_tile_src(k, src, n):
        toks.append(P.dma("sp", buf[p0:p1, :], ap, reads=[("xres", n)] if src != "in" else [], writes=[key]))
    return toks


def phase_norm(k, P, st, layer, src):
    nc = k.nc
    xin = [st.enter_context(nc.sbuf_tensor(f"xin{i}", [128, D], F32)) for i in range(3)]
    junk = st.enter_context(nc.sbuf_tensor("njunk", [128, D], BF16))
    xn = [st.enter_context(nc.sbuf_tensor(f"xn{i}", [128, D], BF16)) for i in range(2)]
    ss = st.enter_context(nc.sbuf_tensor("nss", [128, NT], F32))
    rs = st.enter_context(nc.sbuf_tensor("nrs", [128, NT], F32))
    tp = [st.enter_context(nc.psum_tensor(f"ntp{i}", [128, 8, 128], BF16)) for i in range(2)]
    P.op("dve", lambda e: e.memset(ss[:], 0.0), [], [("nss", n) for n in range(NT)])
    for n in range(NT):
        xb = xin[n % 3]
        kx = ("xin", n % 3)
        load_x_tile(k, P, src, n, xb, kx)
        _act(P, junk[:], xb[:], AF.Square, [kx], ["njunk", ("nss", n)], accum=ss[:, n:n + 1])
        _act(P, rs[:, n:n + 1], ss[:, n:n + 1], AF.Ln, [("nss", n)], [("nrs", n)], scale=1.0 / D, bias=EPS)
        _act(P, rs[:, n:n + 1], rs[:, n:n + 1], AF.Exp, [("nrs", n)], [("nrs", n)], scale=-0.5)
        xnb = xn[n % 2]
        kn = ("xn", n % 2)
        P.op("act", lambda e, xnb=xnb, xb=xb, n=n: e.mul(xnb[:], xb[:], rs[:, n:n + 1]), [kx, ("nrs", n)], [kn])
        tpb = tp[n % 2]
        kt = ("ntp", n % 2)
        for c in range(8):
            P.op("pe", lambda e, c=c, tpb=tpb, xnb=xnb: e.transpose(tpb[:, c, :], xnb[:, c * 128:(c + 1) * 128], k.ident[:]),
                 [kn, "ident"], [kt], inc=(c == 7))
        nw = k.normw[:, layer, :]
        P.op("dve", lambda e, tpb=tpb, n=n, nw=nw: e.tensor_tensor(
            k.hT[:, :, n * 128:(n + 1) * 128], tpb[:], nw.unsqueeze(2).to_broadcast([128, 8, 128]), ALU.mult),
            [kt, "normw"], [("hT", n)])


def phase_outproj(k, P, st, wout_ap, src, dst, final):
    nc = k.nc
    wo = st.enter_context(nc.sbuf_tensor("wo", [128, 8, D], BF16))
    xin = [st.enter_context(nc.sbuf_tensor(f"oxin{i}", [128, D], F32)) for i in range(3)]
    xo = [st.enter_context(nc.sbuf_tensor(f"oxo{i}", [128, D], F32)) for i in range(3)]
    po = [st.enter_context(nc.psum_tensor(f"opo{i}", [128, 2, 512], F32)) for i in range(2)]
    for h in range(8):
        P.dma("pool", wo[:, h, :], wout_ap[h * 128:(h + 1) * 128, :], writes=[("wo", h)])
    if final:
        k.fnw = st.enter_context(nc.sbuf_tensor("fnw", [128, D], F32))
        P.dma("sp", k.fnw[:], k.fnw_d, writes=["fnw"])
        junk = st.enter_context(nc.sbuf_tensor("ojunk", [128, D], BF16))
        ss = st.enter_context(nc.sbuf_tensor("oss", [128, NT], F32))
        rs = st.enter_context(nc.sbuf_tensor("ors", [128, NT], F32))
        yo = [st.enter_context(nc.sbuf_tensor(f"oyo{i}", [128, D], F32)) for i in range(3)]
        P.op("dve", lambda e: e.memset(ss[:], 0.0), [], [("oss", n) for n in range(NT)])
    for n in range(NT):
        xb = xin[n % 3]
        kx = ("oxin", n % 3)
        load_x_tile(k, P, src, n, xb, kx)
        pb = po[n % 2]
        kp = ("opo", n % 2)
        for half in range(2):
            for h in range(8):
                _mm(P, pb[:, half, :], k.OT[:, h, n * 128:(n + 1) * 128], wo[:, h, half * 512:(half + 1) * 512],
                    h == 0, h == 7, [("OT", n), ("wo", h)], [kp], inc=(half == 1 and h == 7))
        xob = xo[n % 3]
        ko = ("oxo", n % 3)
        _tt(P, "dve", xob[:], xb[:], pb[:].rearrange("p a b -> p (a b)"), ALU.add, [kx, kp], [ko])
        if not final:
            P.dma("sp", k.xres[n * 128:(n + 1) * 128, :], xob[:], reads=[ko], writes=[("xres", n)])
        else:
            _act(P, junk[:], xob[:], AF.Square, [ko], ["ojunk", ("oss", n)], accum=ss[:, n:n + 1])
            _act(P, rs[:, n:n + 1], ss[:, n:n + 1], AF.Ln, [("oss", n)], [("ors", n)], scale=1.0 / D, bias=EPS)
            _act(P, rs[:, n:n + 1], rs[:, n:n + 1], AF.Exp, [("ors", n)], [("ors", n)], scale=-0.5)
            yb = yo[n % 3]
            ky = ("oyo", n % 3)
            _stt(P, "dve", yb[:], xob[:], rs[:, n:n + 1], k.fnw[:], ALU.mult, ALU.mult, [ko, ("ors", n), "fnw"], [ky])
            if n < 16:
                P.dma("sp", k.yp[n * 128:(n + 1) * 128, :], yb[:], reads=[ky])
            else:
                for b in range(4):
                    P.dma("sp", k.ys[4 * b:4 * b + 4, :], yb[32 * b:32 * b + 4, :], reads=[ky])


def phase_hgrn(k, P, st, j):
    nc = k.nc
    sb = lambda name, shape, dt: st.enter_context(nc.sbuf_tensor(name, shape, dt))
    ps = lambda name, shape, dt: st.enter_context(nc.psum_tensor(name, shape, dt))
    win = k.a_w_in[j].rearrange("(kc p) (b h c) -> p kc b h c", p=128, b=4, h=8)
    k.m01 = sb("m01", [128, 128], F32)
    k.cmask = sb("cmask", [128, T], F32)
    k.padmask = sb("padmask", [128, 128], F32)
    k.rmask = sb("rmask", [128, 4], F32)
    P.dma("sp", k.m01[:], k.c_m01, writes=["m01"])
    P.dma("sp", k.cmask[:], k.c_cmask, writes=["cmask"])
    P.dma("sp", k.padmask[:], k.c_padmask, writes=["padmask"])
    P.dma("sp", k.rmask[:], k.c_rmask, writes=["rmask"])
    Wh = [sb(f"Wh{i}", [128, 8, 4, 128], BF16) for i in range(2)]
    NB = 2
    QTA = sb("QTA", [128, T], F32)
    GsA = sb("GsA", [128, T], F32)
    E = [sb(f"E{i}", [128, 512], F32) for i in range(NB)]
    U = [sb(f"U{i}", [128, 512], F32) for i in range(NB)]
    L1 = [sb(f"L1{i}", [128, 512], F32) for i in range(NB)]
    L2 = [sb(f"L2{i}", [128, 512], F32) for i in range(NB)]
    KT = [sb(f"KT{i}", [128, 512], F32) for i in range(NB)]
    BT = [sb(f"BT{i}", [128, 512], F32) for i in range(NB)]
    R2 = [sb(f"R2{i}", [128, 512], F32) for i in range(NB)]
    EX = [sb(f"EX{i}", [128, 512], F32) for i in range(3)]
    Dd = [sb(f"Dd{i}", [128, 16], F32) for i in range(NB)]
    Qt = [sb(f"Qt{i}", [128, 512], BF16) for i in range(NB)]
    Kt = [sb(f"Kt{i}", [128, 512], BF16) for i in range(NB)]
    Kh = [sb(f"Kh{i}", [128, 512], BF16) for i in range(NB)]
    V = [sb(f"V{i}", [128, 4, 128], BF16) for i in range(NB)]
    KH = [sb(f"KH{i}", [128, 4, 128], BF16) for i in range(NB)]
    ATm = [sb(f"ATm{i}", [128, 128], BF16) for i in range(2)]
    S32 = [sb(f"S32{i}", [128, 128], F32) for i in range(4)]
    Sbf = [sb(f"Sbf{i}", [128, 128], BF16) for i in range(4)]
    Osb = [sb(f"Osb{i}", [128, 512], F32) for i in range(2)]
    SQ = [sb(f"SQ{i}", [128, 512], F32) for i in range(2)]
    RSD = [sb(f"RSD{i}", [128, 512], F32) for i in range(2)]
    T1 = [sb(f"T1{i}", [128, 512], F32) for i in range(2)]
    pp = [ps(f"pp{i}", [128, 512], F32) for i in range(2)]
    pAB = [ps(f"pAB{i}", [128, 512], F32) for i in range(2)]
    pk = pAB[0][:, 256:512].bitcast(BF16).rearrange("p (a b) -> p a b", b=128)
    pdS = [ps(f"pdS{i}", [128, 512], F32) for i in range(2)]
    pot = [ps(f"pot{i}", [128, 512], F32) for i in range(2)]
    Vblk = [sb(f"Vblk{i}", [128, 4, 4, 128], BF16) for i in range(NB)]

    lb = k.lbv[:, j, :]
    oml = k.omlv[:, j, :]
    onw = k.onwv[:, j, :]
    clb = k.clbv[:, j, :]
    CLIP = float(np.exp(np.float32(60.0)))

    def load_wh(h):
        wb = Wh[h % 2]
        for b in range(4):
            P.dma("pool", wb[:, :, b, :], win[:, :, b, h, :], writes=[("Wh", h % 2, b)])

    load_wh(0)
    sidx = [0]
    cache_jobs = []
    if j == 0:
        for b in range(4):
            cache_jobs.append([(k.c2048, k.k2048s, 2048, b)])
        cache_jobs.append([(k.c512, k.k512s, 512, b) for b in range(4)])
        cache_jobs.append([(k.c128, k.k128s, 128, b) for b in range(4)])

    def proj_fm(h, b, tb, pbuf, kpb):
        t0, tn = TBS[tb]
        wb = Wh[h % 2]
        for kc in range(8):
            _mm(P, pbuf[:, 0:tn], wb[:, kc, b, :], k.hT[:, kc, t0:t0 + tn], kc == 0, kc == 7,
                [("Wh", h % 2, b)] + [("hT", n) for n in range(t0 // 128, (t0 + tn) // 128)], [kpb], inc=(kc == 7))

    for h in range(HG_HEADS):
        if h + 1 < 8:
            load_wh(h + 1)
        wb = Wh[h % 2]
        ci = 0
        if h < len(cache_jobs):
            for src_c, dst_c, W, b in cache_jobs[h]:
                P.dma("act", dst_c[b, 0:W - 4], src_c[b, 4:W])
        si = sidx[0]
        P.op("pool", lambda e, si=si: e.memset(S32[si][:], 0.0), [], [("S32", si)])
        P.op("pool", lambda e, si=si: e.memset(Sbf[si][:], 0.0), [], [("Sbf", si)])
        for tb in range(5):
            t0, tn = TBS[tb]
            proj_fm(h, 0, tb, pp[tb % 2], ("pp", tb % 2))
            _act(P, QTA[:, t0:t0 + tn], pp[tb % 2][:, 0:tn], AF.Silu, [("pp", tb % 2)], [("QTA", tb)])
        for tb in range(5):
            t0, tn = TBS[tb]
            proj_fm(h, 3, tb, pp[(tb + 1) % 2], ("pp", (tb + 1) % 2))
            _act(P, GsA[:, t0:t0 + tn], pp[(tb + 1) % 2][:, 0:tn], AF.Silu, [("pp", (tb + 1) % 2)], [("GsA", tb)])
        if HG_LEVEL <= 1:
            return
        for tb in range(HG_TBS):
            t0, tn = TBS[tb]
            ntile = tn // 128
            nch = tn // 32
            s = tb % NB
            sl = slice(0, tn)
            proj_fm(h, 1, tb, pp[1], ("pp", 1))
            _act(P, E[s][:, sl], pp[1][:, sl], AF.Exp, [("pp", 1)], [("E", s)], scale=-1.0)
            _act(P, L1[s][:, sl], E[s][:, sl], AF.Ln, [("E", s)], [("L1", s)], bias=1.0)
            _act(P, U[s][:, sl], L1[s][:, sl], AF.Exp, [("L1", s)], [("U", s)], scale=-1.0)
            _ts(P, "dve", L2[s][:, sl], E[s][:, sl], CLIP, None, ALU.min, None, [("E", s)], [("L2", s)])
            _act(P, L2[s][:, sl], L2[s][:, sl], AF.Ln, [("L2", s), "lbv"], [("L2", s)], scale=lb[:, h:h + 1], bias=1.0)
            _tt(P, "dve", L2[s][:, sl], L2[s][:, sl], L1[s][:, sl], ALU.subtract, [("L2", s), ("L1", s)], [("L2", s)])
            if tb == 4:
                _tt(P, "dve", L2[s][:, sl], L2[s][:, sl], k.padmask[:], ALU.mult, [("L2", s), "padmask"], [("L2", s)])
            _stt(P, "dve", KT[s][:, sl], E[s][:, sl], oml[:, h:h + 1], U[s][:, sl], ALU.mult, ALU.mult,
                 [("E", s), ("U", s), "lbv"], [("KT", s)])
            if HG_LEVEL <= 2:
                continue
            P.op("dve", lambda e, s=s, sl=sl, t0=t0, tn=tn: e.tensor_tensor_scan(
                BT[s][:, sl], k.cmask[:, t0:t0 + tn], L2[s][:, sl], 0.0, ALU.mult, ALU.add),
                [("L2", s), "cmask"], [("BT", s)])
            b3 = BT[s][:, sl].rearrange("p (c j) -> p c j", j=32)
            _act(P, Dd[s][:, 0:nch], b3[:, :, 31], AF.Exp, [("BT", s)], [("Dd", s)])
            P.op("dve", lambda e, s=s, sl=sl, b3=b3, nch=nch: e.tensor_tensor(
                R2[s][:, sl].rearrange("p (c j) -> p c j", j=32), b3[:, :, 31:32].to_broadcast([128, nch, 32]), b3, ALU.subtract),
                [("BT", s)], [("R2", s)])
            _act(P, EX[0][:, sl], BT[s][:, sl], AF.Exp, [("BT", s)], [("EX", 0)])
            _tt(P, "dve", Qt[s][:, sl], QTA[:, t0:t0 + tn], EX[0][:, sl], ALU.mult, [("QTA", tb), ("EX", 0)], [("Qt", s)])
            _act(P, EX[1][:, sl], BT[s][:, sl], AF.Exp, [("BT", s)], [("EX", 1)], scale=-1.0)
            _tt(P, "pool", Kt[s][:, sl], KT[s][:, sl], EX[1][:, sl], ALU.mult, [("KT", s), ("EX", 1)], [("Kt", s)])
            _act(P, EX[2][:, sl], R2[s][:, sl], AF.Exp, [("R2", s)], [("EX", 2)])
            _tt(P, "pool", Kh[s][:, sl], KT[s][:, sl], EX[2][:, sl], ALU.mult, [("KT", s), ("EX", 2)], [("Kh", s)])
            if HG_LEVEL <= 3:
                continue
            for i in range(ntile):
                n = t0 // 128 + i
                for kc in range(8):
                    _mm(P, pp[0][:, i * 128:(i + 1) * 128], k.hT[:, kc, n * 128:(n + 1) * 128], wb[:, kc, 2, :], kc == 0, kc == 7,
                        [("Wh", h % 2, 2), ("hT", n)], [("pp", 0)], inc=(kc == 7 and i == ntile - 1))
            _cp(P, "act", V[s][:, 0:ntile, :], pp[0][:, 0:tn].rearrange("p (a b) -> p a b", b=128), [("pp", 0)], [("V", s)])
            for c in range(4):
                P.op("act", lambda e, s=s, c=c, ntile=ntile, tn=tn: e.mul(
                    Vblk[s][:, 0:ntile, c, :], pp[0][:, 0:tn].rearrange("p (a b) -> p a b", b=128), k.rmask[:, c:c + 1]),
                    [("pp", 0), "rmask"], [("Vblk", s)])
            for i in range(ntile):
                P.op("pe", lambda e, i=i, s=s: e.transpose(pk[:, i, :], Kh[s][:, i * 128:(i + 1) * 128], k.ident[:]),
                     [("Kh", s), "ident"], [("pAB", 0)], inc=(i == ntile - 1))
            _cp(P, "act", KH[s][:, 0:ntile, :], pk[:, 0:ntile, :], [("pAB", 0)], [("KH", s)])
            if HG_LEVEL <= 4:
                continue
            po = pot[tb % 2]
            kpo = ("pot", tb % 2)
            for i in range(ntile):
                n = t0 // 128 + i
                cs = slice(i * 128, (i + 1) * 128)
                pat = pAB[i % 2][:, 0:128]
                _mm(P, pat, Kt[s][:, cs], Qt[s][:, cs], True, True, [("Kt", s), ("Qt", s)], [("pAB", i % 2)], True)
                am = ATm[i % 2]
                _tt(P, "dve", am[:], pat, k.m01[:], ALU.mult, [("pAB", i % 2), "m01"], [("ATm", i % 2)])
                pd = pdS[i % 2]
                if HG_SUB <= 1:
                    continue
                _mm(P, pd[:], KH[s][:, i, :], Vblk[s][:, i, :, :].rearrange("p c d -> p (c d)"), True, True,
                    [("KH", s), ("Vblk", s)], [("pdS", i % 2)], True)
                if HG_SUB <= 2:
                    continue
                _mm(P, po[:, cs], V[s][:, i, :], am[:], True, HG_SUB <= 3, [("V", s), ("ATm", i % 2)], [kpo], inc=(HG_SUB <= 3))
                if HG_SUB <= 3:
                    continue
                for c in range(4):
                    if tb == 4:
                        si = (sidx[0] + 1) % 4
                        sidx[0] = si
                        P.dma("sp", S32[si][:], k.sh[j, c, h], writes=[("S32", si)])
                        _cp(P, "act", Sbf[si][:], S32[si][:], [("S32", si)], [("Sbf", si)])
                    si = sidx[0]
                    cc = slice(i * 128 + 32 * c, i * 128 + 32 * c + 32)
                    _mm(P, po[:, cc], Sbf[si][:], Qt[s][:, cc], False, True, [("Sbf", si), ("Qt", s)], [kpo], inc=True)
                    if HG_SUB <= 4:
                        continue
                    so = (si + 1) % 4
                    chl = i * 4 + c
                    _stt(P, "dve", S32[so][:], S32[si][:], Dd[s][:, chl:chl + 1], pd[:, c * 128:(c + 1) * 128], ALU.mult, ALU.add,
                         [("S32", si), ("Dd", s), ("pdS", i % 2)], [("S32", so)])
                    _cp(P, "act", Sbf[so][:], S32[so][:], [("S32", so)], [("Sbf", so)])
                    sidx[0] = so
                    if tb == 3 and i == 3 and c == 3:
                        P.dma("sp", k.hp[j, h], S32[so][:], reads=[("S32", so)])
                    if tb == 4:
                        P.dma("sp", k.hs[j, c, h], S32[so][:], reads=[("S32", so)])
            if HG_LEVEL <= 5:
                continue
            ob = Osb[tb % 2]
            _cp(P, "act", ob[:, sl], po[:, sl], [kpo], [("Osb", tb % 2)])
            _act(P, SQ[tb % 2][:, sl], po[:, sl], AF.Square, [kpo], [("SQ", tb % 2)])
            _mm(P, pp[0][:, sl], k.onesf[:], SQ[tb % 2][:, sl], True, True, [("SQ", tb % 2), "onesf"], [("pp", 0)], True)
            _act(P, RSD[tb % 2][:, sl], pp[0][:, sl], AF.Ln, [("pp", 0)], [("RSD", tb % 2)], scale=1.0 / 128, bias=EPS)
            _act(P, RSD[tb % 2][:, sl], RSD[tb % 2][:, sl], AF.Exp, [("RSD", tb % 2)], [("RSD", tb % 2)], scale=-0.5)
            _stt(P, "dve", T1[tb % 2][:, sl], ob[:, sl], onw[:, h:h + 1], GsA[:, t0:t0 + tn], ALU.mult, ALU.mult,
                 [("Osb", tb % 2), ("GsA", tb), "onwv"], [("T1", tb % 2)])
            _tt(P, "pool", k.OT[:, h, t0:t0 + tn], T1[tb % 2][:, sl], RSD[tb % 2][:, sl], ALU.mult,
                [("T1", tb % 2), ("RSD", tb % 2)], [("OT", n) for n in range(t0 // 128, (t0 + tn) // 128)])
        if HG_LEVEL <= 6:
            return


GROUPS = ((128, 1), (512, 4), (2048, 16))


def g_tiles(g):
    tiles = []
    if g == 0:
        for n in range(16):
            tiles.append(dict(cols=slice(128 * n, 128 * n + 128), qr=n // 4, oc=slice((n % 4) * 128, (n % 4) * 128 + 128),
                              prev=(n - 1 if n >= 1 else None)))
    elif g == 1:
        for rho in range(4):
            for jb in range(4):
                tiles.append(dict(cols=slice(512 * jb + rho, 512 * jb + 512, 4), qr=jb, oc=slice(rho, 512, 4),
                                  prev=(rho * 4 + jb - 1 if jb >= 1 else None)))
    else:
        for rho in range(16):
            tiles.append(dict(cols=slice(rho, 2048, 16), qr=None, oc=None, prev=None))
    return tiles


def phase_attn(k, P, st):
    nc = k.nc
    sb = lambda name, shape, dt: st.enter_context(nc.sbuf_tensor(name, shape, dt))
    ps = lambda name, shape, dt: st.enter_context(nc.psum_tensor(name, shape, dt))
    win = k.b_w_in.rearrange("(kc p) (b h c) -> p kc b h c", p=128, b=10, h=8)
    k.identf = sb("identf", [128, 128], F32)
    k.cosT = sb("cosT", [128, T], F32)
    k.sinT = sb("sinT", [128, T], F32)
    k.permS = sb("permS", [128, 128], F32)
    k.mpo = sb("mpo", [128, 2, 128], BF16)
    k.mnew = sb("mnew", [128, 2, 128], BF16)
    k.mc9 = sb("mc9", [128, 16], BF16)
    P.dma("sp", k.identf[:], k.c_ident, writes=["identf"])
    P.dma("sp", k.cosT[:], k.c_cosT, writes=["cosT"])
    P.dma("sp", k.sinT[:], k.c_sinT, writes=["sinT"])
    P.dma("sp", k.permS[:], k.c_permS, writes=["permS"])
    P.dma("pool", k.mpo[:], k.c_mpo, writes=["mpo"])
    P.dma("pool", k.mnew[:], k.c_mnew, writes=["mnew"])
    P.dma("pool", k.mc9[:], k.c_mc9, writes=["mc9"])
    SCALE = 128.0 ** -0.5
    Wqk = [sb(f"Wqk{i}", [128, 8, 2, 128], BF16) for i in range(3)]
    Wv = [sb(f"Wv{i}", [128, 8, 128], BF16) for i in range(3)]
    Wg = [sb(f"Wg{i}", [128, 8, 128], BF16) for i in range(2)]
    QT = [sb(f"aQT{g}", [128, T], BF16) for g in range(3)]
    KT = [sb(f"aKT{g}", [128, T], BF16) for g in range(3)]
    Vt = [sb(f"aVt{g}", [128, 17, 128], BF16) for g in range(3)]
    GT = sb("aGT", [128, T], BF16)
    PT2 = sb("aPT2", [128, 16, 128], BF16)
    RAW = [sb(f"aRAW{i}", [128, 512], F32) for i in range(2)]
    T1 = [sb(f"aT1{i}", [128, 512], F32) for i in range(2)]
    T2 = [sb(f"aT2{i}", [128, 512], F32) for i in range(2)]
    KR = [sb(f"aKR{i}", [128, 512], F32) for i in range(2)]
    KO = [sb(f"aKO{i}", [128, 4, 128], F32) for i in range(2)]
    VO = [sb(f"aVO{i}", [128, 4, 128], F32) for i in range(2)]
    EXs = [sb(f"aEX{i}", [128, 512], BF16) for i in range(2)]
    PTb = [sb(f"aPT{i}", [128, 512], BF16) for i in range(3)]
    REC = sb("aREC", [128, 512], F32)
    ON = sb("aON", [128, 512], F32)
    CK = [sb(f"aCK{i}", [128, 9, 2, 128], BF16) for i in range(2)]
    KcT = [sb(f"aKcT{i}", [128, 9, 128], BF16) for i in range(2)]
    PS9 = [sb(f"aPS9{i}", [128, 16], BF16) for i in range(2)]
    EX9 = [sb(f"aEX9{i}", [128, 16], BF16) for i in range(2)]
    zer = sb("azer", [128, 128], BF16)
    onesb = sb("aonesb", [128, 128], BF16)
    pq = [ps(f"apq{i}", [128, 512], F32) for i in range(2)]
    ppm = ps("appm", [128, 512], F32)
    pS = [ps(f"apS{i}", [128, 512], F32) for i in range(2)]
    pN = ps("apN", [128, 512], F32)
    pD = ps("apD", [128, 512], F32)
    ptr = ps("aptr", [128, 512], F32)
    ptr_bf = ptr[:].bitcast(BF16)
    P.op("pool", lambda e: e.memset(zer[:], 0.0), [], ["zer"])
    P.op("pool", lambda e: e.memset(onesb[:], 1.0), [], ["onesb"])

    caches = (k.c128, k.c512, k.c2048)
    outs_p = (k.k128p, k.k512p, k.k2048p)
    outs_s = (k.k128s, k.k512s, k.k2048s)

    def load_w(h, g):
        i = (h * 3 + g) % 3
        for m in range(2):
            P.dma("pool", Wqk[i][:, :, m, :], win[:, :, 3 * g + m, h, :], writes=[("Wqk", i)])
        P.dma("pool", Wv[i][:], win[:, :, 3 * g + 2, h, :], writes=[("Wv", i)])
        if g == 0:
            P.dma("pool", Wg[h % 2][:], win[:, :, 9, h, :], writes=[("Wg", h % 2)])

    def load_cache(h, b):
        ck = CK[(h * 4 + b) % 2]
        key = ("CK", (h * 4 + b) % 2)
        P.dma("pool", ck[:, 0, :, :], k.c128[b, :, :, h, :], writes=[key])
        for t in range(4):
            P.dma("pool", ck[:, 1 + t, :, :], k.c512[b, t:512:4, :, h, :], writes=[key])
            P.dma("pool", ck[:, 5 + t, :, :], k.c2048[b, t:2048:16, :, h, :], writes=[key])

    hTk = lambda t0, tn: [("hT", n) for n in range(t0 // 128, (t0 + tn) // 128)]
    bcnt = [0]
    load_w(0, 0)
    for h in range(8):
        for tb in range(5):
            t0, tn = TBS[tb]
            pb = pq[tb % 2]
            for kc in range(8):
                _mm(P, pb[:, 0:tn], Wg[h % 2][:, kc, :], k.hT[:, kc, t0:t0 + tn], kc == 0, kc == 7,
                    [("Wg", h % 2)] + hTk(t0, tn), [("pq", tb % 2)], inc=(kc == 7))
            _act(P, GT[:, t0:t0 + tn], pb[:, 0:tn], AF.Silu, [("pq", tb % 2)], [("GT", tb)])
        for g in range(3):
            W, dil = GROUPS[g]
            wi = (h * 3 + g) % 3
            if g < 2:
                load_w(h, g + 1)
            elif h < 7:
                load_w(h + 1, 0)
            blocks = [(m, tb) for m in range(2) for tb in range(5)]
            binfo = {}

            def stage_a(i):
                m, tb = blocks[i]
                t0, tn = TBS[tb]
                bi = bcnt[0] % 2
                bcnt[0] += 1
                binfo[i] = bi
                pb = pq[bi]
                for kc in range(8):
                    _mm(P, pb[:, 0:tn], Wqk[wi][:, kc, m, :], k.hT[:, kc, t0:t0 + tn], kc == 0, kc == 7,
                        [("Wqk", wi)] + hTk(t0, tn), [("pq", bi)], inc=(kc == 7))
                _cp(P, "act", RAW[bi][:, 0:tn], pb[:, 0:tn], [("pq", bi)], [("RAW", bi)])

            def stage_b(i):
                m, tb = blocks[i]
                t0, tn = TBS[tb]
                bi = binfo[i]
                _mm(P, ppm[:, 0:tn], k.permS[:], RAW[bi][:, 0:tn], True, True, [("RAW", bi), "permS"], ["ppm"], True)
                _tt(P, "dve", T1[bi][:, 0:tn], RAW[bi][:, 0:tn], k.cosT[:, t0:t0 + tn], ALU.mult, [("RAW", bi), "cosT"], [("T1", bi)])
                _tt(P, "dve", T2[bi][:, 0:tn], ppm[:, 0:tn], k.sinT[:, t0:t0 + tn], ALU.mult, ["ppm", "sinT"], [("T2", bi)])
                dst = QT[g] if m == 0 else KT[g]
                dkey = ("QT" if m == 0 else "KT", g, tb)
                _tt(P, "pool", dst[:, t0:t0 + tn], T1[bi][:, 0:tn], T2[bi][:, 0:tn], ALU.add, [("T1", bi), ("T2", bi)], [dkey])
                keep_tiles = [n for n in range(t0 // 128, (t0 + tn) // 128) if n == 16 or (n + 1) * 128 > SEQ - W]
                if m == 1 and keep_tiles:
                    _tt(P, "dve", KR[bi][:, 0:tn], T1[bi][:, 0:tn], T2[bi][:, 0:tn], ALU.add, [("T1", bi), ("T2", bi)], [("KR", bi)])
                    for n in keep_tiles:
                        il = n - t0 // 128
                        P.op("pe", lambda e, il=il, bi=bi: e.transpose(ptr[:, il * 128:(il + 1) * 128], KR[bi][:, il * 128:(il + 1) * 128], k.identf[:]),
                             [("KR", bi), "identf"], ["ptr"], inc=(n == keep_tiles[-1]))
                    i0 = keep_tiles[0] - t0 // 128
                    nk = len(keep_tiles)
                    kob = KO[bi]
                    _cp(P, "act", kob[:, 0:nk, :], ptr[:, i0 * 128:(i0 + nk) * 128].rearrange("p (a b) -> p a b", b=128),
                        ["ptr"], [("KO", bi)])
                    for a_, n in enumerate(keep_tiles):
                        if n < 16:
                            r0 = n * 128 - (SEQ - W)
                            P.dma("sp", outs_p[g][r0:r0 + 128, 0, h, :], kob[:, a_, :], reads=[("KO", bi)])
                        else:
                            for b in range(4):
                                P.dma("sp", outs_s[g][b, W - 4:W, 0, h, :], kob[32 * b:32 * b + 4, a_, :], reads=[("KO", bi)])

            stage_a(0)
            for i in range(len(blocks)):
                if i + 1 < len(blocks):
                    stage_a(i + 1)
                stage_b(i)
            tiles = g_tiles(g)
            vt = Vt[g]
            allt = [tl["cols"] for tl in tiles] + [slice(2048, 2176)]
            for c0 in range(0, 17, 4):
                idxs = list(range(c0, min(c0 + 4, 17)))
                bi = bcnt[0] % 2
                bcnt[0] += 1
                pb = pq[bi]
                for a_, ti in enumerate(idxs):
                    cs = allt[ti]
                    ntl = (2048 // 128) if ti == 16 else None
                    for kc in range(8):
                        _mm(P, pb[:, a_ * 128:(a_ + 1) * 128], k.hT[:, kc, cs], Wv[wi][:, kc, :], kc == 0, kc == 7,
                            [("Wv", wi)] + [("hT", n) for n in range(17)], [("pq", bi)], inc=(kc == 7 and ti == idxs[-1]))
                na = len(idxs)
                vob = VO[bi]
                _cp(P, "act", vob[:, 0:na, :], pb[:, 0:na * 128].rearrange("p (a b) -> p a b", b=128), [("pq", bi)], [("VO", bi)])
                _cp(P, "act", vt[:, c0:c0 + na, :], pb[:, 0:na * 128].rearrange("p (a b) -> p a b", b=128), [("pq", bi)], [("Vt", g, c0 // 4)])
                for a_, ti in enumerate(idxs):
                    if ti == 16:
                        for b in range(4):
                            P.dma("sp", outs_s[g][b, W - 4:W, 1, h, :], vob[32 * b:32 * b + 4, a_, :], reads=[("VO", bi)])
                        continue
                    cs = allt[ti]
                    first = cs.start
                    step = cs.step or 1
                    last = first + step * 127
                    if last < SEQ - W:
                        continue
                    r0 = first - (SEQ - W)
                    if r0 < 0:
                        continue
                    P.dma("sp", outs_p[g][r0:r0 + step * 127 + 1:step, 1, h, :], vob[:, a_, :], reads=[("VO", bi)])
        vkeys = lambda g: [("Vt", g, i) for i in range(5)]
        qkeys = lambda g: [("QT", g, i) for i in range(5)]
        kkeys = lambda g: [("KT", g, i) for i in range(5)]
        t2 = g_tiles(2)
        for c0 in range(0, 16, 4):
            bi = (c0 // 4) % 2
            for a_ in range(4):
                cs = t2[c0 + a_]["cols"]
                _mm(P, pS[bi][:, a_ * 128:(a_ + 1) * 128], KT[2][:, cs], QT[2][:, cs], True, True,
                    kkeys(2) + qkeys(2), [("pS", bi)], inc=(a_ == 3))
            _act(P, EXs[bi][:], pS[bi][:], AF.Exp, [("pS", bi)], [("EXs", bi)], scale=SCALE)
            P.op("pool", lambda e, bi=bi, c0=c0: e.tensor_tensor(
                PT2[:, c0:c0 + 4, :], EXs[bi][:].rearrange("p (a b) -> p a b", b=128),
                k.mpo[:, 1:2, :].to_broadcast([128, 4, 128]), ALU.mult), [("EXs", bi), "mpo"], [("PT2", c0 // 4)])
        pcnt = [0]
        for qr in range(5):
            wq = 512 if qr < 4 else 128
            _mm(P, pN[:, 0:wq], zer[:], k.hT[:, 0, 0:wq], True, False, ["zer", ("hT", 0), ("hT", 1), ("hT", 2), ("hT", 3)], ["pN"], inc=False)
            _mm(P, pD[:, 0:wq], zer[:], k.hT[:, 0, 0:wq], True, False, ["zer", ("hT", 0), ("hT", 1), ("hT", 2), ("hT", 3)], ["pD"], inc=False)

            def pv(vl, pt, oc, rk):
                P.op("pe", lambda e: e.matmul(pN[:, oc], vl, pt, start=False, stop=False, skip_group_check=True), rk, ["pN"], inc=False)
                P.op("pe", lambda e: e.matmul(pD[:, oc], onesb[:], pt, start=False, stop=False, skip_group_check=True), rk + ["onesb"], ["pD"], inc=True)

            if qr < 4:
                work = []
                for g in range(2):
                    tiles = g_tiles(g)
                    for ti, tl in enumerate(tiles):
                        if tl["qr"] == qr:
                            work.append((g, tiles, ti, tl))
                winfo = {}

                def sc_a(j):
                    g, tiles, ti, tl = work[j]
                    pi = pcnt[0] % 2
                    pcnt[0] += 1
                    kbs = ([tl["prev"]] if tl["prev"] is not None else []) + [ti]
                    off = 2 - len(kbs)
                    winfo[j] = (pi, kbs, off)
                    for x, kb in enumerate(kbs):
                        _mm(P, pS[pi][:, (off + x) * 128:(off + x + 1) * 128], KT[g][:, tiles[kb]["cols"]], QT[g][:, tl["cols"]], True, True,
                            kkeys(g) + qkeys(g), [("pS", pi)], inc=(x == len(kbs) - 1))

                def sc_b(j):
                    g, tiles, ti, tl = work[j]
                    pi, kbs, off = winfo[j]
                    ex = EXs[pi]
                    _act(P, ex[:, off * 128:256], pS[pi][:, off * 128:256], AF.Exp, [("pS", pi)], [("EXs", pi)], scale=SCALE)
                    pt = PTb[j % 3]
                    kpt = ("PTb", j % 3)
                    P.op("pool", lambda e, pt=pt, ex=ex, off=off: e.tensor_tensor(
                        pt[:, off * 128:256], ex[:, off * 128:256], k.mpo[:, off:2, :].rearrange("p a b -> p (a b)"), ALU.mult),
                        [("EXs", pi), "mpo"], [kpt])
                    for x, kb in enumerate(kbs):
                        pv(Vt[g][:, kb, :], pt[:, (off + x) * 128:(off + x + 1) * 128], tl["oc"], [kpt] + vkeys(g))

                sc_a(0)
                for j in range(len(work)):
                    if j + 1 < len(work):
                        sc_a(j + 1)
                    sc_b(j)
                for rho in range(16):
                    pv(Vt[2][:, rho, :], PT2[:, rho, 32 * qr:32 * qr + 32], slice(rho, 512, 16), [("PT2", rho // 4)] + vkeys(2))
            else:
                for g in range(3):
                    pi = pcnt[0] % 2
                    pcnt[0] += 1
                    _mm(P, pS[pi][:, 0:128], KT[g][:, 2048:2176], QT[g][:, 2048:2176], True, True, kkeys(g) + qkeys(g), [("pS", pi)], True)
                    ex = EXs[pi]
                    _act(P, ex[:, 0:128], pS[pi][:, 0:128], AF.Exp, [("pS", pi)], [("EXs", pi)], scale=SCALE)
                    pt = PTb[pcnt[0] % 3]
                    kpt = ("PTb", pcnt[0] % 3)
                    msk = k.mnew[:, 0, :] if g == 0 else k.mnew[:, 1, :]
                    _tt(P, "pool", pt[:, 0:128], ex[:, 0:128], msk, ALU.mult, [("EXs", pi), "mnew"], [kpt])
                    pv(Vt[g][:, 16, :], pt[:, 0:128], slice(0, 128), [kpt] + vkeys(g))
                def ca(b):
                    ci = (h * 4 + b) % 2
                    if b == 0:
                        load_cache(h, 0)
                        load_cache(h, 1)
                    ck = CK[ci]
                    kck = ("CK", ci)
                    for x in range(9):
                        dstp = ptr_bf[:, x * 128:(x + 1) * 128] if x < 8 else ptr_bf[:, 0:128]
                        P.op("pe", lambda e, x=x, ck=ck, dstp=dstp: e.transpose(dstp, ck[:, x, 0, :], k.ident[:]),
                             [kck, "ident"], ["ptr"], inc=(x == 7 or x == 8))
                        if x == 7:
                            _cp(P, "act", KcT[ci][:, 0:8, :], ptr_bf[:, 0:1024].rearrange("p (a b) -> p a b", b=128), ["ptr"], [("KcT", ci, 0)])
                        if x == 8:
                            _cp(P, "act", KcT[ci][:, 8, :], ptr_bf[:, 0:128], ["ptr"], [("KcT", ci, 1)])

                def cb(b):
                    ci = (h * 4 + b) % 2
                    ck = CK[ci]
                    kck = ("CK", ci)
                    pi = pcnt[0] % 2
                    pcnt[0] += 1
                    qb = 2048 + 32 * b
                    _mm(P, pS[pi][:, 0:4], KcT[ci][:, 0, :], QT[0][:, qb:qb + 4], True, True, [("KcT", ci, 0)] + qkeys(0), [("pS", pi)], inc=False)
                    for t in range(4):
                        _mm(P, pS[pi][:, 4 + t:5 + t], KcT[ci][:, 1 + t, :], QT[1][:, qb + t:qb + t + 1], True, True,
                            [("KcT", ci, 0)] + qkeys(1), [("pS", pi)], inc=False)
                        _mm(P, pS[pi][:, 8 + t:9 + t], KcT[ci][:, 5 + t, :], QT[2][:, qb + t:qb + t + 1], True, True,
                            [("KcT", ci, 0), ("KcT", ci, 1)] + qkeys(2), [("pS", pi)], inc=(t == 3))
                    _act(P, EX9[ci][:, 0:12], pS[pi][:, 0:12], AF.Exp, [("pS", pi)], [("EX9", ci)], scale=SCALE)
                    _tt(P, "pool", PS9[ci][:, 0:12], EX9[ci][:, 0:12], k.mc9[:, 0:12], ALU.mult, [("EX9", ci), "mc9"], [("PS9", ci)])
                    pv(ck[:, 0, 1, :], PS9[ci][:, 0:4], slice(32 * b, 32 * b + 4), [("PS9", ci), kck])
                    for t in range(4):
                        pv(ck[:, 1 + t, 1, :], PS9[ci][:, 4 + t:5 + t], slice(32 * b + t, 32 * b + t + 1), [("PS9", ci), kck])
                        pv(ck[:, 5 + t, 1, :], PS9[ci][:, 8 + t:9 + t], slice(32 * b + t, 32 * b + t + 1), [("PS9", ci), kck])

                ca(0)
                for b in range(4):
                    if b + 1 < 4:
                        ca(b + 1)
                    cb(b)
                    if b + 2 < 4:
                        load_cache(h, b + 2)
            P.op("pe", lambda e, wq=wq: e.matmul(pN[:, 0:wq], zer[:], k.hT[:, 0, 0:wq], start=False, stop=True, skip_group_check=True), ["zer"], ["pN"], inc=False)
            P.op("pe", lambda e, wq=wq: e.matmul(pD[:, 0:wq], zer[:], k.hT[:, 0, 0:wq], start=False, stop=True, skip_group_check=True), ["zer"], ["pD"], inc=True)
            q0 = 512 * qr
            P.op("dve", lambda e, wq=wq: e.reciprocal(REC[:, 0:wq], pD[:, 0:wq]), ["pD"], ["REC"])
            _tt(P, "dve", ON[:, 0:wq], pN[:, 0:wq], REC[:, 0:wq], ALU.mult, ["pN", "REC"], ["ON"])
            _tt(P, "pool", k.OT[:, h, q0:q0 + wq], ON[:, 0:wq], GT[:, q0:q0 + wq], ALU.mult, ["ON", ("GT", qr)],
                [("OT", n) for n in range(q0 // 128, (q0 + wq) // 128)])


PI = math.pi
TWO_PI = 2.0 * math.pi
CW1 = float(np.float32(6.28125))
CW2 = float(np.float32(TWO_PI - 6.28125))
CW3 = float(TWO_PI - CW1 - CW2)


def phase_s5(k, P, lst):
    nc = k.nc
    f1 = lambda st: (lambda name, shape, dt: st.enter_context(nc.sbuf_tensor(name, shape, dt)))
    f2 = lambda st: (lambda name, shape, dt: st.enter_context(nc.psum_tensor(name, shape, dt)))
    lsb = f1(lst)
    YG = lsb("sYG", [128, 8, T], BF16)
    XLo = lsb("sXLo", [64, 5, 64, 2], F32)
    win = k.c_w_in.rearrange("(kc p) (b m c) -> p kc b m c", p=128, b=2, m=8)
    hTk = lambda t0, tn: [("hT", n) for n in range(t0 // 128, (t0 + tn) // 128)]

    with contextlib.ExitStack() as st:
        sb, ps = f1(st), f2(st)
        uT = sb("suT", [128, 8, T], BF16)
        with contextlib.ExitStack() as st2:
            sb2, ps2 = f1(st2), f2(st2)
            Wu = sb2("sWu", [128, 8, 8, 128], BF16)
            pu = [ps2(f"spu{i}", [128, 512], F32) for i in range(2)]
            for m in range(8):
                P.dma("pool", Wu[:, :, m, :], win[:, :, 0, m, :], writes=[("Wu", m)])
            cnt = 0
            for m in range(8):
                for tb in range(5):
                    t0, tn = TBS[tb]
                    pb = pu[cnt % 2]
                    for kc in range(8):
                        _mm(P, pb[:, 0:tn], Wu[:, kc, m, :], k.hT[:, kc, t0:t0 + tn], kc == 0, kc == 7,
                            [("Wu", m)] + hTk(t0, tn), [("pu", cnt % 2)], inc=(kc == 7))
                    _cp(P, "act", uT[:, m, t0:t0 + tn], pb[:, 0:tn], [("pu", cnt % 2)], [("uT", m, tb)])
                    cnt += 1
            P.barrier()
            P.emit()
        BL = sb("sBL", [128, 8, 2, 128], BF16)
        BLp = sb("sBLp", [128, 8, 2, 128], BF16)
        CL1 = sb("sCL1", [128, 64, 32], BF16)
        CL2 = sb("sCL2", [128, 64, 32], BF16)
        DG = sb("sDG", [128, 8, 32], BF16)
        rr = sb("srr", [128, 64], F32)
        AB = sb("sAB", [128, 64, 96], F32)
        RS0 = sb("sRS0", [128, 64, 4], F32)
        VL = sb("sVL", [128, 64, 5], F32)
        CLt = sb("sCLt", [128, 64, 5], F32)
        SLt = sb("sSLt", [128, 64, 5], F32)
        with contextlib.ExitStack() as st2:
            sb2, ps2 = f1(st2), f2(st2)
            sm = lambda name: sb2(name, [128, 64], F32)
            are, aim, ldt = sm("s_are"), sm("s_aim"), sm("s_ldt")
            dt_, th, x1, x2, cs_, sn_ = sm("s_dt"), sm("s_th"), sm("s_x1"), sm("s_x2"), sm("s_cs"), sm("s_sn")
            bre, bim, den, xr, cre, cim, cis = sm("s_bre"), sm("s_bim"), sm("s_den"), sm("s_xr"), sm("s_cre"), sm("s_cim"), sm("s_cis")
            bT1 = sb2("s_bT1", [128, 64, 16], F32)
            bT2 = sb2("s_bT2", [128, 64, 16], F32)
            BBm = sb2("s_BB", [128, 64, 16], F32)
            BB2 = sb2("s_BB2", [128, 64, 16], F32)
            cTr = sb2("s_cTr", [128, 64, 16], F32)
            cTi = sb2("s_cTi", [128, 64, 16], F32)
            s0a = sb2("s_s0a", [128, 64, 4], F32)
            mulc = sb2("s_mul", [128, 96], F32)
            ki = sb2("s_ki", [128, 32, 96], mybir.dt.int32)
            evo = sb2("s_evo", [128, 4], F32)
            dd = sb2("s_dd", [128, 8], F32)
            identf = sb2("s_identf", [128, 128], F32)
            ptp = ps2("s_ptp", [128, 128], F32)
            for tl, src in ((are, k.s_are), (aim, k.s_aim), (ldt, k.s_ldt), (bT1, k.s_bT1), (bT2, k.s_bT2), (cTr, k.s_cTr),
                            (cTi, k.s_cTi), (s0a, k.s_s0), (mulc, k.c_mul), (evo, k.c_evo), (dd, k.s_dd), (identf, k.c_ident)):
                P.dma("sp", tl[:], src, writes=[id(tl)])
            V = "dve"
            _ts(P, V, are[:], are[:], -1e-4, None, ALU.min, None, [id(are)], [id(are)])
            _act(P, dt_[:], ldt[:], AF.Exp, [id(ldt)], [id(dt_)])
            _tt(P, V, x1[:], are[:], dt_[:], ALU.mult, [id(are), id(dt_)], [id(x1)])
            _act(P, rr[:], x1[:], AF.Exp, [id(x1)], ["rr"])
            _tt(P, V, th[:], aim[:], dt_[:], ALU.mult, [id(aim), id(dt_)], [id(th)])
            P.op(V, lambda e: e.tensor_tensor(AB[:], th[:].unsqueeze(2).to_broadcast([128, 64, 96]),
                                              mulc[:].unsqueeze(1).to_broadcast([128, 64, 96]), ALU.mult), [id(th), id(mulc)], ["AB"])
            for hf in range(2):
                gs_ = slice(32 * hf, 32 * hf + 32)
                _ts(P, V, ki[:], AB[:, gs_, :], 1.0 / TWO_PI, None, ALU.mult, None, ["AB"], [id(ki)])
                _stt(P, V, AB[:, gs_, :], ki[:], -CW1, AB[:, gs_, :], ALU.mult, ALU.add, [id(ki), "AB"], ["AB"])
                _stt(P, V, AB[:, gs_, :], ki[:], -(CW2 + CW3), AB[:, gs_, :], ALU.mult, ALU.add, [id(ki), "AB"], ["AB"])
            _act(P, sn_[:], AB[:, :, 32], AF.Sin, ["AB"], [id(sn_)])
            ki2 = ki[:, 0, 0:64]
            _ts(P, V, ki2, AB[:, :, 32], 1.0 / TWO_PI, 0.25, ALU.mult, ALU.add, ["AB"], [id(ki)])
            _stt(P, V, x2[:], ki2, -TWO_PI, AB[:, :, 32], ALU.mult, ALU.add, [id(ki), "AB"], [id(x2)])
            _act(P, cs_[:], x2[:], AF.Sin, [id(x2)], [id(cs_)], bias=PI / 2)
            _tt(P, V, bre[:], rr[:], cs_[:], ALU.mult, ["rr", id(cs_)], [id(bre)])
            _tt(P, V, bim[:], rr[:], sn_[:], ALU.mult, ["rr", id(sn_)], [id(bim)])
            _tt(P, V, den[:], are[:], are[:], ALU.mult, [id(are)], [id(den)])
            _tt(P, V, x1[:], aim[:], aim[:], ALU.mult, [id(aim)], [id(x1)])
            _tt(P, V, den[:], den[:], x1[:], ALU.add, [id(den), id(x1)], [id(den)])
            P.op(V, lambda e: e.reciprocal(den[:], den[:]), [id(den)], [id(den)])
            _ts(P, V, xr[:], bre[:], -1.0, None, ALU.add, None, [id(bre)], [id(xr)])
            _tt(P, V, cre[:], xr[:], are[:], ALU.mult, [id(xr), id(are)], [id(cre)])
            _tt(P, V, x1[:], bim[:], aim[:], ALU.mult, [id(bim), id(aim)], [id(x1)])
            _tt(P, V, cre[:], cre[:], x1[:], ALU.add, [id(cre), id(x1)], [id(cre)])
            _tt(P, V, cre[:], cre[:], den[:], ALU.mult, [id(cre), id(den)], [id(cre)])
            _tt(P, V, cim[:], bim[:], are[:], ALU.mult, [id(bim), id(are)], [id(cim)])
            _tt(P, V, x1[:], xr[:], aim[:], ALU.mult, [id(xr), id(aim)], [id(x1)])
            _tt(P, V, cim[:], cim[:], x1[:], ALU.subtract, [id(cim), id(x1)], [id(cim)])
            _tt(P, V, cim[:], cim[:], den[:], ALU.mult, [id(cim), id(den)], [id(cim)])
            _ts(P, V, cis[0:64, :], cim[0:64, :], -1.0, None, ALU.mult, None, [id(cim)], [id(cis)])
            _cp(P, V, cis[64:128, :], cim[64:128, :], [id(cim)], [id(cis)])
            P.op(V, lambda e: e.tensor_tensor(BBm[:], bT1[:], cre[:].unsqueeze(2).to_broadcast([128, 64, 16]), ALU.mult), [id(bT1), id(cre)], [id(BBm)])
            P.op(V, lambda e: e.tensor_tensor(BB2[:], bT2[:], cis[:].unsqueeze(2).to_broadcast([128, 64, 16]), ALU.mult), [id(bT2), id(cis)], [id(BB2)])
            _tt(P, V, BBm[:], BBm[:], BB2[:], ALU.add, [id(BBm), id(BB2)], [id(BBm)])
            for t8 in range(8):
                P.op("pe", lambda e, t8=t8: e.transpose(ptp[:], BBm[:, 8 * t8:8 * t8 + 8, :].rearrange("p a b -> p (a b)"), identf[:]),
                     [id(BBm), id(identf)], ["ptp"])
                for g2 in range(2):
                    _ts(P, V, BL[:, t8, g2, :], ptp[:], evo[:, g2:g2 + 1], None, ALU.mult, None, ["ptp", id(evo)], [("BL", t8)])
                    _ts(P, V, BLp[:, t8, g2, 0:64], ptp[:, 64:128], evo[:, g2:g2 + 1], None, ALU.mult, None, ["ptp", id(evo)], [("BLp", t8)])
                    _ts(P, V, BLp[:, t8, g2, 64:128], ptp[:, 0:64], evo[:, 2 + g2:3 + g2], None, ALU.mult, None, ["ptp", id(evo)], [("BLp", t8)])
            P.op("pool", lambda e: e.memset(CL1[:], 0.0), [], ["CL1"])
            P.op("pool", lambda e: e.memset(CL2[:], 0.0), [], ["CL2"])
            c1v = CL1[:].rearrange("p (a two) c -> p a two c", two=2)
            c2v = CL2[:].rearrange("p (a two) c -> p a two c", two=2)
            crv = cTr[:].rearrange("p (a two) c -> p a two c", two=2)
            civ = cTi[:].rearrange("p (a two) c -> p a two c", two=2)
            for g2 in range(2):
                cs16 = slice(16 * g2, 16 * g2 + 16)
                _cp(P, V, c1v[0:64, :, g2, cs16], crv[0:64, :, g2, :], [id(cTr)], ["CL1"])
                _ts(P, V, c1v[64:128, :, g2, cs16], civ[64:128, :, g2, :], -1.0, None, ALU.mult, None, [id(cTi)], ["CL1"])
                _ts(P, V, c2v[0:64, :, g2, cs16], civ[0:64, :, g2, :], -1.0, None, ALU.mult, None, [id(cTi)], ["CL2"])
                _ts(P, V, c2v[64:128, :, g2, cs16], crv[64:128, :, g2, :], -1.0, None, ALU.mult, None, [id(cTr)], ["CL2"])
            for t8 in range(8):
                for q4 in range(4):
                    _ts(P, V, DG[32 * q4:32 * q4 + 32, t8, :], identf[32 * q4:32 * q4 + 32, 32 * q4:32 * q4 + 32],
                        dd[32 * q4:32 * q4 + 32, t8:t8 + 1], None, ALU.mult, None, [id(identf), id(dd)], ["DG"])
            P.op(V, lambda e: e.tensor_tensor(RS0[:], s0a[:], rr[:].unsqueeze(2).to_broadcast([128, 64, 4]), ALU.mult), [id(s0a), "rr"], ["RS0"])
            P.barrier()
            P.emit()
        with contextlib.ExitStack() as st2:
            sb2, ps2 = f1(st2), f2(st2)
            ANG = [sb2(f"sANG{i}", [128, 512], F32) for i in range(3)]
            COS = [sb2(f"sCOS{i}", [128, 512], F32) for i in range(2)]
            SIN = [sb2(f"sSIN{i}", [128, 512], F32) for i in range(2)]
            KI1 = [sb2(f"sKI1{i}", [128, 512], mybir.dt.int32) for i in range(2)]
            BU = [sb2(f"sBU{i}", [128, 512], F32) for i in range(2)]
            BP = [sb2(f"sBP{i}", [128, 512], F32) for i in range(2)]
            T1 = [sb2(f"sT1{i}", [128, 512], F32) for i in range(2)]
            T2 = [sb2(f"sT2{i}", [128, 512], F32) for i in range(2)]
            Wt = [sb2(f"sW{i}", [128, 512], F32) for i in range(2)]
            Vv = [sb2(f"sV{i}", [128, 512], F32) for i in range(3)]
            CV = [sb2(f"sCV{i}", [128, 512], BF16) for i in range(2)]
            SV = [sb2(f"sSV{i}", [128, 512], BF16) for i in range(2)]
            pbu = ps2("spbu", [128, 512], F32)
            pbp = ps2("spbp", [128, 512], F32)
            py = [ps2(f"spy{i}", [128, 512], F32) for i in range(5)]
            pbb = [pbu, pbp, ps2("spb3", [128, 512], F32)]
            steps = [(g, tb) for g in range(64) for tb in range(5)]
            carry = {}

            def st_info(i):
                g, tb = steps[i]
                t8, q4, g2 = g // 8, (g % 8) // 2, g % 2
                t0, tn = TBS[tb]
                return g, tb, t8, q4, g2, t0, tn, slice(32 * q4, 32 * q4 + 32)

            def emit_bu(i):
                g, tb, t8, q4, g2, t0, tn, rows = st_info(i)
                urows = uT[rows, t8, t0:t0 + tn]
                ba, bb = (2 * i) % 3, (2 * i + 1) % 3
                _mm(P, pbb[ba][:, 0:tn], BL[rows, t8, g2, :], urows, True, True, [("BL", t8), ("uT", t8, tb)], [("pbb", ba)], True, tp=(32 * q4, 0))
                _mm(P, pbb[bb][:, 0:tn], BLp[rows, t8, g2, :], urows, True, True, [("BLp", t8), ("uT", t8, tb)], [("pbb", bb)], True, tp=(32 * q4, 0))

            def emit_ang(i):
                g, tb, t8, q4, g2, t0, tn, rows = st_info(i)
                bi = i % 3
                if tb < 4:
                    for a in range(8):
                        _act(P, ANG[bi][:, a * 64:(a + 1) * 64], AB[:, g, 32:96], AF.Identity, ["AB"], [("ANG", bi)],
                             bias=AB[:, g, 8 * tb + a:8 * tb + a + 1])
                else:
                    for b in range(4):
                        _cp(P, "act", ANG[bi][:, 32 * b:32 * b + 32], AB[:, g, 32:64], ["AB"], [("ANG", bi)])

            def emit_tables(i):
                g, tb, t8, q4, g2, t0, tn, rows = st_info(i)
                bi = i % 2
                ai = i % 3
                _ts(P, "dve", KI1[bi][:, 0:tn], ANG[ai][:, 0:tn], 1.0 / TWO_PI, 0.0, ALU.mult, ALU.add, [("ANG", ai)], [("KI1", bi)])
                _stt(P, "dve", SIN[bi][:, 0:tn], KI1[bi][:, 0:tn], -TWO_PI, ANG[ai][:, 0:tn], ALU.mult, ALU.add, [("KI1", bi), ("ANG", ai)], [("SIN", bi)])
                _act(P, COS[bi][:, 0:tn], SIN[bi][:, 0:tn], AF.Abs, [("SIN", bi)], [("COS", bi)])
                _act(P, COS[bi][:, 0:tn], COS[bi][:, 0:tn], AF.Sin, [("COS", bi)], [("COS", bi)], scale=-1.0, bias=PI / 2)
                _act(P, SIN[bi][:, 0:tn], SIN[bi][:, 0:tn], AF.Sin, [("SIN", bi)], [("SIN", bi)])
                if tb == 3:
                    _cp(P, "pool", CLt[:, g, 0:1], COS[bi][:, 511:512], [("COS", bi)], ["CLt"])
                    _cp(P, "pool", SLt[:, g, 0:1], SIN[bi][:, 511:512], [("SIN", bi)], ["SLt"])
                if tb == 4:
                    _cp(P, "pool", CLt[:, g, 1:5], COS[bi][:, 3:128:32], [("COS", bi)], ["CLt"])
                    _cp(P, "pool", SLt[:, g, 1:5], SIN[bi][:, 3:128:32], [("SIN", bi)], ["SLt"])

            emit_bu(0)
            emit_ang(0)
            emit_ang(1)
            emit_tables(0)
            for i in range(len(steps)):
                g, tb, t8, q4, g2, t0, tn, rows = st_info(i)
                bi = i % 2
                vi = i % 3
                ba, bb = (2 * i) % 3, (2 * i + 1) % 3
                urows = uT[rows, t8, t0:t0 + tn]
                if i + 2 < len(steps):
                    emit_ang(i + 2)
                _tt(P, "dve", T1[bi][:, 0:tn], pbb[ba][:, 0:tn], COS[bi][:, 0:tn], ALU.mult, [("pbb", ba), ("COS", bi)], [("T1", bi)])
                _tt(P, "dve", T2[bi][:, 0:tn], pbb[bb][:, 0:tn], SIN[bi][:, 0:tn], ALU.mult, [("pbb", bb), ("SIN", bi)], [("T2", bi)])
                if i + 1 < len(steps):
                    emit_bu(i + 1)
                _tt(P, "pool", Wt[bi][:, 0:tn], T1[bi][:, 0:tn], T2[bi][:, 0:tn], ALU.add, [("T1", bi), ("T2", bi)], [("W", bi)])
                if tb == 4:
                    w4 = Wt[bi][:, 0:128].rearrange("p (a b) -> p a b", b=32)
                    _tt(P, "pool", w4[:, :, 0], w4[:, :, 0], RS0[:, g, :], ALU.add, [("W", bi), "RS0"], [("W", bi)])
                if i + 1 < len(steps):
                    emit_tables(i + 1)
                vb = Vv[vi]
                if tb < 4:
                    init = 0.0 if tb == 0 else carry["v"]
                    rd = [("W", bi), "rr"] + ([("V", (vi - 1) % 3)] if tb > 0 else [])
                    P.op("dve", lambda e, vb=vb, bi=bi, init=init, g=g: e.tensor_tensor_scan(
                        vb[:, 0:512], rr[:, g:g + 1].to_broadcast([128, 512]), Wt[bi][:, 0:512], init, ALU.mult, ALU.add), rd, [("V", vi)])
                    carry["v"] = vb[:, 511:512]
                    if tb == 3:
                        _cp(P, "pool", VL[:, g, 0:1], vb[:, 511:512], [("V", vi)], ["VL"])
                else:
                    for b in range(4):
                        P.op("dve", lambda e, vb=vb, bi=bi, b=b, g=g: e.tensor_tensor_scan(
                            vb[:, 32 * b:32 * b + 32], rr[:, g:g + 1].to_broadcast([128, 32]), Wt[bi][:, 32 * b:32 * b + 32], 0.0, ALU.mult, ALU.add),
                            [("W", bi), "rr"], [("V", vi)])
                    _cp(P, "pool", VL[:, g, 1:5], vb[:, 3:128:32], [("V", vi)], ["VL"])
                _tt(P, "pool", CV[bi][:, 0:tn], vb[:, 0:tn], COS[bi][:, 0:tn], ALU.mult, [("V", vi), ("COS", bi)], [("CV", bi)])
                _tt(P, "pool", SV[bi][:, 0:tn], vb[:, 0:tn], SIN[bi][:, 0:tn], ALU.mult, [("V", vi), ("SIN", bi)], [("SV", bi)])
                yo = py[tb][rows, 0:tn]
                first = (g2 == 0)
                P.op("pe", lambda e, yo=yo, g=g, bi=bi, tn=tn, first=first, q4=q4: e.matmul(
                    yo, CL1[:, g, :], CV[bi][:, 0:tn], start=first, stop=False, tile_position=(0, 32 * q4), skip_group_check=True),
                    ["CL1", ("CV", bi)], [("py", tb)], inc=False)
                last = (g2 == 1)
                P.op("pe", lambda e, yo=yo, g=g, bi=bi, tn=tn, q4=q4: e.matmul(
                    yo, CL2[:, g, :], SV[bi][:, 0:tn], start=False, stop=False, tile_position=(0, 32 * q4), skip_group_check=True),
                    ["CL2", ("SV", bi)], [("py", tb)], inc=(not last))
                if last:
                    P.op("pe", lambda e, yo=yo, t8=t8, rows=rows, urows=urows, q4=q4: e.matmul(
                        yo, DG[rows, t8, :], urows, start=False, stop=True, tile_position=(32 * q4, 32 * q4), skip_group_check=True),
                        ["DG", ("uT", t8, tb)], [("py", tb)], inc=True)
                if g % 8 == 7 and tb == 4:
                    for tb2 in range(5):
                        t0b, tnb = TBS[tb2]
                        bj = tb2 % 2
                        _cp(P, "act", BU[bj][:, 0:tnb], py[tb2][:, 0:tnb], [("py", tb2)], [("BU", bj)])
                        _act(P, BP[bj][:, 0:tnb], py[tb2][:, 0:tnb], AF.Square, [("py", tb2)], [("BP", bj)])
                        _ts(P, "dve", BP[bj][:, 0:tnb], BP[bj][:, 0:tnb], 0.044715, 1.0, ALU.mult, ALU.add, [("BP", bj)], [("BP", bj)])
                        _tt(P, "dve", BP[bj][:, 0:tnb], BP[bj][:, 0:tnb], BU[bj][:, 0:tnb], ALU.mult, [("BP", bj), ("BU", bj)], [("BP", bj)])
                        _act(P, BP[bj][:, 0:tnb], BP[bj][:, 0:tnb], AF.Tanh, [("BP", bj)], [("BP", bj)], scale=math.sqrt(2.0 / math.pi))
                        _stt(P, "dve", BP[bj][:, 0:tnb], BP[bj][:, 0:tnb], 1.0, BU[bj][:, 0:tnb], ALU.add, ALU.mult, [("BP", bj), ("BU", bj)], [("BP", bj)])
                        P.op("act", lambda e, t8=t8, t0b=t0b, tnb=tnb, bj=bj: e.mul(YG[:, t8, t0b:t0b + tnb], BP[bj][:, 0:tnb], 0.5), [("BP", bj)], [("YG", t8, tb2)])
            k_ps2 = sb2("sPS2", [128, 128], F32)
            k_idf = sb2("sidf2", [128, 128], F32)
            XL = T2[0][:, 0:320].rearrange("p (a b) -> p a b", b=5)
            XT = T1[0][:, 0:320].rearrange("p (a b) -> p a b", b=5)
            P.dma("sp", k_ps2[:], k.c_ps2, writes=["ps2"])
            P.dma("sp", k_idf[:], k.c_ident, writes=["idf2"])
            _mm(P, pbu[:, 0:320], k_ps2[:], VL[:].rearrange("p a b -> p (a b)"), True, True, ["ps2", "VL"], [("pbb", 0)], True)
            _tt(P, "dve", T1[0][:, 0:320], pbu[:, 0:320], SLt[:].rearrange("p a b -> p (a b)"), ALU.mult, [("pbb", 0), "SLt"], [("T1", 0)])
            _tt(P, "dve", XL, VL[:], CLt[:], ALU.mult, ["VL", "CLt"], [("T2", 0)])
            _tt(P, "dve", XL, XL, XT, ALU.add, [("T2", 0), ("T1", 0)], [("T2", 0)])
            for c in range(5):
                P.op("pe", lambda e, c=c: e.transpose(pbp[0:64, c * 128:(c + 1) * 128] if c < 4 else pbu[0:64, 384:512], XL[:, :, c], k_idf[:]),
                     [("T2", 0), "idf2"], [("pbb", 1) if c < 4 else ("pbb", 0)])
                src = pbp[0:64, c * 128:(c + 1) * 128] if c < 4 else pbu[0:64, 384:512]
                _cp(P, "dve", XLo[:, c, :, :], src.rearrange("g (h p) -> g p h", h=2), [("pbb", 1) if c < 4 else ("pbb", 0)], [("XLo", c)])
            P.dma("sp", k.s5p, XLo[:, 0, :, :], reads=[("XLo", 0)])
            for b in range(4):
                P.dma("sp", k.s5s[b], XLo[:, 1 + b, :, :], reads=[("XLo", 1 + b)])
            P.barrier()
            P.emit()
    k.OT = lsb("OT", [128, 8, T], BF16)
    with contextlib.ExitStack() as st:
        sb, ps = f1(st), f2(st)
        Wgl = sb("sWgl", [128, 8, D], BF16)
        Wga = sb("sWga", [128, 8, D], BF16)
        bgl = sb("sbgl", [128, 8], F32)
        SG = [sb(f"sSG{i}", [128, 512], F32) for i in range(2)]
        SGT = [sb(f"sSGT{i}", [128, 512], F32) for i in range(2)]
        GS = [sb(f"sGS{i}", [128, 512], F32) for i in range(2)]
        Y3 = [sb(f"sY3{i}", [128, 512], F32) for i in range(2)]
        pz = [ps(f"spz{i}", [128, 512], F32) for i in range(2)]
        pg = [ps(f"spg{i}", [128, 512], F32) for i in range(2)]
        for kc in range(8):
            P.dma("pool", Wgl[:, kc, :], k.c_w_glu[kc * 128:(kc + 1) * 128, :], writes=[("Wgl", kc)])
            P.dma("pool", Wga[:, kc, :], k.c_w_in[kc * 128:(kc + 1) * 128, D:2 * D], writes=[("Wga", kc)])
        P.dma("sp", bgl[:], k.s_bglu, writes=["bgl"])
        cnt = 0
        for m in range(8):
            for tb in range(5):
                t0, tn = TBS[tb]
                bi = cnt % 2
                cnt += 1
                for kc in range(8):
                    _mm(P, pz[bi][:, 0:tn], Wgl[:, kc, m * 128:(m + 1) * 128], YG[:, kc, t0:t0 + tn], kc == 0, kc == 7,
                        [("Wgl", kc), ("YG", kc, tb)], [("pz", bi)], inc=(kc == 7))
                _act(P, SG[bi][:, 0:tn], pz[bi][:, 0:tn], AF.Sigmoid, [("pz", bi), "bgl"], [("SG", bi)], bias=bgl[:, m:m + 1])
                for kc in range(8):
                    _mm(P, pg[bi][:, 0:tn], Wga[:, kc, m * 128:(m + 1) * 128], k.hT[:, kc, t0:t0 + tn], kc == 0, kc == 7,
                        [("Wga", kc)] + hTk(t0, tn), [("pg", bi)], inc=(kc == 7))
                _act(P, SGT[bi][:, 0:tn], pg[bi][:, 0:tn], AF.Sigmoid, [("pg", bi)], [("SGT", bi)])
                _tt(P, "dve", GS[bi][:, 0:tn], pg[bi][:, 0:tn], SGT[bi][:, 0:tn], ALU.mult, [("pg", bi), ("SGT", bi)], [("GS", bi)])
                _tt(P, "pool", Y3[bi][:, 0:tn], YG[:, m, t0:t0 + tn], SG[bi][:, 0:tn], ALU.mult, [("YG", m, tb), ("SG", bi)], [("Y3", bi)])
                _tt(P, "pool", k.OT[:, m, t0:t0 + tn], Y3[bi][:, 0:tn], GS[bi][:, 0:tn], ALU.mult, [("Y3", bi), ("GS", bi)],
                    [("OT", n) for n in range(t0 // 128, (t0 + tn) // 128)])
        P.barrier()
        P.emit()

def build_nc(nlayers=4, debug=False):
    nc_real = bass.Bass("TRN2", target_bir_lowering=False)

    class _NC:
        def __init__(self, real):
            self._real = real
            self._uid = 0

        def __getattr__(self, name):
            return getattr(self._real, name)

        def sbuf_tensor(self, name, shape, dt):
            self._uid += 1
            return self._real.sbuf_tensor(f"{name}_u{self._uid}", shape, dt)

        def psum_tensor(self, name, shape, dt):
            self._uid += 1
            return self._real.psum_tensor(f"{name}_u{self._uid}", shape, dt)

    nc = _NC(nc_real)
    k = K()
    k.nc = nc
    di = lambda name, shape: nc.dram_tensor(name, list(shape), F32, kind="ExternalInput").ap()
    do = lambda name, shape: nc.dram_tensor(name, list(shape), F32, kind="ExternalOutput").ap()
    k.xp = di("xp", [SEQ, D])
    k.xs = di("xs", [16, D])
    k.sh = di("sh", [2, 4, 8, 128, 128])
    k.normw_d = di("normw_d", [128, 4, 8])
    k.fnw_d = di("fnw_d", [128, D])
    k.a_w_in = di("a_w_in", [2, D, 4096])
    k.a_w_out = di("a_w_out", [2, D, D])
    k.lb_d = di("lb_d", [128, 2, 2, 8])
    k.onw_d = di("onw_d", [128, 2, 8])
    k.c_ident = di("c_ident", [128, 128])
    k.c_m01 = di("c_m01", [128, 128])
    k.c_cmask = di("c_cmask", [128, T])
    k.c_padmask = di("c_padmask", [128, 128])
    k.c_rmask = di("c_rmask", [128, 4])
    k.b_w_in = di("b_w_in", [D, 10240])
    k.b_w_out = di("b_w_out", [D, D])
    k.c128 = di("c128", [4, 128, 2, 8, 128])
    k.c512 = di("c512", [4, 512, 2, 8, 128])
    k.c2048 = di("c2048", [4, 2048, 2, 8, 128])
    k.c_cosT = di("c_cosT", [128, T])
    k.c_sinT = di("c_sinT", [128, T])
    k.c_permS = di("c_permS", [128, 128])
    k.c_mpo = di("c_mpo", [128, 2, 128])
    k.c_mnew = di("c_mnew", [128, 2, 128])
    k.c_mc9 = di("c_mc9", [128, 16])
    k.k128p = do("k128p", [128, 2, 8, 128])
    k.k128s = do("k128s", [4, 128, 2, 8, 128])
    k.k512p = do("k512p", [512, 2, 8, 128])
    k.k512s = do("k512s", [4, 512, 2, 8, 128])
    k.k2048p = do("k2048p", [2048, 2, 8, 128])
    k.k2048s = do("k2048s", [4, 2048, 2, 8, 128])
    k.c_w_in = di("c_w_in", [D, 2 * D])
    k.c_w_glu = di("c_w_glu", [D, D])
    k.c_w_out = di("c_w_out", [D, D])
    k.s_are = di("s_are", [128, 64]); k.s_aim = di("s_aim", [128, 64]); k.s_ldt = di("s_ldt", [128, 64])
    k.s_bT1 = di("s_bT1", [128, 64, 16]); k.s_bT2 = di("s_bT2", [128, 64, 16])
    k.s_cTr = di("s_cTr", [128, 64, 16]); k.s_cTi = di("s_cTi", [128, 64, 16])
    k.s_s0 = di("s_s0", [128, 64, 4])
    k.s_dd = di("s_dd", [128, 8]); k.s_bglu = di("s_bglu", [128, 8])
    k.c_mul = di("c_mul", [128, 96]); k.c_evo = di("c_evo", [128, 4]); k.c_ps2 = di("c_ps2", [128, 128])
    k.s5p = do("s5p", [64, 64, 2])
    k.s5s = do("s5s", [4, 64, 64, 2])
    k.yp = do("yp", [SEQ, D])
    k.ys = do("ys", [16, D])
    k.hp = do("hp", [2, 8, 128, 128])
    k.hs = do("hs", [2, 4, 8, 128, 128])
    k.xres = nc.dram_tensor("xres", [T, D], F32, kind="Internal").ap()
    if debug:
        k.dbg = do("dbg", [T, D])

    with contextlib.ExitStack() as gst:
        P = Prog(nc, gst)
        gsb = lambda name, shape, dt: gst.enter_context(nc.sbuf_tensor(name, shape, dt))
        k.hT = gsb("hT", [128, 8, T], BF16)
        k.ident = gsb("ident", [128, 128], BF16)
        k.onesf = gsb("onesf", [128, 128], F32)
        k.normw = gsb("normw", [128, 4, 8], F32)
        k.lbraw = gsb("lbraw", [128, 2, 8], F32)
        k.lbv = gsb("lbv", [128, 2, 8], F32)
        k.omlv = gsb("omlv", [128, 2, 8], F32)
        k.onwv = gsb("onwv", [128, 2, 8], F32)
        k.clbv = gsb("clbv", [128, 2, 8], F32)
        P.dma("pool", k.ident[:], k.c_ident, writes=["ident"])
        P.dma("sp", k.normw[:], k.normw_d, writes=["normw"])
        P.dma("sp", k.lbraw[:], k.lb_d[:, :, 0, :], writes=["lbraw"])
        P.dma("sp", k.onwv[:], k.onw_d, writes=["onwv"])
        P.op("pool", lambda e: e.memset(k.onesf[:], 1.0), [], ["onesf"])
        P.op("pool", lambda e: e.memset(k.lbv[:, 0, :], 0.0), [], [("lbv", 0)])
        _tt(P, "dve", k.lbv[:, 1, :], k.lbraw[:, 0, :], k.lbraw[:, 1, :], ALU.subtract, ["lbraw"], [("lbv", 1)])
        _act(P, k.lbv[:, 1, :], k.lbv[:, 1, :], AF.Exp, [("lbv", 1)], [("lbv", 1)])
        _ts(P, "dve", k.lbv[:, 1, :], k.lbv[:, 1, :], 1.0, None, ALU.add, None, [("lbv", 1)], [("lbv", 1)])
        P.op("dve", lambda e: e.reciprocal(k.lbv[:, 1, :], k.lbv[:, 1, :]), [("lbv", 1)], [("lbv", 1)])
        _ts(P, "dve", k.omlv[:], k.lbv[:], -1.0, 1.0, ALU.mult, ALU.add, [("lbv", 0), ("lbv", 1)], ["omlv"])
        _ts(P, "dve", k.clbv[:], k.lbv[:], float(np.exp(np.float32(60.0))), None, ALU.mult, None, [("lbv", 0), ("lbv", 1)], ["clbv"])
        P.barrier()
        P.reg = {}
        P.emit()

        kinds = [0, 1, 2, 0]
        for layer in range(nlayers if STOP_AFTER != "setup" else 0):
            src = "in" if layer == 0 else "xres"
            last = (layer == nlayers - 1)
            with contextlib.ExitStack() as st:
                phase_norm(k, P, st, layer, src)
                P.barrier()
                P.emit()
            if STOP_AFTER == "norm":
                break
            with contextlib.ExitStack() as lst:
                if kinds[layer] == 2:
                    phase_s5(k, P, lst)
                    wout = k.c_w_out
                    if STOP_AFTER == "mixer":
                        break
                else:
                    k.OT = lst.enter_context(nc.sbuf_tensor("OT", [128, 8, T], BF16))
                    with contextlib.ExitStack() as st:
                        if kinds[layer] == 0:
                            phase_hgrn(k, P, st, layer // 3)
                            wout = k.a_w_out[layer // 3]
                        elif kinds[layer] == 1:
                            phase_attn(k, P, st)
                            wout = k.b_w_out
                        P.barrier()
                        P.emit()
                    if STOP_AFTER == "mixer":
                        break
                with contextlib.ExitStack() as st:
                    phase_outproj(k, P, st, wout, src, "xres", final=(layer == 3))
                    P.barrier(final=(layer == nlayers - 1))
                    P.emit()
        if debug:
            with contextlib.ExitStack() as st:
                P.dma("sp", k.dbg, k.xres)
                P.barrier(final=True)
                P.emit()
    return nc_real


def host_consts():
    c = {}
    c["c_ident"] = np.eye(128, dtype=np.float32)
    s = np.arange(128)
    c["c_m01"] = ((s[:, None] // 32 == s[None, :] // 32) & (s[:, None] <= s[None, :])).astype(np.float32)
    t = np.arange(T)
    c["c_cmask"] = np.broadcast_to((t % 32 != 0).astype(np.float32), (128, T)).copy()
    c["c_padmask"] = np.broadcast_to(((s % 32) < 4).astype(np.float32), (128, 128)).copy()
    c["c_rmask"] = (s[:, None] // 32 == np.arange(4)[None, :]).astype(np.float32)
    pos = np.concatenate([np.arange(SEQ), PAST + (np.arange(128) % 32)]).astype(np.float32)
    half = 64
    inv_freq = (np.float32(10000.0) ** (-np.arange(half, dtype=np.float32) / np.float32(half))).astype(np.float32)
    ang = (pos[None, :] * inv_freq[:, None]).astype(np.float32).astype(np.float64)
    cos = np.cos(ang).astype(np.float32)
    sin = np.sin(ang).astype(np.float32)
    c["c_cosT"] = np.ascontiguousarray(np.concatenate([cos, cos], axis=0))
    c["c_sinT"] = np.ascontiguousarray(np.concatenate([-sin, sin], axis=0))
    pm = np.zeros((128, 128), np.float32)
    m = np.arange(128)
    pm[(m + 64) % 128, m] = 1.0
    c["c_permS"] = pm
    cq = s[:, None]
    a = s[None, :]
    c["c_mpo"] = np.ascontiguousarray(np.stack([(a <= cq), (a >= cq)], axis=1).astype(np.float32))
    bq, tq = s // 32, s % 32
    same = bq[:, None] == bq[None, :]
    real = (tq[:, None] < 4) & (tq[None, :] < 4)
    diag = (s[:, None] == s[None, :])
    m0 = (same & real & (tq[:, None] <= tq[None, :])) | (diag & (tq[:, None] >= 4))
    c["c_mnew"] = np.ascontiguousarray(np.stack([m0, diag], axis=1).astype(np.float32))
    mc9 = np.ones((128, 16), np.float32)
    for t in range(4):
        mc9[:, t] = (s >= t)
    c["c_mc9"] = mc9
    mul = np.concatenate([64.0 * np.arange(32), 1.0 + np.arange(64)]).astype(np.float32)
    c["c_mul"] = np.ascontiguousarray(np.broadcast_to(mul[None, :], (128, 96)))
    ev = ((s // 16) % 2 == 0).astype(np.float32)
    c["c_evo"] = np.ascontiguousarray(np.stack([ev, 1 - ev, -ev, -(1 - ev)], axis=1))
    ps2 = np.zeros((128, 128), np.float32)
    pp_ = np.arange(64)
    ps2[64 + pp_, pp_] = -1.0
    ps2[pp_, 64 + pp_] = 1.0
    c["c_ps2"] = ps2
    return c


def make_in_maps(inp):
    consts = host_consts()
    normw = np.ascontiguousarray(inp["norm_w"].reshape(4, 8, 128).transpose(2, 0, 1))
    fnw = np.ascontiguousarray(np.broadcast_to(inp["final_norm_w"][None, :], (128, D)))
    lbl = inp["a_lb_logits"].reshape(2, 8, 128).transpose(2, 0, 1)
    lb_d = np.ascontiguousarray(np.stack([lbl, lbl], axis=2))
    onw = np.ascontiguousarray(inp["a_onorm_w"].reshape(2, 8, 128).transpose(2, 0, 1))
    dup = lambda a: np.ascontiguousarray(np.concatenate([a, a], axis=0))
    a_re = inp["c_a_re"][0].T; a_im = inp["c_a_im"][0].T
    s_are, s_aim = dup(a_re), dup(a_im)
    s_ldt = np.ascontiguousarray(np.broadcast_to(inp["c_log_dt"][0][None, :], (128, 64)))
    b_re = inp["c_b_re"][0].transpose(1, 0, 2); b_im = inp["c_b_im"][0].transpose(1, 0, 2)
    s_bT1 = np.ascontiguousarray(np.concatenate([b_re, b_im], axis=0))
    s_bT2 = np.ascontiguousarray(np.concatenate([b_im, b_re], axis=0))
    c_re = inp["c_c_re"][0].transpose(2, 0, 1); c_im = inp["c_c_im"][0].transpose(2, 0, 1)
    s_cTr, s_cTi = dup(c_re), dup(c_im)
    s_dd = np.ascontiguousarray(inp["c_d"][0].reshape(8, 128).T)
    s_bglu = np.ascontiguousarray(inp["c_b_glu"][0].reshape(8, 128).T)
    maps = []
    for c in range(NCORES):
        m = dict(consts)
        m["xp"] = np.ascontiguousarray(inp["x_prompt"][c])
        m["xs"] = np.ascontiguousarray(inp["x_sample"][4 * c:4 * c + 4].reshape(16, D))
        m["sh"] = np.ascontiguousarray(inp["state_hgrn"][:, 4 * c:4 * c + 4])
        m["normw_d"] = normw
        m["fnw_d"] = fnw
        m["a_w_in"] = inp["a_w_in"]
        m["a_w_out"] = inp["a_w_out"]
        m["b_w_in"] = inp["b_w_in"][0]
        m["b_w_out"] = inp["b_w_out"][0]
        m["c_w_in"] = inp["c_w_in"][0]
        m["c_w_glu"] = inp["c_w_glu"][0]
        m["c_w_out"] = inp["c_w_out"][0]
        m["s_are"], m["s_aim"], m["s_ldt"] = s_are, s_aim, s_ldt
        m["s_bT1"], m["s_bT2"], m["s_cTr"], m["s_cTi"] = s_bT1, s_bT2, s_cTr, s_cTi
        st5 = inp["state_s5"][0, 4 * c:4 * c + 4]
        m["s_s0"] = np.ascontiguousarray(np.concatenate([st5[..., 0].transpose(2, 1, 0), st5[..., 1].transpose(2, 1, 0)], axis=0))
        m["s_dd"], m["s_bglu"] = s_dd, s_bglu
        m["c128"] = np.ascontiguousarray(inp["cache_kv_w128"][0, 4 * c:4 * c + 4])
        m["c512"] = np.ascontiguousarray(inp["cache_kv_w512"][0, 4 * c:4 * c + 4])
        m["c2048"] = np.ascontiguousarray(inp["cache_kv_w2048"][0, 4 * c:4 * c + 4])
        m["lb_d"] = lb_d
        m["onw_d"] = onw
        maps.append(m)
    return maps


def kernel(**inputs):
    inp = {k_: np.asarray(v) for k_, v in inputs.items()}
    nc = build_nc()
    maps = make_in_maps(inp)
    res = run_bass_kernel_spmd(nc, maps, core_ids=list(range(NCORES)))
    r = res.results
    cat = lambda name: np.concatenate([r[c][name] for c in range(NCORES)], axis=0)
    stk = lambda name: np.stack([r[c][name] for c in range(NCORES)], axis=0)
    y_prompt = stk("yp")
    y_sample = np.concatenate([r[c]["ys"].reshape(4, 4, D) for c in range(NCORES)], axis=0)
    hgrn_p = np.stack([r[c]["hp"] for c in range(NCORES)], axis=1)
    hgrn_s = np.concatenate([r[c]["hs"] for c in range(NCORES)], axis=1)
    outs = [y_prompt, y_sample, hgrn_p, hgrn_s]
    for w in (128, 512, 2048):
        outs.append(stk(f"k{w}p")[None])
        outs.append(cat(f"k{w}s")[None])
    outs.append(stk("s5p")[None])
    outs.append(cat("s5s")[None])
    return tuple(np.ascontiguousarray(o, dtype=np.float32) for o in outs)
```

```python
import contextlib
import math
import numpy as np
import concourse.bass as bass
import concourse.mybir as mybir
from concourse.bass_utils import run_bass_kernel_spmd

F32 = mybir.dt.float32
BF16 = mybir.dt.bfloat16
AF = mybir.ActivationFunctionType
ALU = mybir.AluOpType
AX = mybir.AxisListType

NCORES = 8
D = 1024
SEQ = 2048
NT = 17
T = NT * 128
TBS = [(0, 512), (512, 512), (1024, 512), (1536, 512), (2048, 128)]
EPS = 1e-6
PAST = 8192
STOP_AFTER = None
HG_LEVEL = 99
HG_SUB = 99
HG_HEADS = 8
HG_TBS = 5


class Prog:
    ENGS = ("pe", "act", "dve", "pool", "sp")
    NDS = {"sp": 40, "pool": 16, "act": 32}

    def __init__(self, nc, st):
        self.nc = nc
        self.sems = {e: st.enter_context(nc.semaphore("s_" + e)) for e in self.ENGS}
        self.dsems = {q: [st.enter_context(nc.semaphore(f"d_{q}{i}")) for i in range(n)] for q, n in self.NDS.items()}
        self.cnt = {e: 0 for e in self.ENGS}
        self.ndma = {q: 0 for q in self.NDS}
        self.q = {e: [] for e in self.ENGS}
        self.pend = {e: ([], []) for e in self.ENGS}
        self.reg = {}
        self.waited = {e: {} for e in self.ENGS}
        self.eng = {"pe": nc.tensor, "act": nc.scalar, "dve": nc.vector, "pool": nc.gpsimd, "sp": nc.sync}

    def _deps(self, eng, reads, writes, deps):
        ds = set(d for d in deps if d is not None)
        for k in reads:
            r = self.reg.get(k)
            if r and r[0] is not None:
                ds.add(r[0])
        for k in writes:
            r = self.reg.get(k)
            if r:
                if r[0] is not None:
                    ds.add(r[0])
                ds.update(r[1])
        if eng == "pe":
            ds = set(d for d in ds if d[0] != "pe")
        return ds

    def _register(self, tok, reads, writes):
        for k in reads:
            r = self.reg.setdefault(k, [None, []])
            r[1].append(tok)
        for k in writes:
            self.reg[k] = [tok, []]

    def op(self, eng, fn, reads=(), writes=(), inc=True, deps=()):
        ds = self._deps(eng, reads, writes, deps)
        if inc:
            self.cnt[eng] += 1
            tok = (eng, self.cnt[eng])
            pr, pw = self.pend[eng]
            self._register(tok, list(reads) + pr, list(writes) + pw)
            self.pend[eng] = ([], [])
        else:
            tok = None
            self.pend[eng][0].extend(reads)
            self.pend[eng][1].extend(writes)
        self.q[eng].append(("op", fn, ds, inc))
        return tok

    def dma(self, queue, out, in_, reads=(), writes=(), deps=()):
        ds = self._deps(queue, reads, writes, deps)
        i = self.ndma[queue]
        self.ndma[queue] += 1
        tok = ("dma", queue, i)
        self._register(tok, reads, writes)
        self.q[queue].append(("dma", (out, in_, i), ds, True))
        return tok

    def barrier(self, final=False):
        toks = [(e, self.cnt[e]) for e in self.ENGS if self.cnt[e] > 0]
        for qn, n in self.ndma.items():
            nd = self.NDS[qn]
            for i in range(max(0, n - nd), n):
                toks.append(("dma", qn, i))
        for e in self.ENGS:
            self.op(e, lambda en: en.nop(), deps=toks)
        self.reg = {}

    def emit(self):
        nc = self.nc
        with nc.Block() as block:
            def run(ename):
                def body(e):
                    waited = self.waited[ename]

                    def wait(sem, key, val):
                        if waited.get(key, 0) >= val:
                            return
                        e.wait_ge(sem, val)
                        waited[key] = val

                    for kind, payload, deps, inc in self.q[ename]:
                        for d in sorted(deps, key=str):
                            if d[0] == "dma":
                                _, qn, i = d
                                nd = self.NDS[qn]
                                wait(self.dsems[qn][i % nd], (qn, i % nd), 16 * (i // nd + 1))
                            else:
                                wait(self.sems[d[0]], d[0], d[1])
                        if kind == "op":
                            ins = payload(e)
                            if inc:
                                ins.then_inc(self.sems[ename], 1)
                        else:
                            out, in_, i = payload
                            nd = self.NDS[ename]
                            if i >= nd:
                                wait(self.dsems[ename][i % nd], (ename, i % nd), 16 * (i // nd))
                            e.dma_start(out=out, in_=in_).then_inc(self.dsems[ename][i % nd], 16)
                    self.q[ename] = []
                return body

            block.tensor(run("pe"))
            block.scalar(run("act"))
            block.vector(run("dve"))
            block.gpsimd(run("pool"))
            block.sync(run("sp"))


class K:
    pass


def _mm(P, out, lhsT, rhs, start, stop, reads, writes, inc, tp=None):
    if tp is None:
        return P.op("pe", lambda e: e.matmul(out, lhsT, rhs, start=start, stop=stop, skip_group_check=True), reads, writes, inc=inc)
    return P.op("pe", lambda e: e.matmul(out, lhsT, rhs, start=start, stop=stop, tile_position=tp, skip_group_check=True), reads, writes, inc=inc)


def _act(P, out, in_, func, reads, writes, scale=1.0, bias=0.0, accum=None):
    if accum is None:
        return P.op("act", lambda e: e.activation(out, in_, func, bias=bias, scale=scale), reads, writes)
    return P.op("act", lambda e: e.activation(out, in_, func, bias=bias, scale=scale, accum_out=accum), reads, writes)


def _tt(P, eng, out, a, b, op, reads, writes):
    return P.op(eng, lambda e: e.tensor_tensor(out, a, b, op), reads, writes)


def _ts(P, eng, out, a, s1, s2, op0, op1, reads, writes):
    if op1 is None:
        return P.op(eng, lambda e: e.tensor_scalar(out, a, s1, None, op0), reads, writes)
    return P.op(eng, lambda e: e.tensor_scalar(out, a, s1, s2, op0, op1), reads, writes)


def _stt(P, eng, out, a, sc, b, op0, op1, reads, writes):
    return P.op(eng, lambda e: e.scalar_tensor_tensor(out, a, sc, b, op0, op1), reads, writes)


def _cp(P, eng, out, in_, reads, writes):
    if eng == "act":
        return P.op("act", lambda e: e.copy(out, in_), reads, writes)
    return P.op(eng, lambda e: e.tensor_copy(out, in_), reads, writes)


def x_tile_src(k, src, n):
    if src == "in":
        if n < 16:
            return [((0, 128), k.xp[n * 128:(n + 1) * 128, :])]
        return [((32 * b, 32 * b + 4), k.xs[4 * b:4 * b + 4, :]) for b in range(4)]
    return [((0, 128), k.xres[n * 128:(n + 1) * 128, :])]


def load_x_tile(k, P, src, n, buf, key):
    toks = []
    if src == "in" and n == 16:
        P.op("pool", lambda e: e.memset(buf[:], 0.0), [], [key])
    for (p0, p1), ap in x_tile_src(k, src, n):
        toks.append(P.dma("sp", buf[p0:p1, :], ap, reads=[("xres", n)] if src != "in" else [], writes=[key]))
    return toks


def phase_norm(k, P, st, layer, src):
    nc = k.nc
    xin = [st.enter_context(nc.sbuf_tensor(f"xin{i}", [128, D], F32)) for i in range(3)]
    junk = st.enter_context(nc.sbuf_tensor("njunk", [128, D], BF16))
    xn = [st.enter_context(nc.sbuf_tensor(f"xn{i}", [128, D], BF16)) for i in range(2)]
    ss = st.enter_context(nc.sbuf_tensor("nss", [128, NT], F32))
    rs = st.enter_context(nc.sbuf_tensor("nrs", [128, NT], F32))
    tp = [st.enter_context(nc.psum_tensor(f"ntp{i}", [128, 8, 128], BF16)) for i in range(2)]
    P.op("dve", lambda e: e.memset(ss[:], 0.0), [], [("nss", n) for n in range(NT)])
    for n in range(NT):
        xb = xin[n % 3]
        kx = ("xin", n % 3)
        load_x_tile(k, P, src, n, xb, kx)
        _act(P, junk[:], xb[:], AF.Square, [kx], ["njunk", ("nss", n)], accum=ss[:, n:n + 1])
        _act(P, rs[:, n:n + 1], ss[:, n:n + 1], AF.Ln, [("nss", n)], [("nrs", n)], scale=1.0 / D, bias=EPS)
        _act(P, rs[:, n:n + 1], rs[:, n:n + 1], AF.Exp, [("nrs", n)], [("nrs", n)], scale=-0.5)
        xnb = xn[n % 2]
        kn = ("xn", n % 2)
        P.op("act", lambda e, xnb=xnb, xb=xb, n=n: e.mul(xnb[:], xb[:], rs[:, n:n + 1]), [kx, ("nrs", n)], [kn])
        tpb = tp[n % 2]
        kt = ("ntp", n % 2)
        for c in range(8):
            P.op("pe", lambda e, c=c, tpb=tpb, xnb=xnb: e.transpose(tpb[:, c, :], xnb[:, c * 128:(c + 1) * 128], k.ident[:]),
                 [kn, "ident"], [kt], inc=(c == 7))
        nw = k.normw[:, layer, :]
        P.op("dve", lambda e, tpb=tpb, n=n, nw=nw: e.tensor_tensor(
            k.hT[:, :, n * 128:(n + 1) * 128], tpb[:], nw.unsqueeze(2).to_broadcast([128, 8, 128]), ALU.mult),
            [kt, "normw"], [("hT", n)])


def phase_outproj(k, P, st, wout_ap, src, dst, final):
    nc = k.nc
    wo = st.enter_context(nc.sbuf_tensor("wo", [128, 8, D], BF16))
    xin = [st.enter_context(nc.sbuf_tensor(f"oxin{i}", [128, D], F32)) for i in range(3)]
    xo = [st.enter_context(nc.sbuf_tensor(f"oxo{i}", [128, D], F32)) for i in range(3)]
    po = [st.enter_context(nc.psum_tensor(f"opo{i}", [128, 2, 512], F32)) for i in range(2)]
    for h in range(8):
        P.dma("pool", wo[:, h, :], wout_ap[h * 128:(h + 1) * 128, :], writes=[("wo", h)])
    if final:
        k.fnw = st.enter_context(nc.sbuf_tensor("fnw", [128, D], F32))
        P.dma("sp", k.fnw[:], k.fnw_d, writes=["fnw"])
        junk = st.enter_context(nc.sbuf_tensor("ojunk", [128, D], BF16))
        ss = st.enter_context(nc.sbuf_tensor("oss", [128, NT], F32))
        rs = st.enter_context(nc.sbuf_tensor("ors", [128, NT], F32))
        yo = [st.enter_context(nc.sbuf_tensor(f"oyo{i}", [128, D], F32)) for i in range(3)]
        P.op("dve", lambda e: e.memset(ss[:], 0.0), [], [("oss", n) for n in range(NT)])
    for n in range(NT):
        xb = xin[n % 3]
        kx = ("oxin", n % 3)
        load_x_tile(k, P, src, n, xb, kx)
        pb = po[n % 2]
        kp = ("opo", n % 2)
        for half in range(2):
            for h in range(8):
                _mm(P, pb[:, half, :], k.OT[:, h, n * 128:(n + 1) * 128], wo[:, h, half * 512:(half + 1) * 512],
                    h == 0, h == 7, [("OT", n), ("wo", h)], [kp], inc=(half == 1 and h == 7))
        xob = xo[n % 3]
        ko = ("oxo", n % 3)
        _tt(P, "dve", xob[:], xb[:], pb[:].rearrange("p a b -> p (a b)"), ALU.add, [kx, kp], [ko])
        if not final:
            P.dma("sp", k.xres[n * 128:(n + 1) * 128, :], xob[:], reads=[ko], writes=[("xres", n)])
        else:
            _act(P, junk[:], xob[:], AF.Square, [ko], ["ojunk", ("oss", n)], accum=ss[:, n:n + 1])
            _act(P, rs[:, n:n + 1], ss[:, n:n + 1], AF.Ln, [("oss", n)], [("ors", n)], scale=1.0 / D, bias=EPS)
            _act(P, rs[:, n:n + 1], rs[:, n:n + 1], AF.Exp, [("ors", n)], [("ors", n)], scale=-0.5)
            yb = yo[n % 3]
            ky = ("oyo", n % 3)
            _stt(P, "dve", yb[:], xob[:], rs[:, n:n + 1], k.fnw[:], ALU.mult, ALU.mult, [ko, ("ors", n), "fnw"], [ky])
            if n < 16:
                P.dma("sp", k.yp[n * 128:(n + 1) * 128, :], yb[:], reads=[ky])
            else:
                for b in range(4):
                    P.dma("sp", k.ys[4 * b:4 * b + 4, :], yb[32 * b:32 * b + 4, :], reads=[ky])


def phase_hgrn(k, P, st, j):
    nc = k.nc
    sb = lambda name, shape, dt: st.enter_context(nc.sbuf_tensor(name, shape, dt))
    ps = lambda name, shape, dt: st.enter_context(nc.psum_tensor(name, shape, dt))
    win = k.a_w_in[j].rearrange("(kc p) (b h c) -> p kc b h c", p=128, b=4, h=8)
    k.m01 = sb("m01", [128, 128], F32)
    k.cmask = sb("cmask", [128, T], F32)
    k.padmask = sb("padmask", [128, 128], F32)
    k.rmask = sb("rmask", [128, 4], F32)
    P.dma("sp", k.m01[:], k.c_m01, writes=["m01"])
    P.dma("sp", k.cmask[:], k.c_cmask, writes=["cmask"])
    P.dma("sp", k.padmask[:], k.c_padmask, writes=["padmask"])
    P.dma("sp", k.rmask[:], k.c_rmask, writes=["rmask"])
    Wh = [sb(f"Wh{i}", [128, 8, 4, 128], BF16) for i in range(2)]
    NB = 2
    QTA = sb("QTA", [128, T], F32)
    GsA = sb("GsA", [128, T], F32)
    E = [sb(f"E{i}", [128, 512], F32) for i in range(NB)]
    U = [sb(f"U{i}", [128, 512], F32) for i in range(NB)]
    L1 = [sb(f"L1{i}", [128, 512], F32) for i in range(NB)]
    L2 = [sb(f"L2{i}", [128, 512], F32) for i in range(NB)]
    KT = [sb(f"KT{i}", [128, 512], F32) for i in range(NB)]
    BT = [sb(f"BT{i}", [128, 512], F32) for i in range(NB)]
    R2 = [sb(f"R2{i}", [128, 512], F32) for i in range(NB)]
    EX = [sb(f"EX{i}", [128, 512], F32) for i in range(3)]
    Dd = [sb(f"Dd{i}", [128, 16], F32) for i in range(NB)]
    Qt = [sb(f"Qt{i}", [128, 512], BF16) for i in range(NB)]
    Kt = [sb(f"Kt{i}", [128, 512], BF16) for i in range(NB)]
    Kh = [sb(f"Kh{i}", [128, 512], BF16) for i in range(NB)]
    V = [sb(f"V{i}", [128, 4, 128], BF16) for i in range(NB)]
    KH = [sb(f"KH{i}", [128, 4, 128], BF16) for i in range(NB)]
    ATm = [sb(f"ATm{i}", [128, 128], BF16) for i in range(2)]
    S32 = [sb(f"S32{i}", [128, 128], F32) for i in range(4)]
    Sbf = [sb(f"Sbf{i}", [128, 128], BF16) for i in range(4)]
    Osb = [sb(f"Osb{i}", [128, 512], F32) for i in range(2)]
    SQ = [sb(f"SQ{i}", [128, 512], F32) for i in range(2)]
    RSD = [sb(f"RSD{i}", [128, 512], F32) for i in range(2)]
    T1 = [sb(f"T1{i}", [128, 512], F32) for i in range(2)]
    pp = [ps(f"pp{i}", [128, 512], F32) for i in range(2)]
    pAB = [ps(f"pAB{i}", [128, 512], F32) for i in range(2)]
    pk = pAB[0][:, 256:512].bitcast(BF16).rearrange("p (a b) -> p a b", b=128)
    pdS = [ps(f"pdS{i}", [128, 512], F32) for i in range(2)]
    pot = [ps(f"pot{i}", [128, 512], F32) for i in range(2)]
    Vblk = [sb(f"Vblk{i}", [128, 4, 4, 128], BF16) for i in range(NB)]

    lb = k.lbv[:, j, :]
    oml = k.omlv[:, j, :]
    onw = k.onwv[:, j, :]
    clb = k.clbv[:, j, :]
    CLIP = float(np.exp(np.float32(60.0)))

    def load_wh(h):
        wb = Wh[h % 2]
        for b in range(4):
            P.dma("pool", wb[:, :, b, :], win[:, :, b, h, :], writes=[("Wh", h % 2, b)])

    load_wh(0)
    sidx = [0]
    cache_jobs = []
    if j == 0:
        for b in range(4):
            cache_jobs.append([(k.c2048, k.k2048s, 2048, b)])
        cache_jobs.append([(k.c512, k.k512s, 512, b) for b in range(4)])
        cache_jobs.append([(k.c128, k.k128s, 128, b) for b in range(4)])

    def proj_fm(h, b, tb, pbuf, kpb):
        t0, tn = TBS[tb]
        wb = Wh[h % 2]
        for kc in range(8):
            _mm(P, pbuf[:, 0:tn], wb[:, kc, b, :], k.hT[:, kc, t0:t0 + tn], kc == 0, kc == 7,
                [("Wh", h % 2, b)] + [("hT", n) for n in range(t0 // 128, (t0 + tn) // 128)], [kpb], inc=(kc == 7))

    for h in range(HG_HEADS):
        if h + 1 < 8:
            load_wh(h + 1)
        wb = Wh[h % 2]
        ci = 0
        if h < len(cache_jobs):
            for src_c, dst_c, W, b in cache_jobs[h]:
                P.dma("act", dst_c[b, 0:W - 4], src_c[b, 4:W])
        si = sidx[0]
        P.op("pool", lambda e, si=si: e.memset(S32[si][:], 0.0), [], [("S32", si)])
        P.op("pool", lambda e, si=si: e.memset(Sbf[si][:], 0.0), [], [("Sbf", si)])
        for tb in range(5):
            t0, tn = TBS[tb]
            proj_fm(h, 0, tb, pp[tb % 2], ("pp", tb % 2))
            _act(P, QTA[:, t0:t0 + tn], pp[tb % 2][:, 0:tn], AF.Silu, [("pp", tb % 2)], [("QTA", tb)])
        for tb in range(5):
            t0, tn = TBS[tb]
            proj_fm(h, 3, tb, pp[(tb + 1) % 2], ("pp", (tb + 1) % 2))
            _act(P, GsA[:, t0:t0 + tn], pp[(tb + 1) % 2][:, 0:tn], AF.Silu, [("pp", (tb + 1) % 2)], [("GsA", tb)])
        if HG_LEVEL <= 1:
            return
        for tb in range(HG_TBS):
            t0, tn = TBS[tb]
            ntile = tn // 128
            nch = tn // 32
            s = tb % NB
            sl = slice(0, tn)
            proj_fm(h, 1, tb, pp[1], ("pp", 1))
            _act(P, E[s][:, sl], pp[1][:, sl], AF.Exp, [("pp", 1)], [("E", s)], scale=-1.0)
            _act(P, L1[s][:, sl], E[s][:, sl], AF.Ln, [("E", s)], [("L1", s)], bias=1.0)
            _act(P, U[s][:, sl], L1[s][:, sl], AF.Exp, [("L1", s)], [("U", s)], scale=-1.0)
            _ts(P, "dve", L2[s][:, sl], E[s][:, sl], CLIP, None, ALU.min, None, [("E", s)], [("L2", s)])
            _act(P, L2[s][:, sl], L2[s][:, sl], AF.Ln, [("L2", s), "lbv"], [("L2", s)], scale=lb[:, h:h + 1], bias=1.0)
            _tt(P, "dve", L2[s][:, sl], L2[s][:, sl], L1[s][:, sl], ALU.subtract, [("L2", s), ("L1", s)], [("L2", s)])
            if tb == 4:
                _tt(P, "dve", L2[s][:, sl], L2[s][:, sl], k.padmask[:], ALU.mult, [("L2", s), "padmask"], [("L2", s)])
            _stt(P, "dve", KT[s][:, sl], E[s][:, sl], oml[:, h:h + 1], U[s][:, sl], ALU.mult, ALU.mult,
                 [("E", s), ("U", s), "lbv"], [("KT", s)])
            if HG_LEVEL <= 2:
                continue
            P.op("dve", lambda e, s=s, sl=sl, t0=t0, tn=tn: e.tensor_tensor_scan(
                BT[s][:, sl], k.cmask[:, t0:t0 + tn], L2[s][:, sl], 0.0, ALU.mult, ALU.add),
                [("L2", s), "cmask"], [("BT", s)])
            b3 = BT[s][:, sl].rearrange("p (c j) -> p c j", j=32)
            _act(P, Dd[s][:, 0:nch], b3[:, :, 31], AF.Exp, [("BT", s)], [("Dd", s)])
            P.op("dve", lambda e, s=s, sl=sl, b3=b3, nch=nch: e.tensor_tensor(
                R2[s][:, sl].rearrange("p (c j) -> p c j", j=32), b3[:, :, 31:32].to_broadcast([128, nch, 32]), b3, ALU.subtract),
                [("BT", s)], [("R2", s)])
            _act(P, EX[0][:, sl], BT[s][:, sl], AF.Exp, [("BT", s)], [("EX", 0)])
            _tt(P, "dve", Qt[s][:, sl], QTA[:, t0:t0 + tn], EX[0][:, sl], ALU.mult, [("QTA", tb), ("EX", 0)], [("Qt", s)])
            _act(P, EX[1][:, sl], BT[s][:, sl], AF.Exp, [("BT", s)], [("EX", 1)], scale=-1.0)
            _tt(P, "pool", Kt[s][:, sl], KT[s][:, sl], EX[1][:, sl], ALU.mult, [("KT", s), ("EX", 1)], [("Kt", s)])
            _act(P, EX[2][:, sl], R2[s][:, sl], AF.Exp, [("R2", s)], [("EX", 2)])
            _tt(P, "pool", Kh[s][:, sl], KT[s][:, sl], EX[2][:, sl], ALU.mult, [("KT", s), ("EX", 2)], [("Kh", s)])
            if HG_LEVEL <= 3:
                continue
            for i in range(ntile):
                n = t0 // 128 + i
                for kc in range(8):
                    _mm(P, pp[0][:, i * 128:(i + 1) * 128], k.hT[:, kc, n * 128:(n + 1) * 128], wb[:, kc, 2, :], kc == 0, kc == 7,
                        [("Wh", h % 2, 2), ("hT", n)], [("pp", 0)], inc=(kc == 7 and i == ntile - 1))
            _cp(P, "act", V[s][:, 0:ntile, :], pp[0][:, 0:tn].rearrange("p (a b) -> p a b", b=128), [("pp", 0)], [("V", s)])
            for c in range(4):
                P.op("act", lambda e, s=s, c=c, ntile=ntile, tn=tn: e.mul(
                    Vblk[s][:, 0:ntile, c, :], pp[0][:, 0:tn].rearrange("p (a b) -> p a b", b=128), k.rmask[:, c:c + 1]),
                    [("pp", 0), "rmask"], [("Vblk", s)])
            for i in range(ntile):
                P.op("pe", lambda e, i=i, s=s: e.transpose(pk[:, i, :], Kh[s][:, i * 128:(i + 1) * 128], k.ident[:]),
                     [("Kh", s), "ident"], [("pAB", 0)], inc=(i == ntile - 1))
            _cp(P, "act", KH[s][:, 0:ntile, :], pk[:, 0:ntile, :], [("pAB", 0)], [("KH", s)])
            if HG_LEVEL <= 4:
                continue
            po = pot[tb % 2]
            kpo = ("pot", tb % 2)
            for i in range(ntile):
                n = t0 // 128 + i
                cs = slice(i * 128, (i + 1) * 128)
                pat = pAB[i % 2][:, 0:128]
                _mm(P, pat, Kt[s][:, cs], Qt[s][:, cs], True, True, [("Kt", s), ("Qt", s)], [("pAB", i % 2)], True)
                am = ATm[i % 2]
                _tt(P, "dve", am[:], pat, k.m01[:], ALU.mult, [("pAB", i % 2), "m01"], [("ATm", i % 2)])
                pd = pdS[i % 2]
                if HG_SUB <= 1:
                    continue
                _mm(P, pd[:], KH[s][:, i, :], Vblk[s][:, i, :, :].rearrange("p c d -> p (c d)"), True, True,
                    [("KH", s), ("Vblk", s)], [("pdS", i % 2)], True)
                if HG_SUB <= 2:
                    continue
                _mm(P, po[:, cs], V[s][:, i, :], am[:], True, HG_SUB <= 3, [("V", s), ("ATm", i % 2)], [kpo], inc=(HG_SUB <= 3))
                if HG_SUB <= 3:
                    continue
                for c in range(4):
                    if tb == 4:
                        si = (sidx[0] + 1) % 4
                        sidx[0] = si
                        P.dma("sp", S32[si][:], k.sh[j, c, h], writes=[("S32", si)])
                        _cp(P, "act", Sbf[si][:], S32[si][:], [("S32", si)], [("Sbf", si)])
                    si = sidx[0]
                    cc = slice(i * 128 + 32 * c, i * 128 + 32 * c + 32)
                    _mm(P, po[:, cc], Sbf[si][:], Qt[s][:, cc], False, True, [("Sbf", si), ("Qt", s)], [kpo], inc=True)
                    if HG_SUB <= 4:
                        continue
                    so = (si + 1) % 4
                    chl = i * 4 + c
                    _stt(P, "dve", S32[so][:], S32[si][:], Dd[s][:, chl:chl + 1], pd[:, c * 128:(c + 1) * 128], ALU.mult, ALU.add,
                         [("S32", si), ("Dd", s), ("pdS", i % 2)], [("S32", so)])
                    _cp(P, "act", Sbf[so][:], S32[so][:], [("S32", so)], [("Sbf", so)])
                    sidx[0] = so
                    if tb == 3 and i == 3 and c == 3:
                        P.dma("sp", k.hp[j, h], S32[so][:], reads=[("S32", so)])
                    if tb == 4:
                        P.dma("sp", k.hs[j, c, h], S32[so][:], reads=[("S32", so)])
            if HG_LEVEL <= 5:
                continue
            ob = Osb[tb % 2]
            _cp(P, "act", ob[:, sl], po[:, sl], [kpo], [("Osb", tb % 2)])
            _act(P, SQ[tb % 2][:, sl], po[:, sl], AF.Square, [kpo], [("SQ", tb % 2)])
            _mm(P, pp[0][:, sl], k.onesf[:], SQ[tb % 2][:, sl], True, True, [("SQ", tb % 2), "onesf"], [("pp", 0)], True)
            _act(P, RSD[tb % 2][:, sl], pp[0][:, sl], AF.Ln, [("pp", 0)], [("RSD", tb % 2)], scale=1.0 / 128, bias=EPS)
            _act(P, RSD[tb % 2][:, sl], RSD[tb % 2][:, sl], AF.Exp, [("RSD", tb % 2)], [("RSD", tb % 2)], scale=-0.5)
            _stt(P, "dve", T1[tb % 2][:, sl], ob[:, sl], onw[:, h:h + 1], GsA[:, t0:t0 + tn], ALU.mult, ALU.mult,
                 [("Osb", tb % 2), ("GsA", tb), "onwv"], [("T1", tb % 2)])
            _tt(P, "pool", k.OT[:, h, t0:t0 + tn], T1[tb % 2][:, sl], RSD[tb % 2][:, sl], ALU.mult,
                [("T1", tb % 2), ("RSD", tb % 2)], [("OT", n) for n in range(t0 // 128, (t0 + tn) // 128)])
        if HG_LEVEL <= 6:
            return


GROUPS = ((128, 1), (512, 4), (2048, 16))


def g_tiles(g):
    tiles = []
    if g == 0:
        for n in range(16):
            tiles.append(dict(cols=slice(128 * n, 128 * n + 128), qr=n // 4, oc=slice((n % 4) * 128, (n % 4) * 128 + 128),
                              prev=(n - 1 if n >= 1 else None)))
    elif g == 1:
        for rho in range(4):
            for jb in range(4):
                tiles.append(dict(cols=slice(512 * jb + rho, 512 * jb + 512, 4), qr=jb, oc=slice(rho, 512, 4),
                                  prev=(rho * 4 + jb - 1 if jb >= 1 else None)))
    else:
        for rho in range(16):
            tiles.append(dict(cols=slice(rho, 2048, 16), qr=None, oc=None, prev=None))
    return tiles


def phase_attn(k, P, st):
    nc = k.nc
    sb = lambda name, shape, dt: st.enter_context(nc.sbuf_tensor(name, shape, dt))
    ps = lambda name, shape, dt: st.enter_context(nc.psum_tensor(name, shape, dt))
    win = k.b_w_in.rearrange("(kc p) (b h c) -> p kc b h c", p=128, b=10, h=8)
    k.identf = sb("identf", [128, 128], F32)
    k.cosT = sb("cosT", [128, T], F32)
    k.sinT = sb("sinT", [128, T], F32)
    k.permS = sb("permS", [128, 128], F32)
    k.mpo = sb("mpo", [128, 2, 128], BF16)
    k.mnew = sb("mnew", [128, 2, 128], BF16)
    k.mc9 = sb("mc9", [128, 16], BF16)
    P.dma("sp", k.identf[:], k.c_ident, writes=["identf"])
    P.dma("sp", k.cosT[:], k.c_cosT, writes=["cosT"])
    P.dma("sp", k.sinT[:], k.c_sinT, writes=["sinT"])
    P.dma("sp", k.permS[:], k.c_permS, writes=["permS"])
    P.dma("pool", k.mpo[:], k.c_mpo, writes=["mpo"])
    P.dma("pool", k.mnew[:], k.c_mnew, writes=["mnew"])
    P.dma("pool", k.mc9[:], k.c_mc9, writes=["mc9"])
    SCALE = 128.0 ** -0.5
    Wqk = [sb(f"Wqk{i}", [128, 8, 2, 128], BF16) for i in range(3)]
    Wv = [sb(f"Wv{i}", [128, 8, 128], BF16) for i in range(3)]
    Wg = [sb(f"Wg{i}", [128, 8, 128], BF16) for i in range(2)]
    QT = [sb(f"aQT{g}", [128, T], BF16) for g in range(3)]
    KT = [sb(f"aKT{g}", [128, T], BF16) for g in range(3)]
    Vt = [sb(f"aVt{g}", [128, 17, 128], BF16) for g in range(3)]
    GT = sb("aGT", [128, T], BF16)
    PT2 = sb("aPT2", [128, 16, 128], BF16)
    RAW = [sb(f"aRAW{i}", [128, 512], F32) for i in range(2)]
    T1 = [sb(f"aT1{i}", [128, 512], F32) for i in range(2)]
    T2 = [sb(f"aT2{i}", [128, 512], F32) for i in range(2)]
    KR = [sb(f"aKR{i}", [128, 512], F32) for i in range(2)]
    KO = [sb(f"aKO{i}", [128, 4, 128], F32) for i in range(2)]
    VO = [sb(f"aVO{i}", [128, 4, 128], F32) for i in range(2)]
    EXs = [sb(f"aEX{i}", [128, 512], BF16) for i in range(2)]
    PTb = [sb(f"aPT{i}", [128, 512], BF16) for i in range(3)]
    REC = sb("aREC", [128, 512], F32)
    ON = sb("aON", [128, 512], F32)
    CK = [sb(f"aCK{i}", [128, 9, 2, 128], BF16) for i in range(2)]
    KcT = [sb(f"aKcT{i}", [128, 9, 128], BF16) for i in range(2)]
    PS9 = [sb(f"aPS9{i}", [128, 16], BF16) for i in range(2)]
    EX9 = [sb(f"aEX9{i}", [128, 16], BF16) for i in range(2)]
    zer = sb("azer", [128, 128], BF16)
    onesb = sb("aonesb", [128, 128], BF16)
    pq = [ps(f"apq{i}", [128, 512], F32) for i in range(2)]
    ppm = ps("appm", [128, 512], F32)
    pS = [ps(f"apS{i}", [128, 512], F32) for i in range(2)]
    pN = ps("apN", [128, 512], F32)
    pD = ps("apD", [128, 512], F32)
    ptr = ps("aptr", [128, 512], F32)
    ptr_bf = ptr[:].bitcast(BF16)
    P.op("pool", lambda e: e.memset(zer[:], 0.0), [], ["zer"])
    P.op("pool", lambda e: e.memset(onesb[:], 1.0), [], ["onesb"])

    caches = (k.c128, k.c512, k.c2048)
    outs_p = (k.k128p, k.k512p, k.k2048p)
    outs_s = (k.k128s, k.k512s, k.k2048s)

    def load_w(h, g):
        i = (h * 3 + g) % 3
        for m in range(2):
            P.dma("pool", Wqk[i][:, :, m, :], win[:, :, 3 * g + m, h, :], writes=[("Wqk", i)])
        P.dma("pool", Wv[i][:], win[:, :, 3 * g + 2, h, :], writes=[("Wv", i)])
        if g == 0:
            P.dma("pool", Wg[h % 2][:], win[:, :, 9, h, :], writes=[("Wg", h % 2)])

    def load_cache(h, b):
        ck = CK[(h * 4 + b) % 2]
        key = ("CK", (h * 4 + b) % 2)
        P.dma("pool", ck[:, 0, :, :], k.c128[b, :, :, h, :], writes=[key])
        for t in range(4):
            P.dma("pool", ck[:, 1 + t, :, :], k.c512[b, t:512:4, :, h, :], writes=[key])
            P.dma("pool", ck[:, 5 + t, :, :], k.c2048[b, t:2048:16, :, h, :], writes=[key])

    hTk = lambda t0, tn: [("hT", n) for n in range(t0 // 128, (t0 + tn) // 128)]
    bcnt = [0]
    oq = [0]

    def outq():
        oq[0] += 1
        return "sp" if oq[0] % 2 else "act"
    load_w(0, 0)
    for h in range(8):
        for tb in range(5):
            t0, tn = TBS[tb]
            pb = pq[tb % 2]
            for kc in range(8):
                _mm(P, pb[:, 0:tn], Wg[h % 2][:, kc, :], k.hT[:, kc, t0:t0 + tn], kc == 0, kc == 7,
                    [("Wg", h % 2)] + hTk(t0, tn), [("pq", tb % 2)], inc=(kc == 7))
            _act(P, GT[:, t0:t0 + tn], pb[:, 0:tn], AF.Silu, [("pq", tb % 2)], [("GT", tb)])
        for g in range(3):
            W, dil = GROUPS[g]
            wi = (h * 3 + g) % 3
            if g < 2:
                load_w(h, g + 1)
            elif h < 7:
                load_w(h + 1, 0)
            blocks = [(m, tb) for m in range(2) for tb in range(5)]
            binfo = {}

            def stage_a(i):
                m, tb = blocks[i]
                t0, tn = TBS[tb]
                bi = bcnt[0] % 2
                bcnt[0] += 1
                binfo[i] = bi
                pb = pq[bi]
                for kc in range(8):
                    _mm(P, pb[:, 0:tn], Wqk[wi][:, kc, m, :], k.hT[:, kc, t0:t0 + tn], kc == 0, kc == 7,
                        [("Wqk", wi)] + hTk(t0, tn), [("pq", bi)], inc=(kc == 7))
                _cp(P, "act", RAW[bi][:, 0:tn], pb[:, 0:tn], [("pq", bi)], [("RAW", bi)])

            def stage_b(i):
                m, tb = blocks[i]
                t0, tn = TBS[tb]
                bi = binfo[i]
                _mm(P, ppm[:, 0:tn], k.permS[:], RAW[bi][:, 0:tn], True, True, [("RAW", bi), "permS"], ["ppm"], True)
                _tt(P, "dve", T1[bi][:, 0:tn], RAW[bi][:, 0:tn], k.cosT[:, t0:t0 + tn], ALU.mult, [("RAW", bi), "cosT"], [("T1", bi)])
                _tt(P, "dve", T2[bi][:, 0:tn], ppm[:, 0:tn], k.sinT[:, t0:t0 + tn], ALU.mult, ["ppm", "sinT"], [("T2", bi)])
                dst = QT[g] if m == 0 else KT[g]
                dkey = ("QT" if m == 0 else "KT", g, tb)
                _tt(P, "pool", dst[:, t0:t0 + tn], T1[bi][:, 0:tn], T2[bi][:, 0:tn], ALU.add, [("T1", bi), ("T2", bi)], [dkey])
                keep_tiles = [n for n in range(t0 // 128, (t0 + tn) // 128) if n == 16 or (n + 1) * 128 > SEQ - W]
                if m == 1 and keep_tiles:
                    _tt(P, "dve", KR[bi][:, 0:tn], T1[bi][:, 0:tn], T2[bi][:, 0:tn], ALU.add, [("T1", bi), ("T2", bi)], [("KR", bi)])
                    for n in keep_tiles:
                        il = n - t0 // 128
                        P.op("pe", lambda e, il=il, bi=bi: e.transpose(ptr[:, il * 128:(il + 1) * 128], KR[bi][:, il * 128:(il + 1) * 128], k.identf[:]),
                             [("KR", bi), "identf"], ["ptr"], inc=(n == keep_tiles[-1]))
                    i0 = keep_tiles[0] - t0 // 128
                    nk = len(keep_tiles)
                    kob = KO[bi]
                    _cp(P, "act", kob[:, 0:nk, :], ptr[:, i0 * 128:(i0 + nk) * 128].rearrange("p (a b) -> p a b", b=128),
                        ["ptr"], [("KO", bi)])
                    for a_, n in enumerate(keep_tiles):
                        if n < 16:
                            r0 = n * 128 - (SEQ - W)
                            P.dma(outq(), outs_p[g][r0:r0 + 128, 0, h, :], kob[:, a_, :], reads=[("KO", bi)])
                        else:
                            for b in range(4):
                                P.dma(outq(), outs_s[g][b, W - 4:W, 0, h, :], kob[32 * b:32 * b + 4, a_, :], reads=[("KO", bi)])

            stage_a(0)
            for i in range(len(blocks)):
                if i + 1 < len(blocks):
                    stage_a(i + 1)
                stage_b(i)
            tiles = g_tiles(g)
            vt = Vt[g]
            allt = [tl["cols"] for tl in tiles] + [slice(2048, 2176)]
            for c0 in range(0, 17, 4):
                idxs = list(range(c0, min(c0 + 4, 17)))
                bi = bcnt[0] % 2
                bcnt[0] += 1
                pb = pq[bi]
                for a_, ti in enumerate(idxs):
                    cs = allt[ti]
                    ntl = (2048 // 128) if ti == 16 else None
                    for kc in range(8):
                        _mm(P, pb[:, a_ * 128:(a_ + 1) * 128], k.hT[:, kc, cs], Wv[wi][:, kc, :], kc == 0, kc == 7,
                            [("Wv", wi)] + [("hT", n) for n in range(17)], [("pq", bi)], inc=(kc == 7 and ti == idxs[-1]))
                na = len(idxs)
                vob = VO[bi]
                _cp(P, "act", vob[:, 0:na, :], pb[:, 0:na * 128].rearrange("p (a b) -> p a b", b=128), [("pq", bi)], [("VO", bi)])
                _cp(P, "act", vt[:, c0:c0 + na, :], pb[:, 0:na * 128].rearrange("p (a b) -> p a b", b=128), [("pq", bi)], [("Vt", g, c0 // 4)])
                for a_, ti in enumerate(idxs):
                    if ti == 16:
                        for b in range(4):
                            P.dma(outq(), outs_s[g][b, W - 4:W, 1, h, :], vob[32 * b:32 * b + 4, a_, :], reads=[("VO", bi)])
                        continue
                    cs = allt[ti]
                    first = cs.start
                    step = cs.step or 1
                    last = first + step * 127
                    if last < SEQ - W:
                        continue
                    r0 = first - (SEQ - W)
                    if r0 < 0:
                        continue
                    P.dma(outq(), outs_p[g][r0:r0 + step * 127 + 1:step, 1, h, :], vob[:, a_, :], reads=[("VO", bi)])
        vkeys = lambda g: [("Vt", g, i) for i in range(5)]
        qkeys = lambda g: [("QT", g, i) for i in range(5)]
        kkeys = lambda g: [("KT", g, i) for i in range(5)]
        t2 = g_tiles(2)
        for c0 in range(0, 16, 4):
            bi = (c0 // 4) % 2
            for a_ in range(4):
                cs = t2[c0 + a_]["cols"]
                _mm(P, pS[bi][:, a_ * 128:(a_ + 1) * 128], KT[2][:, cs], QT[2][:, cs], True, True,
                    kkeys(2) + qkeys(2), [("pS", bi)], inc=(a_ == 3))
            _act(P, EXs[bi][:], pS[bi][:], AF.Exp, [("pS", bi)], [("EXs", bi)], scale=SCALE)
            P.op("pool", lambda e, bi=bi, c0=c0: e.tensor_tensor(
                PT2[:, c0:c0 + 4, :], EXs[bi][:].rearrange("p (a b) -> p a b", b=128),
                k.mpo[:, 1:2, :].to_broadcast([128, 4, 128]), ALU.mult), [("EXs", bi), "mpo"], [("PT2", c0 // 4)])
        pcnt = [0]
        for qr in range(5):
            wq = 512 if qr < 4 else 128
            _mm(P, pN[:, 0:wq], zer[:], k.hT[:, 0, 0:wq], True, False, ["zer", ("hT", 0), ("hT", 1), ("hT", 2), ("hT", 3)], ["pN"], inc=False)
            _mm(P, pD[:, 0:wq], zer[:], k.hT[:, 0, 0:wq], True, False, ["zer", ("hT", 0), ("hT", 1), ("hT", 2), ("hT", 3)], ["pD"], inc=False)

            def pv(vl, pt, oc, rk):
                P.op("pe", lambda e: e.matmul(pN[:, oc], vl, pt, start=False, stop=False, skip_group_check=True), rk, ["pN"], inc=False)
                P.op("pe", lambda e: e.matmul(pD[:, oc], onesb[:], pt, start=False, stop=False, skip_group_check=True), rk + ["onesb"], ["pD"], inc=True)

            if qr < 4:
                work = []
                for g in range(2):
                    tiles = g_tiles(g)
                    for ti, tl in enumerate(tiles):
                        if tl["qr"] == qr:
                            work.append((g, tiles, ti, tl))
                winfo = {}

                def sc_a(j):
                    g, tiles, ti, tl = work[j]
                    pi = pcnt[0] % 2
                    pcnt[0] += 1
                    kbs = ([tl["prev"]] if tl["prev"] is not None else []) + [ti]
                    off = 2 - len(kbs)
                    winfo[j] = (pi, kbs, off)
                    for x, kb in enumerate(kbs):
                        _mm(P, pS[pi][:, (off + x) * 128:(off + x + 1) * 128], KT[g][:, tiles[kb]["cols"]], QT[g][:, tl["cols"]], True, True,
                            kkeys(g) + qkeys(g), [("pS", pi)], inc=(x == len(kbs) - 1))

                def sc_b(j):
                    g, tiles, ti, tl = work[j]
                    pi, kbs, off = winfo[j]
                    ex = EXs[pi]
                    _act(P, ex[:, off * 128:256], pS[pi][:, off * 128:256], AF.Exp, [("pS", pi)], [("EXs", pi)], scale=SCALE)
                    pt = PTb[j % 3]
                    kpt = ("PTb", j % 3)
                    P.op("pool", lambda e, pt=pt, ex=ex, off=off: e.tensor_tensor(
                        pt[:, off * 128:256], ex[:, off * 128:256], k.mpo[:, off:2, :].rearrange("p a b -> p (a b)"), ALU.mult),
                        [("EXs", pi), "mpo"], [kpt])
                    for x, kb in enumerate(kbs):
                        pv(Vt[g][:, kb, :], pt[:, (off + x) * 128:(off + x + 1) * 128], tl["oc"], [kpt] + vkeys(g))

                sc_a(0)
                for j in range(len(work)):
                    if j + 1 < len(work):
                        sc_a(j + 1)
                    sc_b(j)
                for rho in range(16):
                    pv(Vt[2][:, rho, :], PT2[:, rho, 32 * qr:32 * qr + 32], slice(rho, 512, 16), [("PT2", rho // 4)] + vkeys(2))
            else:
                for g in range(3):
                    pi = pcnt[0] % 2
                    pcnt[0] += 1
                    _mm(P, pS[pi][:, 0:128], KT[g][:, 2048:2176], QT[g][:, 2048:2176], True, True, kkeys(g) + qkeys(g), [("pS", pi)], True)
                    ex = EXs[pi]
                    _act(P, ex[:, 0:128], pS[pi][:, 0:128], AF.Exp, [("pS", pi)], [("EXs", pi)], scale=SCALE)
                    pt = PTb[pcnt[0] % 3]
                    kpt = ("PTb", pcnt[0] % 3)
                    msk = k.mnew[:, 0, :] if g == 0 else k.mnew[:, 1, :]
                    _tt(P, "pool", pt[:, 0:128], ex[:, 0:128], msk, ALU.mult, [("EXs", pi), "mnew"], [kpt])
                    pv(Vt[g][:, 16, :], pt[:, 0:128], slice(0, 128), [kpt] + vkeys(g))
                def ca(b):
                    ci = (h * 4 + b) % 2
                    if b == 0:
                        load_cache(h, 0)
                        load_cache(h, 1)
                    ck = CK[ci]
                    kck = ("CK", ci)
                    for x in range(9):
                        dstp = ptr_bf[:, x * 128:(x + 1) * 128] if x < 8 else ptr_bf[:, 0:128]
                        P.op("pe", lambda e, x=x, ck=ck, dstp=dstp: e.transpose(dstp, ck[:, x, 0, :], k.ident[:]),
                             [kck, "ident"], ["ptr"], inc=(x == 7 or x == 8))
                        if x == 7:
                            _cp(P, "act", KcT[ci][:, 0:8, :], ptr_bf[:, 0:1024].rearrange("p (a b) -> p a b", b=128), ["ptr"], [("KcT", ci, 0)])
                        if x == 8:
                            _cp(P, "act", KcT[ci][:, 8, :], ptr_bf[:, 0:128], ["ptr"], [("KcT", ci, 1)])

                def cb(b):
                    ci = (h * 4 + b) % 2
                    ck = CK[ci]
                    kck = ("CK", ci)
                    pi = pcnt[0] % 2
                    pcnt[0] += 1
                    qb = 2048 + 32 * b
                    _mm(P, pS[pi][:, 0:4], KcT[ci][:, 0, :], QT[0][:, qb:qb + 4], True, True, [("KcT", ci, 0)] + qkeys(0), [("pS", pi)], inc=False)
                    for t in range(4):
                        _mm(P, pS[pi][:, 4 + t:5 + t], KcT[ci][:, 1 + t, :], QT[1][:, qb + t:qb + t + 1], True, True,
                            [("KcT", ci, 0)] + qkeys(1), [("pS", pi)], inc=False)
                        _mm(P, pS[pi][:, 8 + t:9 + t], KcT[ci][:, 5 + t, :], QT[2][:, qb + t:qb + t + 1], True, True,
                            [("KcT", ci, 0), ("KcT", ci, 1)] + qkeys(2), [("pS", pi)], inc=(t == 3))
                    _act(P, EX9[ci][:, 0:12], pS[pi][:, 0:12], AF.Exp, [("pS", pi)], [("EX9", ci)], scale=SCALE)
                    _tt(P, "pool", PS9[ci][:, 0:12], EX9[ci][:, 0:12], k.mc9[:, 0:12], ALU.mult, [("EX9", ci), "mc9"], [("PS9", ci)])
                    pv(ck[:, 0, 1, :], PS9[ci][:, 0:4], slice(32 * b, 32 * b + 4), [("PS9", ci), kck])
                    for t in range(4):
                        pv(ck[:, 1 + t, 1, :], PS9[ci][:, 4 + t:5 + t], slice(32 * b + t, 32 * b + t + 1), [("PS9", ci), kck])
                        pv(ck[:, 5 + t, 1, :], PS9[ci][:, 8 + t:9 + t], slice(32 * b + t, 32 * b + t + 1), [("PS9", ci), kck])

                ca(0)
                for b in range(4):
                    if b + 1 < 4:
                        ca(b + 1)
                    cb(b)
                    if b + 2 < 4:
                        load_cache(h, b + 2)
            P.op("pe", lambda e, wq=wq: e.matmul(pN[:, 0:wq], zer[:], k.hT[:, 0, 0:wq], start=False, stop=True, skip_group_check=True), ["zer"], ["pN"], inc=False)
            P.op("pe", lambda e, wq=wq: e.matmul(pD[:, 0:wq], zer[:], k.hT[:, 0, 0:wq], start=False, stop=True, skip_group_check=True), ["zer"], ["pD"], inc=True)
            q0 = 512 * qr
            P.op("dve", lambda e, wq=wq: e.reciprocal(REC[:, 0:wq], pD[:, 0:wq]), ["pD"], ["REC"])
            _tt(P, "dve", ON[:, 0:wq], pN[:, 0:wq], REC[:, 0:wq], ALU.mult, ["pN", "REC"], ["ON"])
            _tt(P, "pool", k.OT[:, h, q0:q0 + wq], ON[:, 0:wq], GT[:, q0:q0 + wq], ALU.mult, ["ON", ("GT", qr)],
                [("OT", n) for n in range(q0 // 128, (q0 + wq) // 128)])


PI = math.pi
TWO_PI = 2.0 * math.pi
CW1 = float(np.float32(6.28125))
CW2 = float(np.float32(TWO_PI - 6.28125))
CW3 = float(TWO_PI - CW1 - CW2)


def phase_s5(k, P, lst):
    nc = k.nc
    f1 = lambda st: (lambda name, shape, dt: st.enter_context(nc.sbuf_tensor(name, shape, dt)))
    f2 = lambda st: (lambda name, shape, dt: st.enter_context(nc.psum_tensor(name, shape, dt)))
    lsb = f1(lst)
    YG = lsb("sYG", [128, 8, T], BF16)
    XLo = lsb("sXLo", [64, 5, 64, 2], F32)
    win = k.c_w_in.rearrange("(kc p) (b m c) -> p kc b m c", p=128, b=2, m=8)
    hTk = lambda t0, tn: [("hT", n) for n in range(t0 // 128, (t0 + tn) // 128)]

    with contextlib.ExitStack() as st:
        sb, ps = f1(st), f2(st)
        uT = sb("suT", [128, 8, T], BF16)
        with contextlib.ExitStack() as st2:
            sb2, ps2 = f1(st2), f2(st2)
            Wu = sb2("sWu", [128, 8, 8, 128], BF16)
            pu = [ps2(f"spu{i}", [128, 512], F32) for i in range(2)]
            for m in range(8):
                P.dma("pool", Wu[:, :, m, :], win[:, :, 0, m, :], writes=[("Wu", m)])
            cnt = 0
            for m in range(8):
                for tb in range(5):
                    t0, tn = TBS[tb]
                    pb = pu[cnt % 2]
                    for kc in range(8):
                        _mm(P, pb[:, 0:tn], Wu[:, kc, m, :], k.hT[:, kc, t0:t0 + tn], kc == 0, kc == 7,
                            [("Wu", m)] + hTk(t0, tn), [("pu", cnt % 2)], inc=(kc == 7))
                    _cp(P, "act", uT[:, m, t0:t0 + tn], pb[:, 0:tn], [("pu", cnt % 2)], [("uT", m, tb)])
                    cnt += 1
            P.barrier()
            P.emit()
        BL = sb("sBL", [128, 8, 2, 128], BF16)
        BLp = sb("sBLp", [128, 8, 2, 128], BF16)
        CL1 = sb("sCL1", [128, 64, 32], BF16)
        CL2 = sb("sCL2", [128, 64, 32], BF16)
        DG = sb("sDG", [128, 8, 32], BF16)
        rr = sb("srr", [128, 64], F32)
        AB = sb("sAB", [128, 64, 96], F32)
        RS0 = sb("sRS0", [128, 64, 4], F32)
        VL = sb("sVL", [128, 64, 5], F32)
        CLt = sb("sCLt", [128, 64, 5], F32)
        SLt = sb("sSLt", [128, 64, 5], F32)
        with contextlib.ExitStack() as st2:
            sb2, ps2 = f1(st2), f2(st2)
            sm = lambda name: sb2(name, [128, 64], F32)
            are, aim, ldt = sm("s_are"), sm("s_aim"), sm("s_ldt")
            dt_, th, x1, x2, cs_, sn_ = sm("s_dt"), sm("s_th"), sm("s_x1"), sm("s_x2"), sm("s_cs"), sm("s_sn")
            bre, bim, den, xr, cre, cim, cis = sm("s_bre"), sm("s_bim"), sm("s_den"), sm("s_xr"), sm("s_cre"), sm("s_cim"), sm("s_cis")
            bT1 = sb2("s_bT1", [128, 64, 16], F32)
            bT2 = sb2("s_bT2", [128, 64, 16], F32)
            BBm = sb2("s_BB", [128, 64, 16], F32)
            BB2 = sb2("s_BB2", [128, 64, 16], F32)
            cTr = sb2("s_cTr", [128, 64, 16], F32)
            cTi = sb2("s_cTi", [128, 64, 16], F32)
            s0a = sb2("s_s0a", [128, 64, 4], F32)
            mulc = sb2("s_mul", [128, 96], F32)
            ki = sb2("s_ki", [128, 32, 96], mybir.dt.int32)
            evo = sb2("s_evo", [128, 4], F32)
            dd = sb2("s_dd", [128, 8], F32)
            identf = sb2("s_identf", [128, 128], F32)
            ptp = ps2("s_ptp", [128, 128], F32)
            for tl, src in ((are, k.s_are), (aim, k.s_aim), (ldt, k.s_ldt), (bT1, k.s_bT1), (bT2, k.s_bT2), (cTr, k.s_cTr),
                            (cTi, k.s_cTi), (s0a, k.s_s0), (mulc, k.c_mul), (evo, k.c_evo), (dd, k.s_dd), (identf, k.c_ident)):
                P.dma("sp", tl[:], src, writes=[id(tl)])
            V = "dve"
            _ts(P, V, are[:], are[:], -1e-4, None, ALU.min, None, [id(are)], [id(are)])
            _act(P, dt_[:], ldt[:], AF.Exp, [id(ldt)], [id(dt_)])
            _tt(P, V, x1[:], are[:], dt_[:], ALU.mult, [id(are), id(dt_)], [id(x1)])
            _act(P, rr[:], x1[:], AF.Exp, [id(x1)], ["rr"])
            _tt(P, V, th[:], aim[:], dt_[:], ALU.mult, [id(aim), id(dt_)], [id(th)])
            P.op(V, lambda e: e.tensor_tensor(AB[:], th[:].unsqueeze(2).to_broadcast([128, 64, 96]),
                                              mulc[:].unsqueeze(1).to_broadcast([128, 64, 96]), ALU.mult), [id(th), id(mulc)], ["AB"])
            for hf in range(2):
                gs_ = slice(32 * hf, 32 * hf + 32)
                _ts(P, V, ki[:], AB[:, gs_, :], 1.0 / TWO_PI, None, ALU.mult, None, ["AB"], [id(ki)])
                _stt(P, V, AB[:, gs_, :], ki[:], -CW1, AB[:, gs_, :], ALU.mult, ALU.add, [id(ki), "AB"], ["AB"])
                _stt(P, V, AB[:, gs_, :], ki[:], -(CW2 + CW3), AB[:, gs_, :], ALU.mult, ALU.add, [id(ki), "AB"], ["AB"])
            _act(P, sn_[:], AB[:, :, 32], AF.Sin, ["AB"], [id(sn_)])
            ki2 = ki[:, 0, 0:64]
            _ts(P, V, ki2, AB[:, :, 32], 1.0 / TWO_PI, 0.25, ALU.mult, ALU.add, ["AB"], [id(ki)])
            _stt(P, V, x2[:], ki2, -TWO_PI, AB[:, :, 32], ALU.mult, ALU.add, [id(ki), "AB"], [id(x2)])
            _act(P, cs_[:], x2[:], AF.Sin, [id(x2)], [id(cs_)], bias=PI / 2)
            _tt(P, V, bre[:], rr[:], cs_[:], ALU.mult, ["rr", id(cs_)], [id(bre)])
            _tt(P, V, bim[:], rr[:], sn_[:], ALU.mult, ["rr", id(sn_)], [id(bim)])
            _tt(P, V, den[:], are[:], are[:], ALU.mult, [id(are)], [id(den)])
            _tt(P, V, x1[:], aim[:], aim[:], ALU.mult, [id(aim)], [id(x1)])
            _tt(P, V, den[:], den[:], x1[:], ALU.add, [id(den), id(x1)], [id(den)])
            P.op(V, lambda e: e.reciprocal(den[:], den[:]), [id(den)], [id(den)])
            _ts(P, V, xr[:], bre[:], -1.0, None, ALU.add, None, [id(bre)], [id(xr)])
            _tt(P, V, cre[:], xr[:], are[:], ALU.mult, [id(xr), id(are)], [id(cre)])
            _tt(P, V, x1[:], bim[:], aim[:], ALU.mult, [id(bim), id(aim)], [id(x1)])
            _tt(P, V, cre[:], cre[:], x1[:], ALU.add, [id(cre), id(x1)], [id(cre)])
            _tt(P, V, cre[:], cre[:], den[:], ALU.mult, [id(cre), id(den)], [id(cre)])
            _tt(P, V, cim[:], bim[:], are[:], ALU.mult, [id(bim), id(are)], [id(cim)])
            _tt(P, V, x1[:], xr[:], aim[:], ALU.mult, [id(xr), id(aim)], [id(x1)])
            _tt(P, V, cim[:], cim[:], x1[:], ALU.subtract, [id(cim), id(x1)], [id(cim)])
            _tt(P, V, cim[:], cim[:], den[:], ALU.mult, [id(cim), id(den)], [id(cim)])
            _ts(P, V, cis[0:64, :], cim[0:64, :], -1.0, None, ALU.mult, None, [id(cim)], [id(cis)])
            _cp(P, V, cis[64:128, :], cim[64:128, :], [id(cim)], [id(cis)])
            P.op(V, lambda e: e.tensor_tensor(BBm[:], bT1[:], cre[:].unsqueeze(2).to_broadcast([128, 64, 16]), ALU.mult), [id(bT1), id(cre)], [id(BBm)])
            P.op(V, lambda e: e.tensor_tensor(BB2[:], bT2[:], cis[:].unsqueeze(2).to_broadcast([128, 64, 16]), ALU.mult), [id(bT2), id(cis)], [id(BB2)])
            _tt(P, V, BBm[:], BBm[:], BB2[:], ALU.add, [id(BBm), id(BB2)], [id(BBm)])
            for t8 in range(8):
                P.op("pe", lambda e, t8=t8: e.transpose(ptp[:], BBm[:, 8 * t8:8 * t8 + 8, :].rearrange("p a b -> p (a b)"), identf[:]),
                     [id(BBm), id(identf)], ["ptp"])
                for g2 in range(2):
                    _ts(P, V, BL[:, t8, g2, :], ptp[:], evo[:, g2:g2 + 1], None, ALU.mult, None, ["ptp", id(evo)], [("BL", t8)])
                    _ts(P, V, BLp[:, t8, g2, 0:64], ptp[:, 64:128], evo[:, g2:g2 + 1], None, ALU.mult, None, ["ptp", id(evo)], [("BLp", t8)])
                    _ts(P, V, BLp[:, t8, g2, 64:128], ptp[:, 0:64], evo[:, 2 + g2:3 + g2], None, ALU.mult, None, ["ptp", id(evo)], [("BLp", t8)])
            P.op("pool", lambda e: e.memset(CL1[:], 0.0), [], ["CL1"])
            P.op("pool", lambda e: e.memset(CL2[:], 0.0), [], ["CL2"])
            c1v = CL1[:].rearrange("p (a two) c -> p a two c", two=2)
            c2v = CL2[:].rearrange("p (a two) c -> p a two c", two=2)
            crv = cTr[:].rearrange("p (a two) c -> p a two c", two=2)
            civ = cTi[:].rearrange("p (a two) c -> p a two c", two=2)
            for g2 in range(2):
                cs16 = slice(16 * g2, 16 * g2 + 16)
                _cp(P, V, c1v[0:64, :, g2, cs16], crv[0:64, :, g2, :], [id(cTr)], ["CL1"])
                _ts(P, V, c1v[64:128, :, g2, cs16], civ[64:128, :, g2, :], -1.0, None, ALU.mult, None, [id(cTi)], ["CL1"])
                _ts(P, V, c2v[0:64, :, g2, cs16], civ[0:64, :, g2, :], -1.0, None, ALU.mult, None, [id(cTi)], ["CL2"])
                _ts(P, V, c2v[64:128, :, g2, cs16], crv[64:128, :, g2, :], -1.0, None, ALU.mult, None, [id(cTr)], ["CL2"])
            for t8 in range(8):
                for q4 in range(4):
                    _ts(P, V, DG[32 * q4:32 * q4 + 32, t8, :], identf[32 * q4:32 * q4 + 32, 32 * q4:32 * q4 + 32],
                        dd[32 * q4:32 * q4 + 32, t8:t8 + 1], None, ALU.mult, None, [id(identf), id(dd)], ["DG"])
            P.op(V, lambda e: e.tensor_tensor(RS0[:], s0a[:], rr[:].unsqueeze(2).to_broadcast([128, 64, 4]), ALU.mult), [id(s0a), "rr"], ["RS0"])
            P.barrier()
            P.emit()
        with contextlib.ExitStack() as st2:
            sb2, ps2 = f1(st2), f2(st2)
            ANG = [sb2(f"sANG{i}", [128, 512], F32) for i in range(3)]
            COS = [sb2(f"sCOS{i}", [128, 512], F32) for i in range(2)]
            SIN = [sb2(f"sSIN{i}", [128, 512], F32) for i in range(2)]
            KI1 = [sb2(f"sKI1{i}", [128, 512], mybir.dt.int32) for i in range(2)]
            BU = [sb2(f"sBU{i}", [128, 512], F32) for i in range(2)]
            BP = [sb2(f"sBP{i}", [128, 512], F32) for i in range(2)]
            T1 = [sb2(f"sT1{i}", [128, 512], F32) for i in range(2)]
            T2 = [sb2(f"sT2{i}", [128, 512], F32) for i in range(2)]
            Wt = [sb2(f"sW{i}", [128, 512], F32) for i in range(2)]
            Vv = [sb2(f"sV{i}", [128, 512], F32) for i in range(3)]
            CV = [sb2(f"sCV{i}", [128, 512], BF16) for i in range(2)]
            SV = [sb2(f"sSV{i}", [128, 512], BF16) for i in range(2)]
            pbu = ps2("spbu", [128, 512], F32)
            pbp = ps2("spbp", [128, 512], F32)
            py = [ps2(f"spy{i}", [128, 512], F32) for i in range(5)]
            pbb = [pbu, pbp, ps2("spb3", [128, 512], F32)]
            steps = [(g, tb) for g in range(64) for tb in range(5)]
            carry = {}

            def st_info(i):
                g, tb = steps[i]
                t8, q4, g2 = g // 8, (g % 8) // 2, g % 2
                t0, tn = TBS[tb]
                return g, tb, t8, q4, g2, t0, tn, slice(32 * q4, 32 * q4 + 32)

            def emit_bu(i):
                g, tb, t8, q4, g2, t0, tn, rows = st_info(i)
                urows = uT[rows, t8, t0:t0 + tn]
                ba, bb = (2 * i) % 3, (2 * i + 1) % 3
                _mm(P, pbb[ba][:, 0:tn], BL[rows, t8, g2, :], urows, True, True, [("BL", t8), ("uT", t8, tb)], [("pbb", ba)], True, tp=(32 * q4, 0))
                _mm(P, pbb[bb][:, 0:tn], BLp[rows, t8, g2, :], urows, True, True, [("BLp", t8), ("uT", t8, tb)], [("pbb", bb)], True, tp=(32 * q4, 0))

            def emit_ang(i):
                g, tb, t8, q4, g2, t0, tn, rows = st_info(i)
                bi = i % 3
                if tb < 4:
                    for a in range(8):
                        _act(P, ANG[bi][:, a * 64:(a + 1) * 64], AB[:, g, 32:96], AF.Identity, ["AB"], [("ANG", bi)],
                             bias=AB[:, g, 8 * tb + a:8 * tb + a + 1])
                else:
                    for b in range(4):
                        _cp(P, "act", ANG[bi][:, 32 * b:32 * b + 32], AB[:, g, 32:64], ["AB"], [("ANG", bi)])

            def emit_tables(i):
                g, tb, t8, q4, g2, t0, tn, rows = st_info(i)
                bi = i % 2
                ai = i % 3
                _ts(P, "dve", KI1[bi][:, 0:tn], ANG[ai][:, 0:tn], 1.0 / TWO_PI, 0.0, ALU.mult, ALU.add, [("ANG", ai)], [("KI1", bi)])
                _stt(P, "dve", SIN[bi][:, 0:tn], KI1[bi][:, 0:tn], -TWO_PI, ANG[ai][:, 0:tn], ALU.mult, ALU.add, [("KI1", bi), ("ANG", ai)], [("SIN", bi)])
                _act(P, COS[bi][:, 0:tn], SIN[bi][:, 0:tn], AF.Abs, [("SIN", bi)], [("COS", bi)])
                _act(P, COS[bi][:, 0:tn], COS[bi][:, 0:tn], AF.Sin, [("COS", bi)], [("COS", bi)], scale=-1.0, bias=PI / 2)
                _act(P, SIN[bi][:, 0:tn], SIN[bi][:, 0:tn], AF.Sin, [("SIN", bi)], [("SIN", bi)])
                if tb == 3:
                    _cp(P, "pool", CLt[:, g, 0:1], COS[bi][:, 511:512], [("COS", bi)], ["CLt"])
                    _cp(P, "pool", SLt[:, g, 0:1], SIN[bi][:, 511:512], [("SIN", bi)], ["SLt"])
                if tb == 4:
                    _cp(P, "pool", CLt[:, g, 1:5], COS[bi][:, 3:128:32], [("COS", bi)], ["CLt"])
                    _cp(P, "pool", SLt[:, g, 1:5], SIN[bi][:, 3:128:32], [("SIN", bi)], ["SLt"])

            emit_bu(0)
            emit_ang(0)
            emit_ang(1)
            emit_tables(0)
            for i in range(len(steps)):
                g, tb, t8, q4, g2, t0, tn, rows = st_info(i)
                bi = i % 2
                vi = i % 3
                ba, bb = (2 * i) % 3, (2 * i + 1) % 3
                urows = uT[rows, t8, t0:t0 + tn]
                if i + 2 < len(steps):
                    emit_ang(i + 2)
                _tt(P, "dve", T1[bi][:, 0:tn], pbb[ba][:, 0:tn], COS[bi][:, 0:tn], ALU.mult, [("pbb", ba), ("COS", bi)], [("T1", bi)])
                _tt(P, "dve", T2[bi][:, 0:tn], pbb[bb][:, 0:tn], SIN[bi][:, 0:tn], ALU.mult, [("pbb", bb), ("SIN", bi)], [("T2", bi)])
                if i + 1 < len(steps):
                    emit_bu(i + 1)
                _tt(P, "pool", Wt[bi][:, 0:tn], T1[bi][:, 0:tn], T2[bi][:, 0:tn], ALU.add, [("T1", bi), ("T2", bi)], [("W", bi)])
                if tb == 4:
                    w4 = Wt[bi][:, 0:128].rearrange("p (a b) -> p a b", b=32)
                    _tt(P, "pool", w4[:, :, 0], w4[:, :, 0], RS0[:, g, :], ALU.add, [("W", bi), "RS0"], [("W", bi)])
                if i + 1 < len(steps):
                    emit_tables(i + 1)
                vb = Vv[vi]
                if tb < 4:
                    init = 0.0 if tb == 0 else carry["v"]
                    rd = [("W", bi), "rr"] + ([("V", (vi - 1) % 3)] if tb > 0 else [])
                    P.op("dve", lambda e, vb=vb, bi=bi, init=init, g=g: e.tensor_tensor_scan(
                        vb[:, 0:512], rr[:, g:g + 1].to_broadcast([128, 512]), Wt[bi][:, 0:512], init, ALU.mult, ALU.add), rd, [("V", vi)])
                    carry["v"] = vb[:, 511:512]
                    if tb == 3:
                        _cp(P, "pool", VL[:, g, 0:1], vb[:, 511:512], [("V", vi)], ["VL"])
                else:
                    for b in range(4):
                        P.op("dve", lambda e, vb=vb, bi=bi, b=b, g=g: e.tensor_tensor_scan(
                            vb[:, 32 * b:32 * b + 32], rr[:, g:g + 1].to_broadcast([128, 32]), Wt[bi][:, 32 * b:32 * b + 32], 0.0, ALU.mult, ALU.add),
                            [("W", bi), "rr"], [("V", vi)])
                    _cp(P, "pool", VL[:, g, 1:5], vb[:, 3:128:32], [("V", vi)], ["VL"])
                _tt(P, "pool", CV[bi][:, 0:tn], vb[:, 0:tn], COS[bi][:, 0:tn], ALU.mult, [("V", vi), ("COS", bi)], [("CV", bi)])
                _tt(P, "pool", SV[bi][:, 0:tn], vb[:, 0:tn], SIN[bi][:, 0:tn], ALU.mult, [("V", vi), ("SIN", bi)], [("SV", bi)])
                yo = py[tb][rows, 0:tn]
                first = (g2 == 0)
                P.op("pe", lambda e, yo=yo, g=g, bi=bi, tn=tn, first=first, q4=q4: e.matmul(
                    yo, CL1[:, g, :], CV[bi][:, 0:tn], start=first, stop=False, tile_position=(0, 32 * q4), skip_group_check=True),
                    ["CL1", ("CV", bi)], [("py", tb)], inc=False)
                last = (g2 == 1)
                P.op("pe", lambda e, yo=yo, g=g, bi=bi, tn=tn, q4=q4: e.matmul(
                    yo, CL2[:, g, :], SV[bi][:, 0:tn], start=False, stop=False, tile_position=(0, 32 * q4), skip_group_check=True),
                    ["CL2", ("SV", bi)], [("py", tb)], inc=(not last))
                if last:
                    P.op("pe", lambda e, yo=yo, t8=t8, rows=rows, urows=urows, q4=q4: e.matmul(
                        yo, DG[rows, t8, :], urows, start=False, stop=True, tile_position=(32 * q4, 32 * q4), skip_group_check=True),
                        ["DG", ("uT", t8, tb)], [("py", tb)], inc=True)
                if g % 8 == 7 and tb == 4:
                    for tb2 in range(5):
                        t0b, tnb = TBS[tb2]
                        bj = tb2 % 2
                        _cp(P, "act", BU[bj][:, 0:tnb], py[tb2][:, 0:tnb], [("py", tb2)], [("BU", bj)])
                        _act(P, BP[bj][:, 0:tnb], py[tb2][:, 0:tnb], AF.Square, [("py", tb2)], [("BP", bj)])
                        _ts(P, "dve", BP[bj][:, 0:tnb], BP[bj][:, 0:tnb], 0.044715, 1.0, ALU.mult, ALU.add, [("BP", bj)], [("BP", bj)])
                        _tt(P, "dve", BP[bj][:, 0:tnb], BP[bj][:, 0:tnb], BU[bj][:, 0:tnb], ALU.mult, [("BP", bj), ("BU", bj)], [("BP", bj)])
                        _act(P, BP[bj][:, 0:tnb], BP[bj][:, 0:tnb], AF.Tanh, [("BP", bj)], [("BP", bj)], scale=math.sqrt(2.0 / math.pi))
                        _stt(P, "dve", BP[bj][:, 0:tnb], BP[bj][:, 0:tnb], 1.0, BU[bj][:, 0:tnb], ALU.add, ALU.mult, [("BP", bj), ("BU", bj)], [("BP", bj)])
                        P.op("act", lambda e, t8=t8, t0b=t0b, tnb=tnb, bj=bj: e.mul(YG[:, t8, t0b:t0b + tnb], BP[bj][:, 0:tnb], 0.5), [("BP", bj)], [("YG", t8, tb2)])
            k_ps2 = sb2("sPS2", [128, 128], F32)
            k_idf = sb2("sidf2", [128, 128], F32)
            XL = T2[0][:, 0:320].rearrange("p (a b) -> p a b", b=5)
            XT = T1[0][:, 0:320].rearrange("p (a b) -> p a b", b=5)
            P.dma("sp", k_ps2[:], k.c_ps2, writes=["ps2"])
            P.dma("sp", k_idf[:], k.c_ident, writes=["idf2"])
            _mm(P, pbu[:, 0:320], k_ps2[:], VL[:].rearrange("p a b -> p (a b)"), True, True, ["ps2", "VL"], [("pbb", 0)], True)
            _tt(P, "dve", T1[0][:, 0:320], pbu[:, 0:320], SLt[:].rearrange("p a b -> p (a b)"), ALU.mult, [("pbb", 0), "SLt"], [("T1", 0)])
            _tt(P, "dve", XL, VL[:], CLt[:], ALU.mult, ["VL", "CLt"], [("T2", 0)])
            _tt(P, "dve", XL, XL, XT, ALU.add, [("T2", 0), ("T1", 0)], [("T2", 0)])
            for c in range(5):
                P.op("pe", lambda e, c=c: e.transpose(pbp[0:64, c * 128:(c + 1) * 128] if c < 4 else pbu[0:64, 384:512], XL[:, :, c], k_idf[:]),
                     [("T2", 0), "idf2"], [("pbb", 1) if c < 4 else ("pbb", 0)])
                src = pbp[0:64, c * 128:(c + 1) * 128] if c < 4 else pbu[0:64, 384:512]
                _cp(P, "dve", XLo[:, c, :, :], src.rearrange("g (h p) -> g p h", h=2), [("pbb", 1) if c < 4 else ("pbb", 0)], [("XLo", c)])
            P.dma("sp", k.s5p, XLo[:, 0, :, :], reads=[("XLo", 0)])
            for b in range(4):
                P.dma("sp", k.s5s[b], XLo[:, 1 + b, :, :], reads=[("XLo", 1 + b)])
            P.barrier()
            P.emit()
    k.OT = lsb("OT", [128, 8, T], BF16)
    with contextlib.ExitStack() as st:
        sb, ps = f1(st), f2(st)
        Wgl = sb("sWgl", [128, 8, D], BF16)
        Wga = sb("sWga", [128, 8, D], BF16)
        bgl = sb("sbgl", [128, 8], F32)
        SG = [sb(f"sSG{i}", [128, 512], F32) for i in range(2)]
        SGT = [sb(f"sSGT{i}", [128, 512], F32) for i in range(2)]
        GS = [sb(f"sGS{i}", [128, 512], F32) for i in range(2)]
        Y3 = [sb(f"sY3{i}", [128, 512], F32) for i in range(2)]
        pz = [ps(f"spz{i}", [128, 512], F32) for i in range(2)]
        pg = [ps(f"spg{i}", [128, 512], F32) for i in range(2)]
        for kc in range(8):
            P.dma("pool", Wgl[:, kc, :], k.c_w_glu[kc * 128:(kc + 1) * 128, :], writes=[("Wgl", kc)])
            P.dma("pool", Wga[:, kc, :], k.c_w_in[kc * 128:(kc + 1) * 128, D:2 * D], writes=[("Wga", kc)])
        P.dma("sp", bgl[:], k.s_bglu, writes=["bgl"])
        cnt = 0
        for m in range(8):
            for tb in range(5):
                t0, tn = TBS[tb]
                bi = cnt % 2
                cnt += 1
                for kc in range(8):
                    _mm(P, pz[bi][:, 0:tn], Wgl[:, kc, m * 128:(m + 1) * 128], YG[:, kc, t0:t0 + tn], kc == 0, kc == 7,
                        [("Wgl", kc), ("YG", kc, tb)], [("pz", bi)], inc=(kc == 7))
                _act(P, SG[bi][:, 0:tn], pz[bi][:, 0:tn], AF.Sigmoid, [("pz", bi), "bgl"], [("SG", bi)], bias=bgl[:, m:m + 1])
                for kc in range(8):
                    _mm(P, pg[bi][:, 0:tn], Wga[:, kc, m * 128:(m + 1) * 128], k.hT[:, kc, t0:t0 + tn], kc == 0, kc == 7,
                        [("Wga", kc)] + hTk(t0, tn), [("pg", bi)], inc=(kc == 7))
                _act(P, SGT[bi][:, 0:tn], pg[bi][:, 0:tn], AF.Sigmoid, [("pg", bi)], [("SGT", bi)])
                _tt(P, "dve", GS[bi][:, 0:tn], pg[bi][:, 0:tn], SGT[bi][:, 0:tn], ALU.mult, [("pg", bi), ("SGT", bi)], [("GS", bi)])
                _tt(P, "pool", Y3[bi][:, 0:tn], YG[:, m, t0:t0 + tn], SG[bi][:, 0:tn], ALU.mult, [("YG", m, tb), ("SG", bi)], [("Y3", bi)])
                _tt(P, "pool", k.OT[:, m, t0:t0 + tn], Y3[bi][:, 0:tn], GS[bi][:, 0:tn], ALU.mult, [("Y3", bi), ("GS", bi)],
                    [("OT", n) for n in range(t0 // 128, (t0 + tn) // 128)])
        P.barrier()
        P.emit()

def build_nc(nlayers=4, debug=False):
    nc_real = bass.Bass("TRN2", target_bir_lowering=False)

    class _NC:
        def __init__(self, real):
            self._real = real
            self._uid = 0

        def __getattr__(self, name):
            return getattr(self._real, name)

        def sbuf_tensor(self, name, shape, dt):
            self._uid += 1
            return self._real.sbuf_tensor(f"{name}_u{self._uid}", shape, dt)

        def psum_tensor(self, name, shape, dt):
            self._uid += 1
            return self._real.psum_tensor(f"{name}_u{self._uid}", shape, dt)

    nc = _NC(nc_real)
    k = K()
    k.nc = nc
    di = lambda name, shape: nc.dram_tensor(name, list(shape), F32, kind="ExternalInput").ap()
    do = lambda name, shape: nc.dram_tensor(name, list(shape), F32, kind="ExternalOutput").ap()
    k.xp = di("xp", [SEQ, D])
    k.xs = di("xs", [16, D])
    k.sh = di("sh", [2, 4, 8, 128, 128])
    k.normw_d = di("normw_d", [128, 4, 8])
    k.fnw_d = di("fnw_d", [128, D])
    k.a_w_in = di("a_w_in", [2, D, 4096])
    k.a_w_out = di("a_w_out", [2, D, D])
    k.lb_d = di("lb_d", [128, 2, 2, 8])
    k.onw_d = di("onw_d", [128, 2, 8])
    k.c_ident = di("c_ident", [128, 128])
    k.c_m01 = di("c_m01", [128, 128])
    k.c_cmask = di("c_cmask", [128, T])
    k.c_padmask = di("c_padmask", [128, 128])
    k.c_rmask = di("c_rmask", [128, 4])
    k.b_w_in = di("b_w_in", [D, 10240])
    k.b_w_out = di("b_w_out", [D, D])
    k.c128 = di("c128", [4, 128, 2, 8, 128])
    k.c512 = di("c512", [4, 512, 2, 8, 128])
    k.c2048 = di("c2048", [4, 2048, 2, 8, 128])
    k.c_cosT = di("c_cosT", [128, T])
    k.c_sinT = di("c_sinT", [128, T])
    k.c_permS = di("c_permS", [128, 128])
    k.c_mpo = di("c_mpo", [128, 2, 128])
    k.c_mnew = di("c_mnew", [128, 2, 128])
    k.c_mc9 = di("c_mc9", [128, 16])
    k.k128p = do("k128p", [128, 2, 8, 128])
    k.k128s = do("k128s", [4, 128, 2, 8, 128])
    k.k512p = do("k512p", [512, 2, 8, 128])
    k.k512s = do("k512s", [4, 512, 2, 8, 128])
    k.k2048p = do("k2048p", [2048, 2, 8, 128])
    k.k2048s = do("k2048s", [4, 2048, 2, 8, 128])
    k.c_w_in = di("c_w_in", [D, 2 * D])
    k.c_w_glu = di("c_w_glu", [D, D])
    k.c_w_out = di("c_w_out", [D, D])
    k.s_are = di("s_are", [128, 64]); k.s_aim = di("s_aim", [128, 64]); k.s_ldt = di("s_ldt", [128, 64])
    k.s_bT1 = di("s_bT1", [128, 64, 16]); k.s_bT2 = di("s_bT2", [128, 64, 16])
    k.s_cTr = di("s_cTr", [128, 64, 16]); k.s_cTi = di("s_cTi", [128, 64, 16])
    k.s_s0 = di("s_s0", [128, 64, 4])
    k.s_dd = di("s_dd", [128, 8]); k.s_bglu = di("s_bglu", [128, 8])
    k.c_mul = di("c_mul", [128, 96]); k.c_evo = di("c_evo", [128, 4]); k.c_ps2 = di("c_ps2", [128, 128])
    k.s5p = do("s5p", [64, 64, 2])
    k.s5s = do("s5s", [4, 64, 64, 2])
    k.yp = do("yp", [SEQ, D])
    k.ys = do("ys", [16, D])
    k.hp = do("hp", [2, 8, 128, 128])
    k.hs = do("hs", [2, 4, 8, 128, 128])
    k.xres = nc.dram_tensor("xres", [T, D], F32, kind="Internal").ap()
    if debug:
        k.dbg = do("dbg", [T, D])

    with contextlib.ExitStack() as gst:
        P = Prog(nc, gst)
        gsb = lambda name, shape, dt: gst.enter_context(nc.sbuf_tensor(name, shape, dt))
        k.hT = gsb("hT", [128, 8, T], BF16)
        k.ident = gsb("ident", [128, 128], BF16)
        k.onesf = gsb("onesf", [128, 128], F32)
        k.normw = gsb("normw", [128, 4, 8], F32)
        k.lbraw = gsb("lbraw", [128, 2, 8], F32)
        k.lbv = gsb("lbv", [128, 2, 8], F32)
        k.omlv = gsb("omlv", [128, 2, 8], F32)
        k.onwv = gsb("onwv", [128, 2, 8], F32)
        k.clbv = gsb("clbv", [128, 2, 8], F32)
        P.dma("pool", k.ident[:], k.c_ident, writes=["ident"])
        P.dma("sp", k.normw[:], k.normw_d, writes=["normw"])
        P.dma("sp", k.lbraw[:], k.lb_d[:, :, 0, :], writes=["lbraw"])
        P.dma("sp", k.onwv[:], k.onw_d, writes=["onwv"])
        P.op("pool", lambda e: e.memset(k.onesf[:], 1.0), [], ["onesf"])
        P.op("pool", lambda e: e.memset(k.lbv[:, 0, :], 0.0), [], [("lbv", 0)])
        _tt(P, "dve", k.lbv[:, 1, :], k.lbraw[:, 0, :], k.lbraw[:, 1, :], ALU.subtract, ["lbraw"], [("lbv", 1)])
        _act(P, k.lbv[:, 1, :], k.lbv[:, 1, :], AF.Exp, [("lbv", 1)], [("lbv", 1)])
        _ts(P, "dve", k.lbv[:, 1, :], k.lbv[:, 1, :], 1.0, None, ALU.add, None, [("lbv", 1)], [("lbv", 1)])
        P.op("dve", lambda e: e.reciprocal(k.lbv[:, 1, :], k.lbv[:, 1, :]), [("lbv", 1)], [("lbv", 1)])
        _ts(P, "dve", k.omlv[:], k.lbv[:], -1.0, 1.0, ALU.mult, ALU.add, [("lbv", 0), ("lbv", 1)], ["omlv"])
        _ts(P, "dve", k.clbv[:], k.lbv[:], float(np.exp(np.float32(60.0))), None, ALU.mult, None, [("lbv", 0), ("lbv", 1)], ["clbv"])
        P.barrier()
        P.reg = {}
        P.emit()

        kinds = [0, 1, 2, 0]
        for layer in range(nlayers if STOP_AFTER != "setup" else 0):
            src = "in" if layer == 0 else "xres"
            last = (layer == nlayers - 1)
            with contextlib.ExitStack() as st:
                phase_norm(k, P, st, layer, src)
                P.barrier()
                P.emit()
            if STOP_AFTER == "norm":
                break
            with contextlib.ExitStack() as lst:
                if kinds[layer] == 2:
                    phase_s5(k, P, lst)
                    wout = k.c_w_out
                    if STOP_AFTER == "mixer":
                        break
                else:
                    k.OT = lst.enter_context(nc.sbuf_tensor("OT", [128, 8, T], BF16))
                    with contextlib.ExitStack() as st:
                        if kinds[layer] == 0:
                            phase_hgrn(k, P, st, layer // 3)
                            wout = k.a_w_out[layer // 3]
                        elif kinds[layer] == 1:
                            phase_attn(k, P, st)
                            wout = k.b_w_out
                        P.barrier()
                        P.emit()
                    if STOP_AFTER == "mixer":
                        break
                with contextlib.ExitStack() as st:
                    phase_outproj(k, P, st, wout, src, "xres", final=(layer == 3))
                    P.barrier(final=(layer == nlayers - 1))
                    P.emit()
        if debug:
            with contextlib.ExitStack() as st:
                P.dma("sp", k.dbg, k.xres)
                P.barrier(final=True)
                P.emit()
    return nc_real


def host_consts():
    c = {}
    c["c_ident"] = np.eye(128, dtype=np.float32)
    s = np.arange(128)
    c["c_m01"] = ((s[:, None] // 32 == s[None, :] // 32) & (s[:, None] <= s[None, :])).astype(np.float32)
    t = np.arange(T)
    c["c_cmask"] = np.broadcast_to((t % 32 != 0).astype(np.float32), (128, T)).copy()
    c["c_padmask"] = np.broadcast_to(((s % 32) < 4).astype(np.float32), (128, 128)).copy()
    c["c_rmask"] = (s[:, None] // 32 == np.arange(4)[None, :]).astype(np.float32)
    pos = np.concatenate([np.arange(SEQ), PAST + (np.arange(128) % 32)]).astype(np.float32)
    half = 64
    inv_freq = (np.float32(10000.0) ** (-np.arange(half, dtype=np.float32) / np.float32(half))).astype(np.float32)
    ang = (pos[None, :] * inv_freq[:, None]).astype(np.float32).astype(np.float64)
    cos = np.cos(ang).astype(np.float32)
    sin = np.sin(ang).astype(np.float32)
    c["c_cosT"] = np.ascontiguousarray(np.concatenate([cos, cos], axis=0))
    c["c_sinT"] = np.ascontiguousarray(np.concatenate([-sin, sin], axis=0))
    pm = np.zeros((128, 128), np.float32)
    m = np.arange(128)
    pm[(m + 64) % 128, m] = 1.0
    c["c_permS"] = pm
    cq = s[:, None]
    a = s[None, :]
    c["c_mpo"] = np.ascontiguousarray(np.stack([(a <= cq), (a >= cq)], axis=1).astype(np.float32))
    bq, tq = s // 32, s % 32
    same = bq[:, None] == bq[None, :]
    real = (tq[:, None] < 4) & (tq[None, :] < 4)
    diag = (s[:, None] == s[None, :])
    m0 = (same & real & (tq[:, None] <= tq[None, :])) | (diag & (tq[:, None] >= 4))
    c["c_mnew"] = np.ascontiguousarray(np.stack([m0, diag], axis=1).astype(np.float32))
    mc9 = np.ones((128, 16), np.float32)
    for t in range(4):
        mc9[:, t] = (s >= t)
    c["c_mc9"] = mc9
    mul = np.concatenate([64.0 * np.arange(32), 1.0 + np.arange(64)]).astype(np.float32)
    c["c_mul"] = np.ascontiguousarray(np.broadcast_to(mul[None, :], (128, 96)))
    ev = ((s // 16) % 2 == 0).astype(np.float32)
    c["c_evo"] = np.ascontiguousarray(np.stack([ev, 1 - ev, -ev, -(1 - ev)], axis=1))
    ps2 = np.zeros((128, 128), np.float32)
    pp_ = np.arange(64)
    ps2[64 + pp_, pp_] = -1.0
    ps2[pp_, 64 + pp_] = 1.0
    c["c_ps2"] = ps2
    return c


def make_in_maps(inp):
    consts = host_consts()
    normw = np.ascontiguousarray(inp["norm_w"].reshape(4, 8, 128).transpose(2, 0, 1))
    fnw = np.ascontiguousarray(np.broadcast_to(inp["final_norm_w"][None, :], (128, D)))
    lbl = inp["a_lb_logits"].reshape(2, 8, 128).transpose(2, 0, 1)
    lb_d = np.ascontiguousarray(np.stack([lbl, lbl], axis=2))
    onw = np.ascontiguousarray(inp["a_onorm_w"].reshape(2, 8, 128).transpose(2, 0, 1))
    dup = lambda a: np.ascontiguousarray(np.concatenate([a, a], axis=0))
    a_re = inp["c_a_re"][0].T; a_im = inp["c_a_im"][0].T
    s_are, s_aim = dup(a_re), dup(a_im)
    s_ldt = np.ascontiguousarray(np.broadcast_to(inp["c_log_dt"][0][None, :], (128, 64)))
    b_re = inp["c_b_re"][0].transpose(1, 0, 2); b_im = inp["c_b_im"][0].transpose(1, 0, 2)
    s_bT1 = np.ascontiguousarray(np.concatenate([b_re, b_im], axis=0))
    s_bT2 = np.ascontiguousarray(np.concatenate([b_im, b_re], axis=0))
    c_re = inp["c_c_re"][0].transpose(2, 0, 1); c_im = inp["c_c_im"][0].transpose(2, 0, 1)
    s_cTr, s_cTi = dup(c_re), dup(c_im)
    s_dd = np.ascontiguousarray(inp["c_d"][0].reshape(8, 128).T)
    s_bglu = np.ascontiguousarray(inp["c_b_glu"][0].reshape(8, 128).T)
    maps = []
    for c in range(NCORES):
        m = dict(consts)
        m["xp"] = np.ascontiguousarray(inp["x_prompt"][c])
        m["xs"] = np.ascontiguousarray(inp["x_sample"][4 * c:4 * c + 4].reshape(16, D))
        m["sh"] = np.ascontiguousarray(inp["state_hgrn"][:, 4 * c:4 * c + 4])
        m["normw_d"] = normw
        m["fnw_d"] = fnw
        m["a_w_in"] = inp["a_w_in"]
        m["a_w_out"] = inp["a_w_out"]
        m["b_w_in"] = inp["b_w_in"][0]
        m["b_w_out"] = inp["b_w_out"][0]
        m["c_w_in"] = inp["c_w_in"][0]
        m["c_w_glu"] = inp["c_w_glu"][0]
        m["c_w_out"] = inp["c_w_out"][0]
        m["s_are"], m["s_aim"], m["s_ldt"] = s_are, s_aim, s_ldt
        m["s_bT1"], m["s_bT2"], m["s_cTr"], m["s_cTi"] = s_bT1, s_bT2, s_cTr, s_cTi
        st5 = inp["state_s5"][0, 4 * c:4 * c + 4]
        m["s_s0"] = np.ascontiguousarray(np.concatenate([st5[..., 0].transpose(2, 1, 0), st5[..., 1].transpose(2, 1, 0)], axis=0))
        m["s_dd"], m["s_bglu"] = s_dd, s_bglu
        m["c128"] = np.ascontiguousarray(inp["cache_kv_w128"][0, 4 * c:4 * c + 4])
        m["c512"] = np.ascontiguousarray(inp["cache_kv_w512"][0, 4 * c:4 * c + 4])
        m["c2048"] = np.ascontiguousarray(inp["cache_kv_w2048"][0, 4 * c:4 * c + 4])
        m["lb_d"] = lb_d
        m["onw_d"] = onw
        maps.append(m)
    return maps


def kernel(**inputs):
    inp = {k_: np.asarray(v) for k_, v in inputs.items()}
    nc = build_nc()
    maps = make_in_maps(inp)
    res = run_bass_kernel_spmd(nc, maps, core_ids=list(range(NCORES)))
    r = res.results
    cat = lambda name: np.concatenate([r[c][name] for c in range(NCORES)], axis=0)
    stk = lambda name: np.stack([r[c][name] for c in range(NCORES)], axis=0)
    y_prompt = stk("yp")
    y_sample = np.concatenate([r[c]["ys"].reshape(4, 4, D) for c in range(NCORES)], axis=0)
    hgrn_p = np.stack([r[c]["hp"] for c in range(NCORES)], axis=1)
    hgrn_s = np.concatenate([r[c]["hs"] for c in range(NCORES)], axis=1)
    outs = [y_prompt, y_sample, hgrn_p, hgrn_s]
    for w in (128, 512, 2048):
        outs.append(stk(f"k{w}p")[None])
        outs.append(cat(f"k{w}s")[None])
    outs.append(stk("s5p")[None])
    outs.append(cat("s5s")[None])
    return tuple(np.ascontiguousarray(o, dtype=np.float32) for o in outs)
```
